# Optimizing a Trainium2 kernel written in Bass

```python
import jax, jax.numpy as jnp
from jax import lax
import numpy as np

D_MODEL = 1024
BATCH = 1
SEQ = 16384
DEPTH = 4

ATT_HEAD_DIM = 64
ATT_WIDTH = D_MODEL // 2
ATT_HEADS = ATT_WIDTH // ATT_HEAD_DIM
Q_BLOCK = 128
LRU_WIDTH = D_MODEL
LRU_BLOCK_DIM = 64
LRU_BLOCKS = LRU_WIDTH // LRU_BLOCK_DIM
LRU_CONV = 4
LRU_C = 8.0
RWKV_HEAD_DIM = 64
RWKV_WIDTH = D_MODEL // 2
RWKV_HEADS = RWKV_WIDTH // RWKV_HEAD_DIM
DECAY_LORA = 64
AAA_LORA = 64
GATE_LORA = 128
RWKV_GN_EPS = 64e-5
D_FF = ((8 * D_MODEL // 3 + 127) // 128) * 128
N_BRANCH = 3
N_SUB = 3
N_ADA = 3 * N_SUB
LN_EPS = 1e-5
DEEPNORM_ALPHA = (2 * DEPTH) ** 0.25
DEEPNORM_BETA = (8 * DEPTH) ** -0.25
ATT_COLS = 3 * ATT_WIDTH + ATT_HEADS
LRU_COLS = 2 * LRU_WIDTH
RWKV_COLS = 3 * RWKV_WIDTH + DECAY_LORA + AAA_LORA + GATE_LORA
GATE_COLS = N_BRANCH * D_MODEL
D_IN = ATT_COLS + LRU_COLS + RWKV_COLS + GATE_COLS

kernel_name = "hybrid_fox_rglru_rwkv7_macaron_deepnorm"


def _split(z, sizes):
    return jnp.split(z, np.cumsum(sizes)[:-1].tolist(), axis=-1)


def _ln_stats(x):
    xf = x.astype(jnp.float32)
    mu = jnp.mean(xf, axis=-1, keepdims=True)
    var = jnp.mean(jnp.square(xf - mu), axis=-1, keepdims=True)
    return (xf - mu) * lax.rsqrt(var + LN_EPS)


def _post_norm(x, g, b):
    return (_ln_stats(x) * g + b).astype(x.dtype)


def _modulate(x, shift, scale):
    return (_ln_stats(x) * (1 + scale[:, None, :]) + shift[:, None, :]).astype(x.dtype)


def _prev_token(z):
    return jnp.pad(z, ((0, 0), (1, 0), (0, 0)))[:, :-1]


def _swiglu(h, w_up, w_down):
    u, gte = _split(h @ w_up, [D_FF, D_FF])
    return (jax.nn.silu(u) * gte) @ w_down


def _fox_attention(q, k, v, f_logit, f_bias):
    B, S, _ = q.shape
    H, dh = ATT_HEADS, ATT_HEAD_DIM
    heads = lambda t: t.reshape(B, S, H, dh).transpose(0, 2, 1, 3)
    q, k, v = heads(q), heads(k), heads(v)
    log_f = jax.nn.log_sigmoid((f_logit + f_bias).astype(jnp.float32))
    F = jnp.cumsum(log_f, axis=1).transpose(0, 2, 1)
    nb = S // Q_BLOCK
    qb = q.reshape(B, H, nb, Q_BLOCK, dh).transpose(2, 0, 1, 3, 4)
    Fb = F.reshape(B, H, nb, Q_BLOCK).transpose(2, 0, 1, 3)
    qpos = jnp.arange(S).reshape(nb, Q_BLOCK)
    kpos = jnp.arange(S)
    scale = dh ** -0.5

    def block(args):
        qi, Fi, pi = args
        s = jnp.einsum('bhqd,bhkd->bhqk', qi, k).astype(jnp.float32) * scale
        s = s + (Fi[..., :, None] - F[..., None, :])
        s = jnp.where(kpos[None, :] <= pi[:, None], s, -jnp.inf)
        p = jax.nn.softmax(s, axis=-1).astype(v.dtype)
        return jnp.einsum('bhqk,bhkd->bhqd', p, v)

    o = lax.map(block, (qb, Fb, qpos))
    return o.transpose(1, 0, 3, 2, 4).reshape(B, S, H * dh)


def _causal_dwconv(x, w, b):
    S = x.shape[1]
    K = w.shape[0]
    xp = jnp.pad(x, ((0, 0), (K - 1, 0), (0, 0)))
    y = xp[:, 0:S] * w[0]
    for j in range(1, K):
        y = y + xp[:, j:j + S] * w[j]
    return y + b


def _lin_combine(left, right):
    a_l, b_l = left
    a_r, b_r = right
    return a_l * a_r, a_r * b_l + b_r


def _rg_lru_branch(xb, yb, conv_w, conv_b, ga_w, ga_b, gx_w, gx_b, lam):
    B, S, _ = xb.shape
    xc = _causal_dwconv(xb, conv_w, conv_b)
    xh = xc.reshape(B, S, LRU_BLOCKS, LRU_BLOCK_DIM)
    r = jax.nn.sigmoid(jnp.einsum('bsni,nij->bsnj', xh, ga_w).reshape(B, S, LRU_WIDTH) + ga_b)
    i = jax.nn.sigmoid(jnp.einsum('bsni,nij->bsnj', xh, gx_w).reshape(B, S, LRU_WIDTH) + gx_b)
    log_a = -LRU_C * r.astype(jnp.float32) * jax.nn.softplus(-lam.astype(jnp.float32))
    a = jnp.exp(log_a)
    u = jnp.sqrt(-jnp.expm1(2 * log_a)) * (i * xc).astype(jnp.float32)
    _, h = lax.associative_scan(_lin_combine, (a, u), axis=1)
    return h.astype(xb.dtype) * jax.nn.gelu(yb)


def _rwkv7_branch(z, mu, w0, w2, a0, a2, g2, k_k, k_a, r_k, gn_w, gn_b):
    B, S, _ = z.shape
    H, N = RWKV_HEADS, RWKV_HEAD_DIM
    f32 = jnp.float32
    z = z + (_prev_token(z) - z) * mu
    r, k, v, wl, al, gl = _split(z, [RWKV_WIDTH] * 3 + [DECAY_LORA, AAA_LORA, GATE_LORA])
    w = -jax.nn.softplus(-(w0 + jnp.tanh(wl) @ w2).astype(f32)) - 0.5
    decay = jnp.exp(-jnp.exp(w))
    a = jax.nn.sigmoid((a0 + al @ a2).astype(f32))
    g = jax.nn.sigmoid(gl) @ g2
    heads = lambda t: t.reshape(B, S, H, N)
    kk = heads((k * k_k).astype(f32))
    kk = kk / jnp.maximum(jnp.linalg.norm(kk, axis=-1, keepdims=True), 1e-12)
    kf = k.astype(f32) * (1 + (a - 1) * k_a.astype(f32))
    rh, kh, vh, ah = heads(r.astype(f32)), heads(kf), heads(v.astype(f32)), heads(a)
    xs = tuple(t.transpose(1, 0, 2, 3) for t in (rh, heads(decay), kh, vh, -kk, kk * ah))

    def step(state, inp):
        r_t, w_t, k_t, v_t, a_t, b_t = inp
        sa = jnp.einsum('bhvk,bhk->bhv', state, a_t)
        state = (state * w_t[:, :, None, :] + sa[..., None] * b_t[:, :, None, :]
                 + v_t[..., None] * k_t[:, :, None, :])
        return state, jnp.einsum('bhvk,bhk->bhv', state, r_t)

    state0 = jnp.zeros((B, H, N, N), f32)
    _, ys = lax.scan(step, state0, xs)
    y = ys.transpose(1, 0, 2, 3)
    m = jnp.mean(y, axis=-1, keepdims=True)
    var = jnp.mean(jnp.square(y - m), axis=-1, keepdims=True)
    y = (y - m) * lax.rsqrt(var + RWKV_GN_EPS) * gn_w.reshape(H, N) + gn_b.reshape(H, N)
    bonus = jnp.sum(rh * kh * r_k, axis=-1, keepdims=True) * vh
    return (y + bonus).reshape(B, S, RWKV_WIDTH).astype(z.dtype) * g


def _mixer(h, w_in, f_bias, conv_w, conv_b, ga_w, ga_b, gx_w, gx_b, lam,
           mu, w0, w2, a0, a2, g2, k_k, k_a, r_k, gn_w, gn_b, wpa, wpb, wpc, w_out):
    z = h @ w_in
    za, zb, zc, zg = _split(z, [ATT_COLS, LRU_COLS, RWKV_COLS, GATE_COLS])
    q, k, v, fl = _split(za, [ATT_WIDTH] * 3 + [ATT_HEADS])
    o_a = _fox_attention(q, k, v, fl, f_bias) @ wpa
    xb, yb = _split(zb, [LRU_WIDTH, LRU_WIDTH])
    o_b = _rg_lru_branch(xb, yb, conv_w, conv_b, ga_w, ga_b, gx_w, gx_b, lam) @ wpb
    o_c = _rwkv7_branch(zc, mu, w0, w2, a0, a2, g2, k_k, k_a, r_k, gn_w, gn_b) @ wpc
    g_a, g_b, g_c = _split(jax.nn.sigmoid(zg), [D_MODEL] * N_BRANCH)
    return (g_a * o_a + g_b * o_b + g_c * o_c) @ w_out


def setup_inputs(seed: int = 0) -> dict:
    key = jax.random.key(seed)
    ks = iter(jax.random.split(key, 40))
    f32 = jnp.float32
    nrm = lambda shape, s: jax.random.normal(next(ks), shape, f32) * s
    uni = lambda shape, lo, hi: jax.random.uniform(next(ks), shape, f32, minval=lo, maxval=hi)
    L, D = DEPTH, D_MODEL
    u = uni((L, LRU_WIDTH), 0.9, 0.999)
    s = u ** (1.0 / LRU_C)
    lam = jnp.log(s) - jnp.log1p(-s)
    return {
        "x": nrm((BATCH, SEQ, D), 1.0),
        "c": nrm((BATCH, D), 1.0),
        "ada_w": nrm((L, D, N_ADA * D), D ** -0.5),
        "ada_b": nrm((L, N_ADA * D), 0.01),
        "ln_g": 1.0 + nrm((L, N_SUB, D), 0.02),
        "ln_b": nrm((L, N_SUB, D), 0.02),
        "ffn_up": nrm((L, 2, D, 2 * D_FF), D ** -0.5),
        "ffn_down": nrm((L, 2, D_FF, D), D_FF ** -0.5 * DEEPNORM_BETA),
        "w_in": nrm((L, D, D_IN), D ** -0.5),
        "fox_f_bias": uni((L, ATT_HEADS), 1.0, 6.0),
        "lru_conv_w": nrm((L, LRU_CONV, LRU_WIDTH), LRU_CONV ** -0.5),
        "lru_conv_b": nrm((L, LRU_WIDTH), 0.01),
        "lru_ga_w": nrm((L, LRU_BLOCKS, LRU_BLOCK_DIM, LRU_BLOCK_DIM), LRU_BLOCK_DIM ** -0.5),
        "lru_ga_b": nrm((L, LRU_WIDTH), 0.01),
        "lru_gx_w": nrm((L, LRU_BLOCKS, LRU_BLOCK_DIM, LRU_BLOCK_DIM), LRU_BLOCK_DIM ** -0.5),
        "lru_gx_b": nrm((L, LRU_WIDTH), 0.01),
        "lru_lambda": lam,
        "rwkv_mu": uni((L, RWKV_COLS), 0.0, 1.0),
        "rwkv_w0": uni((L, RWKV_WIDTH), -6.0, -1.0),
        "rwkv_w2": nrm((L, DECAY_LORA, RWKV_WIDTH), 0.1 * DECAY_LORA ** -0.5),
        "rwkv_a0": nrm((L, RWKV_WIDTH), 0.1),
        "rwkv_a2": nrm((L, AAA_LORA, RWKV_WIDTH), 0.1 * AAA_LORA ** -0.5),
        "rwkv_g2": nrm((L, GATE_LORA, RWKV_WIDTH), GATE_LORA ** -0.5),
        "rwkv_k_k": 0.85 + nrm((L, RWKV_WIDTH), 0.05),
        "rwkv_k_a": 1.0 + nrm((L, RWKV_WIDTH), 0.05),
        "rwkv_r_k": nrm((L, RWKV_HEADS, RWKV_HEAD_DIM), 0.1),
        "rwkv_gn_w": 1.0 + nrm((L, RWKV_WIDTH), 0.02),
        "rwkv_gn_b": nrm((L, RWKV_WIDTH), 0.02),
        "w_proj_a": nrm((L, ATT_WIDTH, D), ATT_WIDTH ** -0.5 * DEEPNORM_BETA),
        "w_proj_b": nrm((L, LRU_WIDTH, D), LRU_WIDTH ** -0.5 * DEEPNORM_BETA),
        "w_proj_c": nrm((L, RWKV_WIDTH, D), RWKV_WIDTH ** -0.5 * DEEPNORM_BETA),
        "w_out": nrm((L, D, D), D ** -0.5 * DEEPNORM_BETA),
    }


def reference(x, c, ada_w, ada_b, ln_g, ln_b, ffn_up, ffn_down, w_in, fox_f_bias,
              lru_conv_w, lru_conv_b, lru_ga_w, lru_ga_b, lru_gx_w, lru_gx_b, lru_lambda,
              rwkv_mu, rwkv_w0, rwkv_w2, rwkv_a0, rwkv_a2, rwkv_g2, rwkv_k_k, rwkv_k_a, rwkv_r_k,
              rwkv_gn_w, rwkv_gn_b, w_proj_a, w_proj_b, w_proj_c, w_out):
    c_act = jax.nn.silu(c)
    for l in range(DEPTH):
        ada = c_act @ ada_w[l] + ada_b[l]
        sh1, sc1, g1, sh2, sc2, g2, sh3, sc3, g3 = jnp.split(ada, N_ADA, axis=-1)
        h = _modulate(x, sh1, sc1)
        y = 0.5 * g1[:, None, :] * _swiglu(h, ffn_up[l, 0], ffn_down[l, 0])
        x = _post_norm(DEEPNORM_ALPHA * x + y, ln_g[l, 0], ln_b[l, 0])
        h = _modulate(x, sh2, sc2)
        y = g2[:, None, :] * _mixer(
            h, w_in[l], fox_f_bias[l],
            lru_conv_w[l], lru_conv_b[l], lru_ga_w[l], lru_ga_b[l], lru_gx_w[l], lru_gx_b[l], lru_lambda[l],
            rwkv_mu[l], rwkv_w0[l], rwkv_w2[l], rwkv_a0[l], rwkv_a2[l], rwkv_g2[l], rwkv_k_k[l], rwkv_k_a[l],
            rwkv_r_k[l], rwkv_gn_w[l], rwkv_gn_b[l],
            w_proj_a[l], w_proj_b[l], w_proj_c[l], w_out[l])
        x = _post_norm(DEEPNORM_ALPHA * x + y, ln_g[l, 1], ln_b[l, 1])
        h = _modulate(x, sh3, sc3)
        y = 0.5 * g3[:, None, :] * _swiglu(h, ffn_up[l, 1], ffn_down[l, 1])
        x = _post_norm(DEEPNORM_ALPHA * x + y, ln_g[l, 2], ln_b[l, 2])
    return x
```

```python
import numpy as np
import ml_dtypes
from contextlib import ExitStack

import concourse.bass as bass
import concourse.mybir as mybir
from concourse.bass_utils import run_bass_kernel_spmd

F32 = mybir.dt.float32
BF16 = mybir.dt.bfloat16
AF = mybir.ActivationFunctionType
ALU = mybir.AluOpType
AX = mybir.AxisListType

NCORES = 8
D = 1024
SEQ = 16384
DEPTH = 4
TPC = SEQ // NCORES
DFF = 2816
NJ = DFF // 128
KC = D // 128
ALPHA = (2 * DEPTH) ** 0.25
LN_EPS = 1e-5
HD = 64


class Tok:
    __slots__ = ("sem", "sid", "val")

    def __init__(self, sem, sid, val):
        self.sem, self.sid, self.val = sem, sid, val


class Buf:
    __slots__ = ("name", "w", "r")

    def __init__(self, name=""):
        self.name = name
        self.w = None
        self.r = []


class Prog:
    def __init__(self, nc, es, n_dma_sems=12):
        self.nc = nc
        self.es = es
        self.engs = {"pe": nc.tensor, "act": nc.scalar, "dve": nc.vector,
                     "pool": nc.gpsimd, "sp": nc.sync}
        self.esem = {}
        self.ecnt = {}
        self.seen = {e: {} for e in self.engs}
        self._sid = 0
        for e in self.engs:
            self.esem[e] = (es.enter_context(nc.semaphore("sem_" + e)), self._newsid())
            self.ecnt[e] = 0
        self.dsem = {}
        self.dpos = {}
        for q in ("sp", "act", "pool"):
            ring = []
            for i in range(n_dma_sems):
                ring.append([es.enter_context(nc.semaphore("dma_%s_%d" % (q, i))), self._newsid(), 0, None])
            self.dsem[q] = ring
            self.dpos[q] = 0
        self.out_toks = []

    def _newsid(self):
        self._sid += 1
        return self._sid

    def buf(self, name=""):
        return Buf(name)

    def bufs(self, n, name=""):
        return [Buf(name + str(i)) for i in range(n)]

    def _wait(self, e, tok):
        if tok is None:
            return
        if e == "pe" and tok.sid == self.esem["pe"][1]:
            return
        if self.seen[e].get(tok.sid, 0) >= tok.val:
            return
        self.engs[e].wait_ge(tok.sem, tok.val)
        self.seen[e][tok.sid] = tok.val

    def _deps(self, e, reads, writes):
        for b in reads:
            if b.w is not None:
                self._wait(e, b.w)
        for b in writes:
            if b.w is not None:
                self._wait(e, b.w)
            for t in b.r:
                self._wait(e, t)

    def _commit(self, tok, reads, writes):
        for b in reads:
            b.r.append(tok)
            if len(b.r) > 64:
                b.r = b.r[-64:]
        for b in writes:
            b.w = tok
            b.r = []

    def op(self, e, fn, reads=(), writes=(), inc=True):
        self._deps(e, reads, writes)
        ins = fn(self.engs[e])
        sem, sid = self.esem[e]
        if inc:
            self.ecnt[e] += 1
            ins.then_inc(sem, 1)
            tok = Tok(sem, sid, self.ecnt[e])
        else:
            assert e == "pe"
            tok = Tok(sem, sid, self.ecnt[e] + 1)
        self._commit(tok, reads, writes)
        return tok

    def dma(self, q, out, in_, reads=(), writes=(), is_output=False):
        ring = self.dsem[q]
        slot = ring[self.dpos[q] % len(ring)]
        self.dpos[q] += 1
        if slot[3] is not None:
            self._wait(q, slot[3])
        self._deps(q, reads, writes)
        ins = self.engs[q].dma_start(out=out, in_=in_)
        slot[2] += 16
        ins.then_inc(slot[0], 16)
        tok = Tok(slot[0], slot[1], slot[2])
        slot[3] = tok
        self._commit(tok, reads, writes)
        if is_output:
            self.out_toks.append(tok)
        return tok

    def barrier(self):
        toks = []
        for e in self.engs:
            if self.ecnt[e] > 0:
                toks.append(Tok(self.esem[e][0], self.esem[e][1], self.ecnt[e]))
        for q in self.dsem:
            for slot in self.dsem[q]:
                if slot[3] is not None:
                    toks.append(slot[3])
        for e in self.engs:
            for t in toks:
                self._wait(e, t)

    def finish(self):
        for t in self.out_toks:
            self._wait("sp", t)
        self.barrier()


TT = 512
NTT = TPC // TT


def _mm(P, ps_ap, lhsT, rhs, start, stop, reads, writes, inc):
    return P.op("pe", lambda e: e.matmul(ps_ap, lhsT, rhs, start=start, stop=stop),
                reads=reads, writes=writes, inc=inc)


class TokCtx:
    def __init__(self, P, nc, es, T=TPC):
        self.P, self.nc = P, nc
        self.T = T
        self.ntt = T // TT
        NTT = self.ntt
        sb = lambda name, shape, dt: es.enter_context(nc.sbuf_tensor("sb_" + name, shape, dt))
        self.xT = sb("xT", [128, KC, T], F32)
        self.hT = sb("hT", [128, KC, T], BF16)
        self.xB = [[P.buf("x%d_%d" % (n, t)) for t in range(NTT)] for n in range(KC)]
        self.hB = [[P.buf("h%d_%d" % (n, t)) for t in range(NTT)] for n in range(KC)]
        self.ones = sb("ones", [128, 128], BF16)
        self.onesB = P.buf("ones")
        self.vec = sb("vec", [128, 40], F32)
        self.vecB = P.buf("vec")
        self.gs = sb("gs", [128, KC], F32)
        self.sc1 = sb("sc1", [128, KC], F32)
        self.gsB = P.buf("gs")
        self.xb = sb("xb16", [128, KC, TT], BF16)
        self.sq = sb("sq16", [128, KC, TT], BF16)
        self.xbB, self.sqB = P.buf("xb"), P.buf("sq")
        self.m = sb("ln_m", [128, TT], F32)
        self.var = sb("ln_var", [128, TT], F32)
        self.rstd = sb("ln_rstd", [128, TT], F32)
        self.mB, self.varB, self.rstdB = P.buf("m"), P.buf("var"), P.buf("rstd")
        self.t1 = [sb("ln_t%d" % i, [128, TT], F32) for i in range(2)]
        self.t1B = P.bufs(2, "t1")
        self.ps = [es.enter_context(nc.psum_tensor("ps%d" % i, [128, TT], F32)) for i in range(8)]
        self.psB = P.bufs(8, "ps")
        P.op("pool", lambda e: e.memset(self.ones[:], 1.0), writes=[self.onesB])

    def load_vec(self, vec_d, gmul):
        P = self.P
        P.dma("sp", self.vec[:], vec_d, writes=[self.vecB])
        P.op("dve", lambda e: e.tensor_scalar(out=self.gs[:], in0=self.vec[:, 0:8], scalar1=float(gmul),
                                              scalar2=None, op0=ALU.mult),
             reads=[self.vecB], writes=[self.gsB])
        P.op("dve", lambda e: e.tensor_scalar(out=self.sc1[:], in0=self.vec[:, 32:40], scalar1=1.0,
                                              scalar2=None, op0=ALU.add),
             reads=[self.vecB], writes=[self.gsB])

    def load_x(self, xT_d, off=0):
        xv = xT_d.rearrange("(n p) t -> p n t", p=128)
        for n in range(KC):
            self.P.dma("sp", self.xT[:, n, :], xv[:, n, off:off + self.T], writes=self.xB[n])

    def load_h(self, hT_d, off=0):
        hv = hT_d.rearrange("(n p) t -> p n t", p=128)
        for n in range(KC):
            self.P.dma("sp", self.hT[:, n, :], hv[:, n, off:off + self.T], writes=self.hB[n])

    def ln_tile(self, tt, eps, scale_ap, bias_ap, out_ap, out_bufs, extra_reads):
        P = self.P
        sl = slice(tt * TT, (tt + 1) * TT)
        xin = [self.xB[n][tt] for n in range(KC)]
        P.op("act", lambda e: e.activation(out=self.sq[:], in_=self.xT[:, :, sl], func=AF.Square),
             reads=xin, writes=[self.sqB])
        P.op("pool", lambda e: e.tensor_copy(out=self.xb[:], in_=self.xT[:, :, sl]),
             reads=xin, writes=[self.xbB])
        s1, s2 = self.ps[6], self.ps[7]
        for n in range(KC):
            _mm(P, s1[:], self.ones[:], self.xb[:, n, :], n == 0, n == KC - 1,
                [self.onesB, self.xbB], [self.psB[6]], n == KC - 1)
        for n in range(KC):
            _mm(P, s2[:], self.ones[:], self.sq[:, n, :], n == 0, n == KC - 1,
                [self.onesB, self.sqB], [self.psB[7]], n == KC - 1)
        P.op("act", lambda e: e.activation(out=self.m[:], in_=s1[:], func=AF.Copy, scale=1.0 / D),
             reads=[self.psB[6]], writes=[self.mB])
        P.op("dve", lambda e: e.tensor_tensor(out=self.var[:], in0=self.m[:], in1=self.m[:], op=ALU.mult),
             reads=[self.mB], writes=[self.varB])
        P.op("dve", lambda e: e.scalar_tensor_tensor(out=self.var[:], in0=s2[:], scalar=1.0 / D, in1=self.var[:],
                                                     op0=ALU.mult, op1=ALU.subtract),
             reads=[self.psB[7], self.varB], writes=[self.varB])
        P.op("dve", lambda e: e.tensor_scalar(out=self.var[:], in0=self.var[:], scalar1=float(eps), scalar2=None,
                                              op0=ALU.add),
             reads=[self.varB], writes=[self.varB])
        P.op("act", lambda e: e.activation(out=self.var[:], in_=self.var[:], func=AF.Sqrt),
             reads=[self.varB], writes=[self.varB])
        P.op("dve", lambda e: e.reciprocal(out=self.rstd[:], in_=self.var[:]),
             reads=[self.varB], writes=[self.rstdB])
        for n in range(KC):
            k = n % 2
            t1, t1B = self.t1[k], self.t1B[k]
            P.op("dve", lambda e: e.tensor_tensor(out=t1[:], in0=self.xT[:, n, sl], in1=self.m[:], op=ALU.subtract),
                 reads=[self.xB[n][tt], self.mB], writes=[t1B])
            P.op("dve", lambda e: e.tensor_tensor(out=t1[:], in0=t1[:], in1=self.rstd[:], op=ALU.mult),
                 reads=[t1B, self.rstdB], writes=[t1B])
            P.op("dve", lambda e: e.tensor_scalar(out=out_ap(n), in0=t1[:], scalar1=scale_ap(n), scalar2=bias_ap(n),
                                                  op0=ALU.mult, op1=ALU.add),
                 reads=[t1B] + extra_reads, writes=[out_bufs(n)])

    def modulate_only(self, hT_out_d, off=0):
        P = self.P
        ho = hT_out_d.rearrange("(n p) t -> p n t", p=128)
        for tt in range(self.ntt):
            sl = slice(tt * TT, (tt + 1) * TT)
            osl = slice(off + tt * TT, off + (tt + 1) * TT)
            self.ln_tile(tt, LN_EPS,
                         lambda n: self.sc1[:, n:n + 1], lambda n: self.vec[:, 24 + n:25 + n],
                         lambda n: self.hT[:, n, sl], lambda n: self.hB[n][tt], [self.vecB, self.gsB])
            for n in range(KC):
                P.dma("sp", ho[:, n, osl], self.hT[:, n, sl], reads=[self.hB[n][tt]], is_output=True)

    def postnorm_and_modulate(self, xT_out_d, hT_out_d, off=0):
        P = self.P
        xo = xT_out_d.rearrange("(n p) t -> p n t", p=128)
        ho = hT_out_d.rearrange("(n p) t -> p n t", p=128) if hT_out_d is not None else None
        for tt in range(self.ntt):
            sl = slice(tt * TT, (tt + 1) * TT)
            osl = slice(off + tt * TT, off + (tt + 1) * TT)
            self.ln_tile(tt, LN_EPS / (ALPHA * ALPHA),
                         lambda n: self.vec[:, 8 + n:9 + n], lambda n: self.vec[:, 16 + n:17 + n],
                         lambda n: self.xT[:, n, sl], lambda n: self.xB[n][tt], [self.vecB])
            for n in range(KC):
                P.dma("sp", xo[:, n, osl], self.xT[:, n, sl], reads=[self.xB[n][tt]], is_output=True)
            if ho is not None:
                self.ln_tile(tt, LN_EPS,
                             lambda n: self.sc1[:, n:n + 1], lambda n: self.vec[:, 24 + n:25 + n],
                             lambda n: self.hT[:, n, sl], lambda n: self.hB[n][tt], [self.vecB, self.gsB])
                for n in range(KC):
                    P.dma("sp", ho[:, n, osl], self.hT[:, n, sl], reads=[self.hB[n][tt]], is_output=True)


def emit_ffn(C, es, w_up_d, w_down_d):
    P, nc = C.P, C.nc
    sb = lambda name, shape, dt: es.enter_context(nc.sbuf_tensor("sb_" + name, shape, dt))
    NH = NJ // 2
    aT = sb("aT", [128, NH, TPC], BF16)
    aB = [[P.buf() for t in range(NTT)] for j in range(NH)]
    NWB = 3
    wup = [sb("wup%d" % i, [128, KC, 256], BF16) for i in range(NWB)]
    wupB = P.bufs(NWB, "wup")
    wdn = [sb("wdn%d" % i, [128, NH, 128], BF16) for i in range(2)]
    wdnB = P.bufs(2, "wdn")
    st = [sb("silu%d" % i, [128, TT], F32) for i in range(2)]
    stB = P.bufs(2, "silu")
    wu_v = w_up_d.rearrange("(kc p) n -> p kc n", p=128)
    wd_v = w_down_d.rearrange("(j p) n -> p j n", p=128)
    it = 0
    for hf in range(2):
        for jj in range(NH):
            j = hf * NH + jj
            wb = (hf * NH + jj) % NWB
            P.dma("pool", wup[wb][:, :, 0:128], wu_v[:, :, j * 128:(j + 1) * 128], writes=[wupB[wb]])
            P.dma("pool", wup[wb][:, :, 128:256], wu_v[:, :, DFF + j * 128:DFF + (j + 1) * 128], writes=[wupB[wb]])
            for tt in range(NTT):
                sl = slice(tt * TT, (tt + 1) * TT)
                b = it % 2
                it += 1
                pu, pg = C.ps[b], C.ps[2 + b]
                for kc in range(KC):
                    _mm(P, pu[:], wup[wb][:, kc, 0:128], C.hT[:, kc, sl], kc == 0, kc == KC - 1,
                        [wupB[wb], C.hB[kc][tt]], [C.psB[b]], kc == KC - 1)
                for kc in range(KC):
                    _mm(P, pg[:], wup[wb][:, kc, 128:256], C.hT[:, kc, sl], kc == 0, kc == KC - 1,
                        [wupB[wb], C.hB[kc][tt]], [C.psB[2 + b]], kc == KC - 1)
                P.op("act", lambda e: e.activation(out=st[b][:], in_=pu[:], func=AF.Silu),
                     reads=[C.psB[b]], writes=[stB[b]])
                P.op("dve", lambda e: e.tensor_tensor(out=aT[:, jj, sl], in0=pg[:], in1=st[b][:], op=ALU.mult),
                     reads=[C.psB[2 + b], stB[b]], writes=[aB[jj][tt]])
        for n in range(KC):
            wb = n % 2
            P.dma("pool", wdn[wb][:], wd_v[:, hf * NH:(hf + 1) * NH, n * 128:(n + 1) * 128], writes=[wdnB[wb]])
            for tt in range(NTT):
                sl = slice(tt * TT, (tt + 1) * TT)
                b = (n * NTT + tt) % 2
                py = C.ps[4 + b]
                for jj in range(NH):
                    _mm(P, py[:], wdn[wb][:, jj, :], aT[:, jj, sl], jj == 0, jj == NH - 1,
                        [wdnB[wb], aB[jj][tt]], [C.psB[4 + b]], jj == NH - 1)
                P.op("dve", lambda e: e.scalar_tensor_tensor(out=C.xT[:, n, sl], in0=py[:], scalar=C.gs[:, n:n + 1],
                                                             in1=C.xT[:, n, sl], op0=ALU.mult, op1=ALU.add),
                     reads=[C.psB[4 + b], C.gsB, C.xB[n][tt]], writes=[C.xB[n][tt]])


def build_ffn_prog(final=False):
    nc = bass.Bass("TRN2", target_bir_lowering=False)
    xT_d = nc.dram_tensor("xT", [D, TPC], F32, kind="ExternalInput").ap()
    hT_d = nc.dram_tensor("hT", [D, TPC], BF16, kind="ExternalInput").ap()
    wu_d = nc.dram_tensor("w_up", [D, 2 * DFF], F32, kind="ExternalInput").ap()
    wd_d = nc.dram_tensor("w_down", [DFF, D], F32, kind="ExternalInput").ap()
    vec_d = nc.dram_tensor("vec", [128, 40], F32, kind="ExternalInput").ap()
    xo_d = nc.dram_tensor("xT_out", [D, TPC], F32, kind="ExternalOutput").ap()
    ho_d = None if final else nc.dram_tensor("hT_out", [D, TPC], BF16, kind="ExternalOutput").ap()
    with ExitStack() as es:
        P = Prog(nc, es)
        C = TokCtx(P, nc, es)
        C.load_vec(vec_d, 0.5 / ALPHA)
        C.load_x(xT_d)
        C.load_h(hT_d)
        with ExitStack() as es2:
            emit_ffn(C, es2, wu_d, wd_d)
            C.postnorm_and_modulate(xo_d, ho_d)
            P.finish()
    return nc


def build_mod_prog():
    nc = bass.Bass("TRN2", target_bir_lowering=False)
    xT_d = nc.dram_tensor("xT", [D, TPC], F32, kind="ExternalInput").ap()
    vec_d = nc.dram_tensor("vec", [128, 40], F32, kind="ExternalInput").ap()
    ho_d = nc.dram_tensor("hT_out", [D, TPC], BF16, kind="ExternalOutput").ap()
    with ExitStack() as es:
        P = Prog(nc, es)
        C = TokCtx(P, nc, es)
        C.load_vec(vec_d, 1.0)
        C.load_x(xT_d)
        C.modulate_only(ho_d)
        P.finish()
    return nc


def build_mixpost_prog():
    nc = bass.Bass("TRN2", target_bir_lowering=False)
    dt = lambda name, shape, dty, kind="ExternalInput": nc.dram_tensor(name, shape, dty, kind=kind).ap()
    xT_d = dt("xT", [D, TPC], F32)
    hT_d = dt("hT", [D, TPC], BF16)
    br_d = dt("brT", [2048, TPC], BF16)
    wg_d = dt("wg", [D, 3072], F32)
    wp_d = dt("wp", [2048, D], F32)
    wo_d = dt("wo", [D, D], F32)
    vec_d = dt("vec", [128, 40], F32)
    xo_d = dt("xT_out", [D, TPC], F32, "ExternalOutput")
    ho_d = dt("hT_out", [D, TPC], BF16, "ExternalOutput")
    with ExitStack() as es:
        P = Prog(nc, es)
        C = TokCtx(P, nc, es, T=TT)
        sb = lambda name, shape, dty: es.enter_context(nc.sbuf_tensor("sb_" + name, shape, dty))
        C.load_vec(vec_d, 1.0 / ALPHA)
        wg = sb("wg", [128, KC, 3072], BF16)
        wp = sb("wp", [128, 16, D], BF16)
        wo = sb("wo", [128, KC, D], BF16)
        wB = P.buf()
        wgv = wg_d.rearrange("(kc p) n -> p kc n", p=128)
        for kc in range(KC):
            P.dma("pool", wg[:, kc, :], wgv[:, kc, :], writes=[wB])
        wpv = wp_d.rearrange("(kc p) n -> p kc n", p=128)
        for kc in range(16):
            P.dma("pool", wp[:, kc, :], wpv[:, kc, :], writes=[wB])
        P.dma("pool", wo[:], wo_d.rearrange("(kc p) n -> p kc n", p=128), writes=[wB])
        br = sb("br", [128, 16, TT], BF16)
        brB = P.buf()
        mT = sb("mT", [128, KC, TT], BF16)
        mB = P.bufs(KC)
        sg = [sb("sg%d" % i, [128, TT], F32) for i in range(3)]
        sgB = P.bufs(3)
        ta = [sb("ta%d" % i, [128, TT], F32) for i in range(2)]
        taB = P.bufs(2)
        brv = br_d.rearrange("(kc p) t -> p kc t", p=128)
        for tk in range(TPC // TT):
            off = tk * TT
            C.load_x(xT_d, off)
            C.load_h(hT_d, off)
            P.dma("sp", br[:], brv[:, :, off:off + TT], writes=[brB])
            for n in range(KC):
                ns = slice(n * 128, (n + 1) * 128)
                for g in range(3):
                    for kc in range(KC):
                        _mm(P, C.ps[g][:], wg[:, kc, g * 1024 + n * 128:g * 1024 + (n + 1) * 128], C.hT[:, kc, :], kc == 0, kc == KC - 1,
                            [wB, C.hB[kc][0]], [C.psB[g]], kc == KC - 1)
                for g, (k0, nk) in enumerate(((0, 4), (4, 8), (12, 4))):
                    for kc in range(nk):
                        _mm(P, C.ps[3 + g][:], wp[:, k0 + kc, ns], br[:, k0 + kc, :], kc == 0, kc == nk - 1,
                            [wB, brB], [C.psB[3 + g]], kc == nk - 1)
                for g in range(3):
                    P.op("act", lambda e: e.activation(out=sg[g][:], in_=C.ps[g][:], func=AF.Sigmoid), reads=[C.psB[g]], writes=[sgB[g]])
                P.op("dve", lambda e: e.tensor_tensor(out=ta[0][:], in0=C.ps[3][:], in1=sg[0][:], op=ALU.mult),
                     reads=[C.psB[3], sgB[0]], writes=[taB[0]])
                P.op("dve", lambda e: e.tensor_tensor(out=ta[1][:], in0=C.ps[4][:], in1=sg[1][:], op=ALU.mult),
                     reads=[C.psB[4], sgB[1]], writes=[taB[1]])
                P.op("dve", lambda e: e.tensor_tensor(out=ta[0][:], in0=ta[0][:], in1=ta[1][:], op=ALU.add),
                     reads=[taB[0], taB[1]], writes=[taB[0]])
                P.op("dve", lambda e: e.tensor_tensor(out=ta[1][:], in0=C.ps[5][:], in1=sg[2][:], op=ALU.mult),
                     reads=[C.psB[5], sgB[2]], writes=[taB[1]])
                P.op("dve", lambda e: e.tensor_tensor(out=mT[:, n, :], in0=ta[0][:], in1=ta[1][:], op=ALU.add),
                     reads=[taB[0], taB[1]], writes=[mB[n]])
            for n2 in range(KC):
                b = n2 % 2
                for n in range(KC):
                    _mm(P, C.ps[b][:], wo[:, n, n2 * 128:(n2 + 1) * 128], mT[:, n, :], n == 0, n == KC - 1,
                        [wB, mB[n]], [C.psB[b]], n == KC - 1)
                P.op("dve", lambda e: e.scalar_tensor_tensor(out=C.xT[:, n2, :], in0=C.ps[b][:], scalar=C.gs[:, n2:n2 + 1],
                                                             in1=C.xT[:, n2, :], op0=ALU.mult, op1=ALU.add),
                     reads=[C.psB[b], C.gsB, C.xB[n2][0]], writes=[C.xB[n2][0]])
            C.postnorm_and_modulate(xo_d, ho_d, off)
        P.finish()
    return nc


def build_ada_prog():
    nc = bass.Bass("TRN2", target_bir_lowering=False)
    dt = lambda name, shape, dty, kind="ExternalInput": nc.dram_tensor(name, shape, dty, kind=kind).ap()
    c_d = dt("c", [128, KC], F32)
    w_d = dt("ada_w", [DEPTH, D, 1152], F32)
    b_d = dt("ada_b", [128, DEPTH * 9], F32)
    o_d = dt("ada_out", [128, DEPTH * 9], F32, "ExternalOutput")
    with ExitStack() as es:
        P = Prog(nc, es)
        sb = lambda name, shape, dty: es.enter_context(nc.sbuf_tensor("sb_" + name, shape, dty))
        ct = sb("c", [128, KC], F32)
        bt = sb("b", [128, DEPTH * 9], F32)
        ot = sb("o", [128, DEPTH * 9], F32)
        cB, bB, oB = P.buf(), P.buf(), P.buf()
        P.dma("sp", ct[:], c_d, writes=[cB])
        P.dma("sp", bt[:], b_d, writes=[bB])
        P.op("act", lambda e: e.activation(out=ct[:], in_=ct[:], func=AF.Silu), reads=[cB], writes=[cB])
        ps = es.enter_context(nc.psum_tensor("ps", [128, 64], F32))
        psB = P.buf()
        wt = [sb("w%d" % i, [128, KC, 1152], F32) for i in range(2)]
        wtB = P.bufs(2)
        for l in range(DEPTH):
            w, wB = wt[l % 2], wtB[l % 2]
            wv = w_d[l].rearrange("(kc p) n -> p kc n", p=128)
            for kc in range(KC):
                P.dma("sp", w[:, kc, :], wv[:, kc, :], writes=[wB])
            for ch in range(9):
                col = l * 9 + ch
                for kc in range(KC):
                    _mm(P, ps[:, col:col + 1], w[:, kc, ch * 128:(ch + 1) * 128], ct[:, kc:kc + 1], kc == 0, kc == KC - 1,
                        [wB, cB], [psB], kc == KC - 1)
        P.op("dve", lambda e: e.tensor_tensor(out=ot[:], in0=ps[:, 0:DEPTH * 9], in1=bt[:], op=ALU.add), reads=[psB, bB], writes=[oB])
        P.dma("sp", o_d, ot[:], reads=[oB], is_output=True)
        P.finish()
    return nc


NQT = SEQ // TT
NKB = SEQ // 128
MASKNEG = -30000.0


class MixCtx:
    def __init__(self, P, nc, es, hT_d):
        self.P, self.nc = P, nc
        self.sb = lambda name, shape, dt: es.enter_context(nc.sbuf_tensor("sb_" + name, shape, dt))
        self.hv = hT_d.rearrange("(n p) t -> p n t", p=128)
        self.psall = es.enter_context(nc.psum_tensor("psall", [128, 8 * TT], F32))
        self.ps = [self.psall[:, i * TT:(i + 1) * TT] for i in range(8)]
        self.psB = P.bufs(8, "ps")
        self.hbuf = [self.sb("hbuf%d" % i, [128, KC, TT], BF16) for i in range(2)]
        self.hbufB = P.bufs(2, "hbuf")
        self.hcnt = 0

    def load_h_tile(self, tt):
        i = self.hcnt % 2
        self.hcnt += 1
        self.P.dma("sp", self.hbuf[i][:], self.hv[:, :, tt * TT:(tt + 1) * TT], writes=[self.hbufB[i]])
        return self.hbuf[i], self.hbufB[i]

    def load_w(self, name, w_d, ncols):
        t = self.sb(name, [128, KC, ncols], BF16)
        b = self.P.buf(name)
        self.P.dma("pool", t[:], w_d.rearrange("(kc p) n -> p kc n", p=128), writes=[b])
        return t, b


def emit_attention(M, es, w_att_d, avec_d, cmask_d, attT_out_d):
    P, nc = M.P, M.nc
    sb = lambda name, shape, dt: es.enter_context(nc.sbuf_tensor("sb_" + name, shape, dt))
    W, WB = M.load_w("w_att", w_att_d, 193)
    Qx = sb("Qx", [70, SEQ], BF16)
    Kx = sb("Kx", [70, SEQ], BF16)
    Vx = sb("Vx", [128, NKB, 65], BF16)
    QB = [P.buf() for _ in range(NQT)]
    KB = [P.buf() for _ in range(NQT)]
    VB = [P.buf() for _ in range(NQT)]
    avec = sb("avec", [128, 4], F32)
    avecB = P.buf()
    P.dma("sp", avec[:], avec_d, writes=[avecB])
    cmask = sb("cmask", [128, 4, TT], BF16)
    ident = sb("identb", [128, 128], BF16)
    cB = P.buf()
    P.dma("sp", cmask[:], cmask_d[:, 0:4 * TT].rearrange("p (a t) -> p a t", a=4), writes=[cB])
    P.dma("sp", ident[:], cmask_d[:, 4 * TT:4 * TT + 128], writes=[cB])
    sel = sb("sel", [128, 8, 70], BF16)
    onesr = sb("onesr", [128, TT], BF16)
    onesf = sb("onesf", [128, TT], F32)
    selB = P.buf()
    P.op("pool", lambda e: e.memset(sel[:], 0.0), writes=[selB])
    P.op("pool", lambda e: e.memset(onesr[:], 1.0), writes=[selB])
    P.op("pool", lambda e: e.memset(onesf[:], 1.0), writes=[selB])
    for i, (c0, c1, v) in enumerate([(64, 67, 1.0), (67, 68, 1.0), (68, 69, 1.0), (69, 70, 1.0),
                                     (67, 70, -1.0), (64, 65, 1.0), (65, 66, 1.0), (66, 67, 1.0)]):
        P.op("pool", lambda e: e.memset(sel[64:65, i, c0:c1], v), writes=[selB])
    P.op("pool", lambda e: e.memset(Vx[:, :, 64:65], 1.0), writes=VB)
    nfb = sb("nfb", [128, 1], F32)
    P.op("dve", lambda e: e.tensor_scalar(out=nfb[:], in0=avec[:, 0:1], scalar1=-1.0, scalar2=None, op0=ALU.mult),
         reads=[avecB], writes=[avecB])
    fl = sb("fl", [128, TT], F32)
    fr = [sb("fr%d" % i, [128, TT], F32) for i in range(2)]
    f16 = [sb("f16_%d" % i, [128, TT], BF16) for i in range(3)]
    flB, frB, f16B = P.buf(), P.bufs(2), P.bufs(3)
    carry = sb("fcarry", [128, 1], F32)
    carryB = P.buf()
    P.op("dve", lambda e: e.memset(carry[:], 0.0), writes=[carryB])
    r64 = slice(64, 65)
    for tt in range(NQT):
        sl = slice(tt * TT, (tt + 1) * TT)
        hb, hbB = M.load_h_tile(tt)
        pq, pk = M.ps[0], M.ps[1]
        for kc in range(KC):
            _mm(P, pq[0:64, :], W[:, kc, 0:64], hb[:, kc, :], kc == 0, kc == KC - 1, [WB, hbB], [M.psB[0]], kc == KC - 1)
        for kc in range(KC):
            _mm(P, pk[0:65, :], W[:, kc, 64:129], hb[:, kc, :], kc == 0, kc == KC - 1, [WB, hbB], [M.psB[1]], kc == KC - 1)
        P.op("act", lambda e: e.activation(out=Qx[0:64, sl], in_=pq[0:64, :], func=AF.Copy, scale=0.125),
             reads=[M.psB[0]], writes=[QB[tt]])
        P.op("dve", lambda e: e.tensor_copy(out=Kx[0:64, sl], in_=pk[0:64, :]), reads=[M.psB[1]], writes=[KB[tt]])
        P.op("act", lambda e: e.activation(out=fl[r64, :], in_=pk[r64, :], func=AF.Exp, scale=-1.0, bias=nfb[r64, :]),
             reads=[M.psB[1], avecB], writes=[flB])
        P.op("act", lambda e: e.activation(out=fl[r64, :], in_=fl[r64, :], func=AF.Ln, bias=1.0),
             reads=[flB], writes=[flB])
        P.op("dve", lambda e: e.tensor_tensor_scan(out=fr[0][r64, :], data0=onesf[r64, :], data1=fl[r64, :],
                                                   initial=carry[r64, :], op0=ALU.mult, op1=ALU.add),
             reads=[flB, carryB, selB], writes=[frB[0]])
        P.op("dve", lambda e: e.tensor_copy(out=carry[r64, :], in_=fr[0][r64, TT - 1:TT]), reads=[frB[0]], writes=[carryB])
        P.op("dve", lambda e: e.tensor_copy(out=f16[0][r64, :], in_=fr[0][r64, :]), reads=[frB[0]], writes=[f16B[0]])
        P.op("dve", lambda e: e.tensor_tensor(out=fr[1][r64, :], in0=fr[0][r64, :], in1=f16[0][r64, :], op=ALU.subtract),
             reads=[frB[0], f16B[0]], writes=[frB[1]])
        P.op("dve", lambda e: e.tensor_copy(out=f16[1][r64, :], in_=fr[1][r64, :]), reads=[frB[1]], writes=[f16B[1]])
        P.op("dve", lambda e: e.tensor_tensor(out=fr[0][r64, :], in0=fr[1][r64, :], in1=f16[1][r64, :], op=ALU.subtract),
             reads=[frB[1], f16B[1]], writes=[frB[0]])
        P.op("dve", lambda e: e.tensor_copy(out=f16[2][r64, :], in_=fr[0][r64, :]), reads=[frB[0]], writes=[f16B[2]])
        pa, pb = M.ps[2], M.ps[3]
        srcs = [onesr, f16[0], f16[1], f16[2]]
        srcB = [selB, f16B[0], f16B[1], f16B[2]]
        for i in range(4):
            _mm(P, pa[0:70, :], sel[r64, i, :], srcs[i][r64, :], i == 0, i == 3, [selB, srcB[i]], [M.psB[2]], i == 3)
        for i in range(4):
            _mm(P, pb[0:70, :], sel[r64, 4 + i, :], srcs[i][r64, :], i == 0, i == 3, [selB, srcB[i]], [M.psB[3]], i == 3)
        P.op("act", lambda e: e.activation(out=Qx[64:70, sl], in_=pa[64:70, :], func=AF.Copy),
             reads=[M.psB[2]], writes=[QB[tt]])
        P.op("dve", lambda e: e.tensor_copy(out=Kx[64:70, sl], in_=pb[64:70, :]), reads=[M.psB[3]], writes=[KB[tt]])
        pv = M.ps[4 + tt % 2]
        for bk in range(4):
            for kc in range(KC):
                _mm(P, pv[:, bk * 64:(bk + 1) * 64], hb[:, kc, bk * 128:(bk + 1) * 128], W[:, kc, 129:193],
                    kc == 0, kc == KC - 1, [WB, hbB], [M.psB[4 + tt % 2]], kc == KC - 1)
        P.op("dve", lambda e: e.tensor_copy(out=Vx[:, tt * 4:(tt + 1) * 4, 0:64],
                                            in_=pv[:, 0:256].rearrange("p (b d) -> p b d", b=4)),
             reads=[M.psB[4 + tt % 2]], writes=[VB[tt]])
    NPB = 3
    pT = [sb("pT%d" % i, [128, 2 * TT], BF16) for i in range(NPB)]
    pTB = P.bufs(NPB)
    den = sb("den", [128, TT], F32)
    bc = sb("bc", [64, TT], F32)
    ao = [sb("ao%d" % i, [64, TT], BF16) for i in range(2)]
    denB, bcB, aoB = P.buf(), P.buf(), P.bufs(2)
    pairs = [(I, J) for I in range(NQT) for J in range(0, 4 * I + 4, 2)]
    LOOK = 2
    slot = [0]
    slots = {}

    def issue_S(n):
        I, J0 = pairs[n]
        b = slot[0] % NPB
        slot[0] += 1
        slots[n] = b
        for k in range(2):
            J = J0 + k
            pS, pSB = M.ps[2 * b + k], M.psB[2 * b + k]
            diag = J >= 4 * I
            _mm(P, pS[:], Kx[:, J * 128:(J + 1) * 128], Qx[:, I * TT:(I + 1) * TT], True, not diag,
                [KB[J // 4], QB[I]], [pSB], not diag)
            if diag:
                _mm(P, pS[:], ident[:], cmask[:, J - 4 * I, :], False, True, [cB], [pSB], True)

    for n in range(min(LOOK, len(pairs))):
        issue_S(n)
    for n, (I, J0) in enumerate(pairs):
        b = slots.pop(n)
        qs = slice(I * TT, (I + 1) * TT)
        po, poB = M.ps[6 + I % 2], M.psB[6 + I % 2]
        nJ = 4 * I + 4
        P.op("act", lambda e: e.activation(out=pT[b][:], in_=M.psall[:, 2 * b * TT:(2 * b + 2) * TT], func=AF.Exp),
             reads=[M.psB[2 * b], M.psB[2 * b + 1]], writes=[pTB[b]])
        if n + LOOK < len(pairs):
            issue_S(n + LOOK)
        for k in range(2):
            J = J0 + k
            _mm(P, po[0:65, :], Vx[:, J, :], pT[b][:, k * TT:(k + 1) * TT], J == 0, J == nJ - 1, [VB[J // 4], pTB[b]], [poB],
                k == 1)
        if J0 + 2 == nJ:
            P.op("dve", lambda e: e.reciprocal(out=den[r64, :], in_=po[r64, :]), reads=[poB], writes=[denB])
            bb_ = slot[0] % NPB
            slot[0] += 1
            pbc, pbcB = M.ps[2 * bb_], M.psB[2 * bb_]
            _mm(P, pbc[0:64, :], onesf[r64, 0:64], den[r64, :], True, True, [selB, denB], [pbcB], True)
            P.op("dve", lambda e: e.tensor_copy(out=bc[:], in_=pbc[0:64, :]), reads=[pbcB], writes=[bcB])
            P.op("dve", lambda e: e.tensor_tensor(out=ao[I % 2][:], in0=po[0:64, :], in1=bc[:], op=ALU.mult),
                 reads=[poB, bcB], writes=[aoB[I % 2]])
            P.dma("sp", attT_out_d[:, qs], ao[I % 2][:], reads=[aoB[I % 2]], is_output=True)


def emit_lru(M, es, w_lru_d, gab_d, lvec_d, lruT_out_d):
    P, nc = M.P, M.nc
    sb = lambda name, shape, dt: es.enter_context(nc.sbuf_tensor("sb_" + name, shape, dt))
    W, WB = M.load_w("w_lru", w_lru_d, 256)
    gab = sb("gab", [128, 256], BF16)
    gabB = P.buf()
    P.dma("pool", gab[:], gab_d, writes=[gabB])
    lv = sb("lvec", [128, 8], F32)
    lvB = P.buf()
    P.dma("sp", lv[:], lvec_d, writes=[lvB])
    cv = sb("lconst", [128, 4], F32)
    cvB = P.buf()
    P.op("act", lambda e: e.activation(out=cv[:, 3:4], in_=lv[:, 7:8], func=AF.Exp, scale=-1.0), reads=[lvB], writes=[cvB])
    P.op("act", lambda e: e.activation(out=cv[:, 3:4], in_=cv[:, 3:4], func=AF.Ln, bias=1.0), reads=[cvB], writes=[cvB])
    P.op("dve", lambda e: e.tensor_scalar(out=cv[:, 0:1], in0=cv[:, 3:4], scalar1=-4.0, scalar2=None, op0=ALU.mult),
         reads=[cvB], writes=[cvB])
    P.op("dve", lambda e: e.tensor_scalar(out=cv[:, 1:3], in0=lv[:, 5:7], scalar1=0.5, scalar2=None, op0=ALU.mult),
         reads=[lvB, cvB], writes=[cvB])
    xbuf = sb("xbuf", [128, 3 + TT], F32)
    xbufB = P.buf()
    P.op("dve", lambda e: e.memset(xbuf[:, 0:3], 0.0), writes=[xbufB])
    names = ["xc", "tr", "a", "ti", "om", "u", "hc", "y2", "gl"]
    T = {n: sb("l_" + n, [128, TT], F32) for n in names}
    B = {n: P.buf(n) for n in names}
    xc16 = sb("xc16", [128, TT], BF16)
    xc16B = P.buf()
    ob = [sb("lo%d" % i, [128, TT], BF16) for i in range(2)]
    obB = P.bufs(2)
    hcar = sb("hcar", [128, 1], F32)
    hcarB = P.buf()
    P.op("dve", lambda e: e.memset(hcar[:], 0.0), writes=[hcarB])
    for tt in range(NQT):
        sl = slice(tt * TT, (tt + 1) * TT)
        hb, hbB = M.load_h_tile(tt)
        px, py = M.ps[0 + tt % 2], M.ps[2 + tt % 2]
        pxB, pyB = M.psB[0 + tt % 2], M.psB[2 + tt % 2]
        for kc in range(KC):
            _mm(P, px[:], W[:, kc, 0:128], hb[:, kc, :], kc == 0, kc == KC - 1, [WB, hbB], [pxB], kc == KC - 1)
        for kc in range(KC):
            _mm(P, py[:], W[:, kc, 128:256], hb[:, kc, :], kc == 0, kc == KC - 1, [WB, hbB], [pyB], kc == KC - 1)
        P.op("act", lambda e: e.activation(out=xbuf[:, 3:3 + TT], in_=px[:], func=AF.Copy), reads=[pxB], writes=[xbufB])
        P.op("dve", lambda e: e.tensor_scalar(out=T["xc"][:], in0=xbuf[:, 3:3 + TT], scalar1=lv[:, 3:4], scalar2=lv[:, 4:5],
                                              op0=ALU.mult, op1=ALU.add), reads=[xbufB, lvB], writes=[B["xc"]])
        for k in (2, 1, 0):
            P.op("dve", lambda e: e.scalar_tensor_tensor(out=T["xc"][:], in0=xbuf[:, k:k + TT], scalar=lv[:, k:k + 1],
                                                         in1=T["xc"][:], op0=ALU.mult, op1=ALU.add),
                 reads=[xbufB, lvB, B["xc"]], writes=[B["xc"]])
        P.op("dve", lambda e: e.tensor_copy(out=xbuf[:, 0:3], in_=xbuf[:, TT:TT + 3]), reads=[xbufB], writes=[xbufB])
        P.op("pool", lambda e: e.tensor_copy(out=xc16[:], in_=T["xc"][:]), reads=[B["xc"]], writes=[xc16B])
        pr, pi = M.ps[4 + tt % 2], M.ps[6 + tt % 2]
        prB, piB = M.psB[4 + tt % 2], M.psB[6 + tt % 2]
        _mm(P, pr[:], gab[:, 0:128], xc16[:], True, True, [gabB, xc16B], [prB], True)
        _mm(P, pi[:], gab[:, 128:256], xc16[:], True, True, [gabB, xc16B], [piB], True)
        P.op("act", lambda e: e.activation(out=T["tr"][:], in_=pr[:], func=AF.Tanh, scale=0.5, bias=cv[:, 1:2]),
             reads=[prB, cvB], writes=[B["tr"]])
        P.op("act", lambda e: e.activation(out=T["a"][:], in_=T["tr"][:], func=AF.Exp, scale=cv[:, 0:1], bias=cv[:, 0:1]),
             reads=[B["tr"], cvB], writes=[B["a"]])
        P.op("act", lambda e: e.activation(out=T["ti"][:], in_=pi[:], func=AF.Tanh, scale=0.5, bias=cv[:, 2:3]),
             reads=[piB, cvB], writes=[B["ti"]])
        P.op("act", lambda e: e.activation(out=T["y2"][:], in_=py[:], func=AF.Square), reads=[pyB], writes=[B["y2"]])
        P.op("dve", lambda e: e.tensor_scalar(out=T["y2"][:], in0=T["y2"][:], scalar1=0.044715, scalar2=1.0,
                                              op0=ALU.mult, op1=ALU.add), reads=[B["y2"]], writes=[B["y2"]])
        P.op("dve", lambda e: e.tensor_tensor(out=T["y2"][:], in0=py[:], in1=T["y2"][:], op=ALU.mult),
             reads=[pyB, B["y2"]], writes=[B["y2"]])
        P.op("act", lambda e: e.activation(out=T["gl"][:], in_=T["y2"][:], func=AF.Tanh, scale=0.7978845608028654),
             reads=[B["y2"]], writes=[B["gl"]])
        P.op("dve", lambda e: e.scalar_tensor_tensor(out=T["gl"][:], in0=T["gl"][:], scalar=1.0, in1=py[:],
                                                     op0=ALU.add, op1=ALU.mult), reads=[B["gl"], pyB], writes=[B["gl"]])
        P.op("dve", lambda e: e.tensor_tensor(out=T["om"][:], in0=T["a"][:], in1=T["a"][:], op=ALU.mult),
             reads=[B["a"]], writes=[B["om"]])
        P.op("dve", lambda e: e.tensor_scalar(out=T["om"][:], in0=T["om"][:], scalar1=-1.0, scalar2=1.0,
                                              op0=ALU.mult, op1=ALU.add), reads=[B["om"]], writes=[B["om"]])
        P.op("act", lambda e: e.activation(out=T["om"][:], in_=T["om"][:], func=AF.Sqrt), reads=[B["om"]], writes=[B["om"]])
        P.op("dve", lambda e: e.scalar_tensor_tensor(out=T["u"][:], in0=T["ti"][:], scalar=1.0, in1=T["xc"][:],
                                                     op0=ALU.add, op1=ALU.mult), reads=[B["ti"], B["xc"]], writes=[B["u"]])
        P.op("dve", lambda e: e.scalar_tensor_tensor(out=T["u"][:], in0=T["u"][:], scalar=0.5, in1=T["om"][:],
                                                     op0=ALU.mult, op1=ALU.mult), reads=[B["u"], B["om"]], writes=[B["u"]])
        P.op("dve", lambda e: e.tensor_tensor_scan(out=T["hc"][:], data0=T["a"][:], data1=T["u"][:], initial=hcar[:],
                                                   op0=ALU.mult, op1=ALU.add), reads=[B["a"], B["u"], hcarB], writes=[B["hc"]])
        P.op("dve", lambda e: e.tensor_copy(out=hcar[:], in_=T["hc"][:, TT - 1:TT]), reads=[B["hc"]], writes=[hcarB])
        o = ob[tt % 2]
        P.op("dve", lambda e: e.scalar_tensor_tensor(out=o[:], in0=T["hc"][:], scalar=0.5, in1=T["gl"][:],
                                                     op0=ALU.mult, op1=ALU.mult), reads=[B["hc"], B["gl"]], writes=[obB[tt % 2]])
        P.dma("sp", lruT_out_d[:, sl], o[:], reads=[obB[tt % 2]], is_output=True)


def emit_rwkv(M, es, w_rw_d, rvec_d, rmat_d, rgn_d, rconst_d, rwT_out_d):
    P, nc = M.P, M.nc
    sb = lambda name, shape, dt: es.enter_context(nc.sbuf_tensor("sb_" + name, shape, dt))
    W, WB = M.load_w("w_rw", w_rw_d, 448)
    rv = sb("rvec", [128, 16], F32)
    rm16 = sb("rmat16", [128, 192], BF16)
    rgn = sb("rgn", [128, 512], F32)
    rc = sb("rconst", [128, 10 * 512], F32)
    cB = P.buf()
    P.dma("sp", rv[:], rvec_d, writes=[cB])
    P.dma("pool", rm16[:], rmat_d, writes=[cB])
    P.dma("sp", rgn[:], rgn_d, writes=[cB])
    P.dma("sp", rc[:], rconst_d, writes=[cB])
    ident = rc[:, 0:128]
    m_su = rc[:, 512:1024].rearrange("p (c t) -> p c t", c=4)
    m_iu = rc[:, 1024:1536].rearrange("p (c t) -> p c t", c=4)
    m_sl = rc[:, 1536:2048].rearrange("p (c t) -> p c t", c=4)
    cmask = rc[:, 2048:2560]
    m_d16 = rc[:, 2560:3072]
    m_o16 = rc[:, 3072:3584]
    m_o32 = rc[:, 3584:4096]
    m_o64 = rc[:, 4096:4608]
    identx4 = rc[:, 4608:5120]
    onesc = rc[:, 128:129]
    ones64 = sb("ones64", [128, 64], F32)
    P.op("pool", lambda e: e.memset(ones64[:], 1.0), writes=[cB])
    dv = sb("rdv", [128, 16], F32)
    P.op("dve", lambda e: e.tensor_scalar(out=dv[:, 0:10], in0=rv[:, 0:10], scalar1=-1.0, scalar2=1.0, op0=ALU.mult, op1=ALU.add),
         reads=[cB], writes=[cB])
    P.op("dve", lambda e: e.tensor_scalar(out=dv[:, 10:12], in0=rv[:, 3:5], scalar1=0.5, scalar2=None, op0=ALU.mult),
         reads=[cB], writes=[cB])
    P.op("dve", lambda e: e.tensor_scalar(out=dv[:, 12:13], in0=rv[:, 6:7], scalar1=0.5, scalar2=None, op0=ALU.mult),
         reads=[cB], writes=[cB])
    P.op("dve", lambda e: e.tensor_scalar(out=dv[:, 13:14], in0=rv[:, 6:7], scalar1=-0.5, scalar2=1.0, op0=ALU.mult, op1=ALU.add),
         reads=[cB], writes=[cB])
    def t64(name, dt=F32, n=TT):
        return sb("r_" + name, [64, n], dt), P.buf(name)
    def t128(name, dt=F32, n=TT):
        return sb("r_" + name, [128, n], dt), P.buf(name)
    zr, zrB = t64("zr", n=TT + 1); zk, zkB = t64("zk", n=TT + 1); zv, zvB = t64("zv", n=TT + 1)
    zwa, zwaB = t128("zwa", n=TT + 1); zg, zgB = t128("zg", n=TT + 1)
    for z, zB in ((zr, zrB), (zk, zkB), (zv, zvB), (zwa, zwaB), (zg, zgB)):
        P.op("dve", lambda e: e.memset(z[:, 0:1], 0.0), writes=[zB])
    rm, rmB = t64("rm"); km, kmB = t64("km"); vm, vmB = t64("vm")
    wam, wamB = t128("wam"); gm, gmB = t128("gm")
    tmp, tmpB = t128("tmp")
    twl, twlB = t128("twl", BF16); sg16, sg16B = t128("sg16", BF16)
    lw, lwB = t64("lw"); ta, taB = t64("ta"); kk, kkB = t64("kk"); kf, kfB = t64("kf"); bb, bbB = t64("bb")
    cw, cwB = t64("cw"); cwx, cwxB = t64("cwx"); dd, ddB = t64("dd")
    E1, E1B = t64("E1"); E2, E2B = t64("E2"); E3, E3B = t64("E3"); E4, E4B = t64("E4")
    AR, ARB = t64("AR", n=4 * 256); bT, bTB = t64("bT"); kT, kTB = t64("kT"); BhT, BhTB = t64("BhT"); KhT, KhTB = t64("KhT")
    prod, prodB = t64("prod")
    ARv = AR[:].rearrange("p (c t) -> p c t", c=4)
    Vt, VtB = t128("Vt", n=256); Bh, BhB = t128("Bh", n=256); Kh, KhB = t128("Kh", n=256)
    X = [t128("X%d" % i) for i in range(2)]
    XT = [t128("XT%d" % i) for i in range(2)]
    Yrb, YrbB = t128("Yrb"); Xak, XakB = t128("Xak"); Yrk, YrkB = t128("Yrk")
    Z = [t128("Z%d" % i) for i in range(2)]
    iv = {n: t128("iv_" + n) for n in ("S0", "S0T", "S1", "S1T", "S2", "S2T", "S3", "S3T", "J", "JT", "Fa", "FaT", "Fb", "FbT",
                                       "U", "L", "W1", "V1", "Ta", "Tb", "Na", "Nb")}
    RpT, RpTB = t64("RpT"); Yl, YlB = t128("Yl", n=256); MT, MTB = t64("MT", n=256); Psi, PsiB = t64("Psi", n=256)
    Tst = [t64("T%d" % i, n=64) for i in range(2)]
    P.op("dve", lambda e: e.memset(Tst[0][0][:], 0.0), writes=[Tst[0][1]])
    Yo, YoB = t128("Yo", n=256); gtm, gtmB = t128("gtm", n=256); sbon, sbonB = t128("sbon", n=4)
    st6, st6B = t128("st6", n=24); mv, mvB = t128("mv", n=8); rs, rsB = t128("rs", n=4)
    oT = [t64("oT%d" % i, BF16) for i in range(2)]
    v3 = lambda ap, c: ap.rearrange("p (c t) -> p c t", c=c)
    ps, psB = M.ps, M.psB
    big = M.psall
    tcur = 0
    for tt in range(NQT):
        sl = slice(tt * TT, (tt + 1) * TT)
        hb, hbB = M.load_h_tile(tt)
        for gi, (c0, c1, z, zB) in enumerate(((0, 64, zr, zrB), (64, 128, zk, zkB), (128, 192, zv, zvB),
                                               (192, 320, zwa, zwaB), (320, 448, zg, zgB))):
            n = c1 - c0
            for kc in range(KC):
                _mm(P, ps[gi][0:n, :], W[:, kc, c0:c1], hb[:, kc, :], kc == 0, kc == KC - 1, [WB, hbB], [psB[gi]], kc == KC - 1)
            P.op("act", lambda e: e.activation(out=z[:, 1:TT + 1], in_=ps[gi][0:n, :], func=AF.Copy), reads=[psB[gi]], writes=[zB])
        for (z, zB, o, oB, col, n) in ((zr, zrB, rm, rmB, 0, 64), (zk, zkB, km, kmB, 1, 64), (zv, zvB, vm, vmB, 2, 64),
                                       (zwa, zwaB, wam, wamB, 8, 128), (zg, zgB, gm, gmB, 9, 128)):
            P.op("dve", lambda e: e.tensor_scalar(out=tmp[0:n, :], in0=z[:, 1:TT + 1], scalar1=dv[0:n, col:col + 1], scalar2=None,
                                                  op0=ALU.mult), reads=[zB, cB], writes=[tmpB])
            P.op("dve", lambda e: e.scalar_tensor_tensor(out=o[:], in0=z[:, 0:TT], scalar=rv[0:n, col:col + 1], in1=tmp[0:n, :],
                                                         op0=ALU.mult, op1=ALU.add), reads=[zB, cB, tmpB], writes=[oB])
            P.op("dve", lambda e: e.tensor_copy(out=z[:, 0:1], in_=z[:, TT:TT + 1]), reads=[zB], writes=[zB])
        P.op("act", lambda e: e.activation(out=twl[0:64, :], in_=wam[0:64, :], func=AF.Tanh), reads=[wamB], writes=[twlB])
        P.op("pool", lambda e: e.tensor_copy(out=twl[64:128, :], in_=wam[64:128, :]), reads=[wamB], writes=[twlB])
        _mm(P, ps[0][0:64, :], rm16[0:64, 0:64], twl[0:64, :], True, True, [cB, twlB], [psB[0]], True)
        _mm(P, ps[1][0:64, :], rm16[64:128, 64:128], twl[64:128, :], True, True, [cB, twlB], [psB[1]], True)
        P.op("act", lambda e: e.activation(out=lw[:], in_=ps[0][0:64, :], func=AF.Tanh, scale=0.5, bias=dv[0:64, 10:11]),
             reads=[psB[0], cB], writes=[lwB])
        P.op("dve", lambda e: e.tensor_scalar(out=lw[:], in0=lw[:], scalar1=-0.3032653298563167, scalar2=-0.3032653298563167,
                                              op0=ALU.mult, op1=ALU.add), reads=[lwB], writes=[lwB])
        P.op("act", lambda e: e.activation(out=ta[:], in_=ps[1][0:64, :], func=AF.Tanh, scale=0.5, bias=dv[0:64, 11:12]),
             reads=[psB[1], cB], writes=[taB])
        P.op("act", lambda e: e.activation(out=tmp[:], in_=gm[:], func=AF.Tanh, scale=0.5), reads=[gmB], writes=[tmpB])
        P.op("dve", lambda e: e.tensor_scalar(out=sg16[:], in0=tmp[:], scalar1=0.5, scalar2=0.5, op0=ALU.mult, op1=ALU.add),
             reads=[tmpB], writes=[sg16B])
        for c in range(4):
            _mm(P, ps[2][:, c * 64:(c + 1) * 64], sg16[:, c * 128:(c + 1) * 128], rm16[:, 128:192], True, True,
                [sg16B, cB], [psB[2]], c == 3)
        P.op("act", lambda e: e.activation(out=gtm[:], in_=ps[2][:, 0:256], func=AF.Copy), reads=[psB[2]], writes=[gtmB])
        P.op("dve", lambda e: e.tensor_scalar(out=kk[:], in0=km[:], scalar1=rv[0:64, 5:6], scalar2=None, op0=ALU.mult),
             reads=[kmB, cB], writes=[kkB])
        P.op("dve", lambda e: e.tensor_tensor(out=tmp[0:64, :], in0=kk[:], in1=kk[:], op=ALU.mult), reads=[kkB], writes=[tmpB])
        _mm(P, ps[3][0:64, :], ones64[0:64, :], tmp[0:64, :], True, True, [cB, tmpB], [psB[3]], True)
        P.op("act", lambda e: e.activation(out=tmp[0:64, :], in_=ps[3][0:64, :], func=AF.Sqrt), reads=[psB[3]], writes=[tmpB])
        P.op("dve", lambda e: e.tensor_scalar(out=tmp[0:64, :], in0=tmp[0:64, :], scalar1=1e-12, scalar2=None, op0=ALU.max),
             reads=[tmpB], writes=[tmpB])
        P.op("dve", lambda e: e.reciprocal(out=tmp[0:64, :], in_=tmp[0:64, :]), reads=[tmpB], writes=[tmpB])
        P.op("dve", lambda e: e.tensor_tensor(out=kk[:], in0=kk[:], in1=tmp[0:64, :], op=ALU.mult), reads=[kkB, tmpB], writes=[kkB])
        P.op("dve", lambda e: e.tensor_scalar(out=kf[:], in0=ta[:], scalar1=dv[0:64, 12:13], scalar2=dv[0:64, 13:14],
                                              op0=ALU.mult, op1=ALU.add), reads=[taB, cB], writes=[kfB])
        P.op("dve", lambda e: e.tensor_tensor(out=kf[:], in0=kf[:], in1=km[:], op=ALU.mult), reads=[kfB, kmB], writes=[kfB])
        P.op("dve", lambda e: e.tensor_scalar(out=bb[:], in0=ta[:], scalar1=0.5, scalar2=0.5, op0=ALU.mult, op1=ALU.add),
             reads=[taB], writes=[bbB])
        P.op("dve", lambda e: e.tensor_tensor(out=bb[:], in0=bb[:], in1=kk[:], op=ALU.mult), reads=[bbB, kkB], writes=[bbB])
        P.op("dve", lambda e: e.tensor_tensor_scan(out=cw[:], data0=cmask[0:64, :], data1=lw[:], initial=0.0,
                                                   op0=ALU.mult, op1=ALU.add), reads=[lwB, cB], writes=[cwB])
        P.op("dve", lambda e: e.tensor_tensor(out=cwx[:], in0=cw[:], in1=lw[:], op=ALU.subtract), reads=[cwB, lwB], writes=[cwxB])
        P.op("dve", lambda e: e.tensor_tensor(out=v3(dd[:], 4), in0=v3(cw[:], 4)[:, :, 127:128].to_broadcast([64, 4, 128]),
                                              in1=v3(cw[:], 4), op=ALU.subtract), reads=[cwB], writes=[ddB])
        P.op("act", lambda e: e.activation(out=E1[:], in_=cw[:], func=AF.Exp), reads=[cwB], writes=[E1B])
        P.op("act", lambda e: e.activation(out=E2[:], in_=cw[:], func=AF.Exp, scale=-1.0), reads=[cwB], writes=[E2B])
        P.op("act", lambda e: e.activation(out=E3[:], in_=cwx[:], func=AF.Exp), reads=[cwxB], writes=[E3B])
        P.op("act", lambda e: e.activation(out=E4[:], in_=dd[:], func=AF.Exp), reads=[ddB], writes=[E4B])
        P.op("dve", lambda e: e.scalar_tensor_tensor(out=ARv[:, :, 0:128], in0=v3(kk[:], 4), scalar=-1.0, in1=v3(E3[:], 4),
                                                     op0=ALU.mult, op1=ALU.mult), reads=[kkB, E3B], writes=[ARB])
        P.op("dve", lambda e: e.tensor_tensor(out=ARv[:, :, 128:256], in0=v3(rm[:], 4), in1=v3(E1[:], 4), op=ALU.mult),
             reads=[rmB, E1B], writes=[ARB])
        P.op("dve", lambda e: e.tensor_tensor(out=bT[:], in0=bb[:], in1=E2[:], op=ALU.mult), reads=[bbB, E2B], writes=[bTB])
        P.op("dve", lambda e: e.tensor_tensor(out=kT[:], in0=kf[:], in1=E2[:], op=ALU.mult), reads=[kfB, E2B], writes=[kTB])
        P.op("dve", lambda e: e.tensor_tensor(out=BhT[:], in0=bb[:], in1=E4[:], op=ALU.mult), reads=[bbB, E4B], writes=[BhTB])
        P.op("dve", lambda e: e.tensor_tensor(out=KhT[:], in0=kf[:], in1=E4[:], op=ALU.mult), reads=[kfB, E4B], writes=[KhTB])
        P.op("dve", lambda e: e.scalar_tensor_tensor(out=prod[:], in0=rm[:], scalar=rv[0:64, 7:8], in1=kf[:],
                                                     op0=ALU.mult, op1=ALU.mult), reads=[rmB, cB, kfB], writes=[prodB])
        for c in range(4):
            _mm(P, ps[4][:, c:c + 1], prod[:, c * 128:(c + 1) * 128], ones64[0:64, 0:1], True, True, [prodB, cB], [psB[4]], c == 3)
        P.op("act", lambda e: e.activation(out=sbon[:], in_=ps[4][:, 0:4], func=AF.Copy), reads=[psB[4]], writes=[sbonB])
        for (src, srcB, dst, dstB, pi) in ((vm, vmB, Vt, VtB, 5), (BhT, BhTB, Bh, BhB, 6), (KhT, KhTB, Kh, KhB, 7)):
            for c in range(4):
                _mm(P, ps[pi][:, c * 64:(c + 1) * 64], src[:, c * 128:(c + 1) * 128], ident[0:64, 0:64], True, True,
                    [srcB, cB], [psB[pi]], c == 3)
            P.op("act" if pi != 6 else "dve", (lambda e: e.activation(out=dst[:], in_=ps[pi][:, 0:256], func=AF.Copy)) if pi != 6 else
                 (lambda e: e.tensor_copy(out=dst[:], in_=ps[pi][:, 0:256])), reads=[psB[pi]], writes=[dstB])
        for c in range(4):
            _mm(P, big[:, c * 256:(c + 1) * 256], bT[:, c * 128:(c + 1) * 128], ARv[:, c, :], True, True,
                [bTB, ARB], [psB[0], psB[1]], c == 3)
        for c in range(4):
            _mm(P, big[:, 1024 + c * 256:1024 + (c + 1) * 256], kT[:, c * 128:(c + 1) * 128], ARv[:, c, :], True, True,
                [kTB, ARB], [psB[2], psB[3]], c == 3)
        for c in range(4):
            _mm(P, ps[4][:, c * 128:(c + 1) * 128], ARv[:, c, 0:128], bT[:, c * 128:(c + 1) * 128], True, True,
                [ARB, bTB], [psB[4]], c == 3)
        a1 = big[:, 0:1024].rearrange("p (c t) -> p c t", c=4)
        a2 = big[:, 1024:2048].rearrange("p (c t) -> p c t", c=4)
        X0, X0B = X[0]
        XT0, XT0B = XT[0]
        P.op("dve", lambda e: e.tensor_tensor(out=v3(X0[:], 4), in0=a1[:, :, 0:128], in1=m_su, op=ALU.mult),
             reads=[psB[0], psB[1], cB], writes=[X0B])
        P.op("dve", lambda e: e.tensor_tensor(out=v3(Yrb[:], 4), in0=a1[:, :, 128:256], in1=m_iu, op=ALU.mult),
             reads=[psB[0], psB[1], cB], writes=[YrbB])
        P.op("dve", lambda e: e.tensor_tensor(out=v3(Xak[:], 4), in0=a2[:, :, 0:128], in1=m_su, op=ALU.mult),
             reads=[psB[2], psB[3], cB], writes=[XakB])
        P.op("dve", lambda e: e.tensor_tensor(out=v3(Yrk[:], 4), in0=a2[:, :, 128:256], in1=m_iu, op=ALU.mult),
             reads=[psB[2], psB[3], cB], writes=[YrkB])
        P.op("dve", lambda e: e.tensor_tensor(out=v3(XT0[:], 4), in0=v3(ps[4], 4), in1=m_sl, op=ALU.mult),
             reads=[psB[4], cB], writes=[XT0B])
        for c in range(4):
            _mm(P, ps[5][:, c * 128:c * 128 + 64], Xak[:, c * 128:(c + 1) * 128], Vt[:, c * 64:(c + 1) * 64], True, True,
                [XakB, VtB], [psB[5]], False)
            _mm(P, ps[5][:, c * 128 + 64:(c + 1) * 128], ARv[:, c, 0:128], ident[0:64, 0:64], True, True,
                [ARB, cB], [psB[5]], c == 3)
        Zc, ZcB = Z[0]
        P.op("act", lambda e: e.activation(out=Zc[:], in_=ps[5], func=AF.Copy), reads=[psB[5]], writes=[ZcB])
        X0, X0B = X[0]
        XT0, XT0B = XT[0]
        bank = [0]

        def mm4(lhs, lhsB, rhs, rhsB):
            bi = bank[0] % 8
            bank[0] += 1
            for c in range(4):
                _mm(P, ps[bi][:, c * 128:(c + 1) * 128], lhs[:, c * 128:(c + 1) * 128], rhs[:, c * 128:(c + 1) * 128],
                    True, True, [lhsB, rhsB], [psB[bi]], c == 3)
            return ps[bi], psB[bi]

        def evac(eng, dst, dstB, src, srcB):
            if eng == "act":
                P.op("act", lambda e: e.activation(out=dst[:], in_=src, func=AF.Copy), reads=[srcB], writes=[dstB])
            else:
                P.op(eng, lambda e: e.tensor_copy(out=dst[:], in_=src), reads=[srcB], writes=[dstB])

        def addto(dst, dstB, psrc, psrcB, other, otherB):
            P.op("dve", lambda e: e.tensor_tensor(out=dst[:], in0=psrc, in1=other[:], op=ALU.add),
                 reads=[psrcB, otherB], writes=[dstB])

        def masked(dst, dstB, src, srcB, mk):
            P.op("pool", lambda e: e.tensor_tensor(out=dst[:], in0=src[:], in1=mk, op=ALU.mult), reads=[srcB, cB], writes=[dstB])

        S, SB = iv["S0"]
        ST, STB = iv["S0T"]
        masked(S, SB, X0, X0B, m_d16)
        masked(ST, STB, XT0, XT0B, m_d16)
        J, JB = iv["J"]
        JT, JTB = iv["JT"]
        P.op("dve", lambda e: e.tensor_tensor(out=J[:], in0=S[:], in1=identx4, op=ALU.add), reads=[SB, cB], writes=[JB])
        P.op("dve", lambda e: e.tensor_tensor(out=JT[:], in0=ST[:], in1=identx4, op=ALU.add), reads=[STB, cB], writes=[JTB])
        F = FT = None
        for lev in range(3):
            Sn, SnB = iv["S%d" % (lev + 1)]
            STn, STnB = iv["S%dT" % (lev + 1)]
            p1, p1B = mm4(ST, STB, S, SB)
            p2, p2B = mm4(S, SB, ST, STB)
            evac("act", Sn, SnB, p1, p1B)
            evac("dve", STn, STnB, p2, p2B)
            if lev == 0:
                P.op("dve", lambda e: e.tensor_tensor(out=J[:], in0=J[:], in1=Sn[:], op=ALU.add), reads=[JB, SnB], writes=[JB])
                P.op("dve", lambda e: e.tensor_tensor(out=JT[:], in0=JT[:], in1=STn[:], op=ALU.add), reads=[JTB, STnB], writes=[JTB])
                q1, q1B = mm4(ST, STB, Sn, SnB)
                q2, q2B = mm4(S, SB, STn, STnB)
                F, FB = iv["Fa"]
                FT, FTB = iv["FaT"]
                addto(F, FB, q1, q1B, J, JB)
                addto(FT, FTB, q2, q2B, JT, JTB)
            else:
                q1, q1B = mm4(FT, FTB, Sn, SnB)
                q2, q2B = mm4(F, FB, STn, STnB)
                Fn, FnB = iv["Fb" if lev == 1 else "Fa"]
                FTn, FTnB = iv["FbT" if lev == 1 else "FaT"]
                addto(Fn, FnB, q1, q1B, F, FB)
                addto(FTn, FTnB, q2, q2B, FT, FTB)
                F, FB, FT, FTB = Fn, FnB, FTn, FTnB
            S, SB, ST, STB = Sn, SnB, STn, STnB
        Tk, TkB, Nk, NkB = F, FB, FT, FTB
        for li, mk in enumerate((m_o16, m_o32, m_o64)):
            U, UB = iv["U"]
            Lm, LB = iv["L"]
            masked(Lm, LB, XT0, XT0B, mk)
            last = li == 2
            if not last:
                masked(U, UB, X0, X0B, mk)
            w1, w1B = mm4(Lm, LB, Tk, TkB)
            W1, W1B = iv["W1"]
            evac("act", W1, W1B, w1, w1B)
            if not last:
                v1, v1B = mm4(U, UB, Nk, NkB)
                V1, V1B = iv["V1"]
                evac("dve", V1, V1B, v1, v1B)
            t2, t2B = mm4(Nk, NkB, W1, W1B)
            Tn, TnB = iv["Ta" if li % 2 == 0 else "Tb"]
            addto(Tn, TnB, t2, t2B, Tk, TkB)
            if not last:
                n2, n2B = mm4(Tk, TkB, V1, V1B)
                Nn, NnB = iv["Na" if li % 2 == 0 else "Nb"]
                addto(Nn, NnB, n2, n2B, Nk, NkB)
                Nk, NkB = Nn, NnB
            Tk, TkB = Tn, TnB
        zp, zpB = mm4(Tk, TkB, Z[0][0], Z[0][1])
        Zf, ZfB = Z[1]
        evac("act", Zf, ZfB, zp, zpB)
        Zv = v3(Zf[:], 4)
        for c in range(4):
            _mm(P, ps[0][0:64, c * 128:(c + 1) * 128], Zv[:, c, 64:128], Yrb[:, c * 128:(c + 1) * 128], True, True,
                [ZfB, YrbB], [psB[0]], c == 3)
        P.op("dve", lambda e: e.tensor_tensor(out=v3(RpT[:], 4), in0=v3(ps[0][0:64, :], 4), in1=ARv[:, :, 128:256], op=ALU.add),
             reads=[psB[0], ARB], writes=[RpTB])
        for c in range(4):
            _mm(P, ps[1][:, c * 64:(c + 1) * 64], Yrb[:, c * 128:(c + 1) * 128], Zv[:, c, 0:64], True, False,
                [YrbB, ZfB], [psB[1]], False)
            _mm(P, ps[1][:, c * 64:(c + 1) * 64], Yrk[:, c * 128:(c + 1) * 128], Vt[:, c * 64:(c + 1) * 64], False, True,
                [YrkB, VtB], [psB[1]], c == 3)
        P.op("act", lambda e: e.activation(out=Yl[:], in_=ps[1][:, 0:256], func=AF.Copy), reads=[psB[1]], writes=[YlB])
        for c in range(4):
            _mm(P, ps[2][0:64, c * 64:(c + 1) * 64], Zv[:, c, 64:128], Bh[:, c * 64:(c + 1) * 64], True, True,
                [ZfB, BhB], [psB[2]], c == 3)
        P.op("act", lambda e: e.activation(out=MT[:], in_=ps[2][0:64, 0:256], func=AF.Copy), reads=[psB[2]], writes=[MTB])
        for c in range(4):
            _mm(P, ps[3][0:64, c * 64:(c + 1) * 64], Kh[:, c * 64:(c + 1) * 64], Vt[:, c * 64:(c + 1) * 64], True, False,
                [KhB, VtB], [psB[3]], False)
            _mm(P, ps[3][0:64, c * 64:(c + 1) * 64], Bh[:, c * 64:(c + 1) * 64], Zv[:, c, 0:64], False, True,
                [BhB, ZfB], [psB[3]], c == 3)
        P.op("dve", lambda e: e.tensor_copy(out=Psi[:], in_=ps[3][0:64, 0:256]), reads=[psB[3]], writes=[PsiB])
        for c in range(4):
            Tc, TcB = Tst[tcur]
            Tn, TnB = Tst[1 - tcur]
            _mm(P, ps[4][:, c * 64:(c + 1) * 64], RpT[:, c * 128:(c + 1) * 128], Tc[:], True, True, [RpTB, TcB], [psB[4]], True)
            _mm(P, ps[5][0:64, c * 64:(c + 1) * 64], MT[:, c * 64:(c + 1) * 64], Tc[:], True, True, [MTB, TcB], [psB[5]], True)
            wc = E1[:, c * 128 + 127:c * 128 + 128]
            P.op("dve", lambda e: e.scalar_tensor_tensor(out=Tn[:], in0=Tc[:], scalar=wc, in1=Psi[:, c * 64:(c + 1) * 64],
                                                         op0=ALU.mult, op1=ALU.add), reads=[TcB, E1B, PsiB], writes=[TnB])
            P.op("dve", lambda e: e.tensor_tensor(out=Tn[:], in0=ps[5][0:64, c * 64:(c + 1) * 64], in1=Tn[:], op=ALU.add),
                 reads=[psB[5], TnB], writes=[TnB])
            tcur = 1 - tcur
        P.op("dve", lambda e: e.tensor_tensor(out=Yo[:], in0=ps[4][:, 0:256], in1=Yl[:], op=ALU.add), reads=[psB[4], YlB], writes=[YoB])
        Yov = v3(Yo[:], 4)
        for c in range(4):
            P.op("dve", lambda e: e.bn_stats(out=st6[:, c * 6:(c + 1) * 6], in_=Yov[:, c, :]), reads=[YoB], writes=[st6B])
        for c in range(4):
            P.op("dve", lambda e: e.bn_aggr(out=mv[:, c * 2:(c + 1) * 2], in_=st6[:, c * 6:(c + 1) * 6]), reads=[st6B], writes=[mvB])
        mvv = v3(mv[:], 4)
        P.op("dve", lambda e: e.tensor_scalar(out=rs[:], in0=mvv[:, :, 1], scalar1=64e-5, scalar2=None, op0=ALU.add),
             reads=[mvB], writes=[rsB])
        P.op("act", lambda e: e.activation(out=rs[:], in_=rs[:], func=AF.Sqrt), reads=[rsB], writes=[rsB])
        P.op("dve", lambda e: e.reciprocal(out=rs[:], in_=rs[:]), reads=[rsB], writes=[rsB])
        for c in range(4):
            P.op("dve", lambda e: e.tensor_scalar(out=Yov[:, c, :], in0=Yov[:, c, :], scalar1=mv[:, 2 * c:2 * c + 1],
                                                  scalar2=rs[:, c:c + 1], op0=ALU.subtract, op1=ALU.mult),
                 reads=[YoB, mvB, rsB], writes=[YoB])
        P.op("dve", lambda e: e.tensor_tensor(out=Yo[:], in0=Yo[:], in1=rgn[:, 0:256], op=ALU.mult), reads=[YoB, cB], writes=[YoB])
        P.op("dve", lambda e: e.tensor_tensor(out=Yo[:], in0=Yo[:], in1=rgn[:, 256:512], op=ALU.add), reads=[YoB, cB], writes=[YoB])
        for c in range(4):
            P.op("dve", lambda e: e.scalar_tensor_tensor(out=Yov[:, c, :], in0=Vt[:, c * 64:(c + 1) * 64], scalar=sbon[:, c:c + 1],
                                                         in1=Yov[:, c, :], op0=ALU.mult, op1=ALU.add),
                 reads=[VtB, sbonB, YoB], writes=[YoB])
        P.op("dve", lambda e: e.tensor_tensor(out=Yo[:], in0=Yo[:], in1=gtm[:], op=ALU.mult), reads=[YoB, gtmB], writes=[YoB])
        for c in range(4):
            _mm(P, ps[6][0:64, c * 128:(c + 1) * 128], Yov[:, c, :], ident, True, True, [YoB, cB], [psB[6]], c == 3)
        o, oB = oT[tt % 2]
        P.op("act", lambda e: e.activation(out=o[:], in_=ps[6][0:64, :], func=AF.Copy), reads=[psB[6]], writes=[oB])
        P.dma("sp", rwT_out_d[:, sl], o[:], reads=[oB], is_output=True)


def build_mix_prog(parts=("att", "lru", "rwkv")):
    nc = bass.Bass("TRN2", target_bir_lowering=False)
    dt = lambda name, shape, dty, kind="ExternalInput": nc.dram_tensor(name, shape, dty, kind=kind).ap()
    hT_d = dt("hT", [D, SEQ], BF16)
    with ExitStack() as es:
        P = Prog(nc, es)
        M = MixCtx(P, nc, es, hT_d)
        if "att" in parts:
            w_att_d = dt("w_att", [D, 193], F32)
            avec_d = dt("avec", [128, 4], F32)
            cmask_d = dt("cmask", [128, 4 * TT + 128], BF16)
            att_o = dt("attT", [64, SEQ], BF16, "ExternalOutput")
            with ExitStack() as es2:
                emit_attention(M, es2, w_att_d, avec_d, cmask_d, att_o)
                P.barrier()
        if "lru" in parts:
            w_lru_d = dt("w_lru", [D, 256], F32)
            gab_d = dt("gab", [128, 256], F32)
            lvec_d = dt("lvec", [128, 8], F32)
            lru_o = dt("lruT", [128, SEQ], BF16, "ExternalOutput")
            with ExitStack() as es2:
                emit_lru(M, es2, w_lru_d, gab_d, lvec_d, lru_o)
                P.barrier()
        if "rwkv" in parts:
            w_rw_d = dt("w_rw", [D, 448], F32)
            rvec_d = dt("rvec", [128, 16], F32)
            rmat_d = dt("rmat", [128, 192], F32)
            rgn_d = dt("rgn", [128, 512], F32)
            rconst_d = dt("rconst", [128, 5120], F32)
            rw_o = dt("rwkvT", [64, SEQ], BF16, "ExternalOutput")
            with ExitStack() as es2:
                emit_rwkv(M, es2, w_rw_d, rvec_d, rmat_d, rgn_d, rconst_d, rw_o)
                P.barrier()
        P.finish()
    return nc


ATT_COLS = 1544
LRU_BASE = 1544
RWKV_BASE = 3592
GATE_BASE = 5384
_BF = ml_dtypes.bfloat16


def _cmask_const():
    p = np.arange(128)[:, None, None]
    a = np.arange(4)[None, :, None]
    c = np.arange(TT)[None, None, :]
    m = np.where(c - p - 128 * a >= 0, 0.0, MASKNEG).astype(np.float32).reshape(128, 4 * TT)
    return np.concatenate([m, np.eye(128, dtype=np.float32)], axis=1).astype(_BF)


def mix_inputs(inp, l, hT_full, parts=("att", "lru", "rwkv")):
    w_in = inp["w_in"][l]
    cm = _cmask_const()
    maps = []
    for j in range(NCORES):
        m = {"hT": hT_full}
        if "att" in parts:
            m["w_att"] = np.ascontiguousarray(np.concatenate(
                [w_in[:, 64 * j:64 * j + 64], w_in[:, 512 + 64 * j:512 + 64 * j + 64],
                 w_in[:, 1536 + j:1537 + j], w_in[:, 1024 + 64 * j:1024 + 64 * j + 64]], axis=1))
            av = np.zeros((128, 4), np.float32)
            av[:, 0] = inp["fox_f_bias"][l][j]
            m["avec"] = av
            m["cmask"] = cm
        if "lru" in parts:
            c0 = LRU_BASE + 128 * j
            m["w_lru"] = np.ascontiguousarray(np.concatenate([w_in[:, c0:c0 + 128], w_in[:, c0 + 1024:c0 + 1152]], axis=1))
            gab = np.zeros((128, 256), np.float32)
            for b in range(2):
                gab[64 * b:64 * b + 64, 64 * b:64 * b + 64] = inp["lru_ga_w"][l][2 * j + b]
                gab[64 * b:64 * b + 64, 128 + 64 * b:128 + 64 * b + 64] = inp["lru_gx_w"][l][2 * j + b]
            m["gab"] = gab
            ch = slice(128 * j, 128 * j + 128)
            m["lvec"] = np.ascontiguousarray(np.stack(
                [inp["lru_conv_w"][l][k][ch] for k in range(4)] +
                [inp["lru_conv_b"][l][ch], inp["lru_ga_b"][l][ch], inp["lru_gx_b"][l][ch], inp["lru_lambda"][l][ch]],
                axis=1).astype(np.float32))
        if "rwkv" in parts:
            m.update(rwkv_inputs(inp, l, j))
        maps.append(m)
    return maps


def _rconst():
    r = np.arange(128)[:, None]
    c = np.arange(128)[None, :]
    ident = np.eye(128, dtype=np.float32)
    su = (r < c).astype(np.float32)
    iu = (r <= c).astype(np.float32)
    sl = (c < r).astype(np.float32)
    cm = np.ones((128, 512), np.float32)
    cm[:, ::128] = 0.0
    first = np.concatenate([ident, np.zeros((128, 384), np.float32)], axis=1)
    blk = lambda k: (r // k) == (c // k)
    d16 = blk(16).astype(np.float32)
    off = lambda k: (blk(2 * k) & ~blk(k)).astype(np.float32)
    t4 = lambda m: np.tile(m, (1, 4))
    return np.ascontiguousarray(np.concatenate([first, t4(su), t4(iu), t4(sl), cm, t4(d16), t4(off(16)), t4(off(32)), t4(off(64)),
                                                t4(ident)], axis=1))


def rwkv_inputs(inp, l, j):
    w_in = inp["w_in"][l]
    b = RWKV_BASE
    hs = slice(64 * j, 64 * j + 64)
    cols = [w_in[:, b + 64 * j:b + 64 * j + 64], w_in[:, b + 512 + 64 * j:b + 512 + 64 * j + 64],
            w_in[:, b + 1024 + 64 * j:b + 1024 + 64 * j + 64], w_in[:, b + 1536:b + 1792]]
    mu = inp["rwkv_mu"][l]
    rvec = np.zeros((128, 16), np.float32)
    rvec[0:64, 0] = mu[64 * j:64 * j + 64]
    rvec[0:64, 1] = mu[512 + 64 * j:512 + 64 * j + 64]
    rvec[0:64, 2] = mu[1024 + 64 * j:1024 + 64 * j + 64]
    rvec[0:64, 3] = inp["rwkv_w0"][l][hs]
    rvec[0:64, 4] = inp["rwkv_a0"][l][hs]
    rvec[0:64, 5] = inp["rwkv_k_k"][l][hs]
    rvec[0:64, 6] = inp["rwkv_k_a"][l][hs]
    rvec[0:64, 7] = inp["rwkv_r_k"][l][j]
    rvec[:, 8] = mu[1536:1664]
    rvec[:, 9] = mu[1664:1792]
    rmat = np.zeros((128, 192), np.float32)
    rmat[0:64, 0:64] = inp["rwkv_w2"][l][:, hs]
    rmat[64:128, 64:128] = inp["rwkv_a2"][l][:, hs]
    rmat[:, 128:192] = inp["rwkv_g2"][l][:, hs]
    rgn = np.concatenate([np.tile(inp["rwkv_gn_w"][l][hs][None, :], (128, 4)),
                          np.tile(inp["rwkv_gn_b"][l][hs][None, :], (128, 4))], axis=1).astype(np.float32)
    return {"w_rw": np.ascontiguousarray(np.concatenate(cols, axis=1)), "rvec": rvec, "rmat": rmat,
            "rgn": np.ascontiguousarray(rgn), "rconst": _rconst()}


_PROGS = {}
_DBG = None


def _prog(name, builder):
    if name not in _PROGS:
        _PROGS[name] = builder()
    return _PROGS[name]


def _run(nc, maps):
    return run_bass_kernel_spmd(nc, maps, core_ids=list(range(NCORES))).results


def _pcol(v):
    return np.ascontiguousarray(np.asarray(v, np.float32).reshape(KC, 128).T)


def _vec(gate, lng, lnb, sh, sc):
    return np.ascontiguousarray(np.concatenate([_pcol(v) for v in (gate, lng, lnb, sh, sc)], axis=1))


def kernel(**inp):
    inp = {k: np.asarray(v) for k, v in inp.items()}
    x = inp["x"][0]
    c128 = _pcol(inp["c"][0])
    maps = []
    for c in range(NCORES):
        cs = slice(c * 1152, (c + 1) * 1152)
        bb = np.stack([inp["ada_b"][l][cs].reshape(9, 128).T for l in range(DEPTH)], axis=1).reshape(128, DEPTH * 9)
        maps.append({"c": c128, "ada_w": np.ascontiguousarray(inp["ada_w"][:, :, cs]), "ada_b": np.ascontiguousarray(bb.astype(np.float32))})
    res = _run(_prog("ada", build_ada_prog), maps)
    ada = np.zeros((DEPTH, 9 * D), np.float32)
    for c in range(NCORES):
        o = res[c]["ada_out"].reshape(128, DEPTH, 9)
        for l in range(DEPTH):
            ada[l, c * 1152:(c + 1) * 1152] = o[:, l, :].T.reshape(-1)
    adas = [np.split(ada[l], 9) for l in range(DEPTH)]
    if _DBG:
        _DBG("ada", 0, ada)
    zeros = np.zeros(D, np.float32)
    xT = [np.ascontiguousarray(x[c * TPC:(c + 1) * TPC].T) for c in range(NCORES)]
    v0 = _vec(zeros, zeros, zeros, adas[0][0], adas[0][1])
    res = _run(_prog("mod", build_mod_prog), [{"xT": xT[c], "vec": v0} for c in range(NCORES)])
    hT = [r["hT_out"] for r in res]
    if _DBG:
        _DBG("h1", 0, hT)
    for l in range(DEPTH):
        sh1, sc1, g1, sh2, sc2, g2, sh3, sc3, g3 = adas[l]
        v = _vec(g1, inp["ln_g"][l, 0], inp["ln_b"][l, 0], sh2, sc2)
        res = _run(_prog("ffn", build_ffn_prog), [{"xT": xT[c], "hT": hT[c], "w_up": inp["ffn_up"][l, 0],
                                                   "w_down": inp["ffn_down"][l, 0], "vec": v} for c in range(NCORES)])
        xT = [r["xT_out"] for r in res]
        hT = [r["hT_out"] for r in res]
        if _DBG:
            _DBG("x1", l, xT)
            _DBG("h2", l, hT)
        hfull = np.ascontiguousarray(np.concatenate(hT, axis=1))
        res = _run(_prog("mix", build_mix_prog), mix_inputs(inp, l, hfull))
        att = np.concatenate([r["attT"] for r in res], axis=0)
        lru = np.concatenate([r["lruT"] for r in res], axis=0)
        rwk = np.concatenate([r["rwkvT"] for r in res], axis=0)
        br = np.concatenate([att, lru, rwk], axis=0)
        if _DBG:
            _DBG("att", l, att)
            _DBG("lru", l, lru)
            _DBG("rwkv", l, rwk)
        wg = np.ascontiguousarray(inp["w_in"][l][:, GATE_BASE:GATE_BASE + 3072])
        wp = np.ascontiguousarray(np.concatenate([inp["w_proj_a"][l], inp["w_proj_b"][l], inp["w_proj_c"][l]], axis=0))
        v = _vec(g2, inp["ln_g"][l, 1], inp["ln_b"][l, 1], sh3, sc3)
        res = _run(_prog("mixpost", build_mixpost_prog),
                   [{"xT": xT[c], "hT": hT[c], "brT": np.ascontiguousarray(br[:, c * TPC:(c + 1) * TPC]), "wg": wg, "wp": wp,
                     "wo": inp["w_out"][l], "vec": v} for c in range(NCORES)])
        xT = [r["xT_out"] for r in res]
        hT = [r["hT_out"] for r in res]
        if _DBG:
            _DBG("x2", l, xT)
            _DBG("h3", l, hT)
        nsh, nsc = (adas[l + 1][0], adas[l + 1][1]) if l + 1 < DEPTH else (zeros, zeros)
        v = _vec(g3, inp["ln_g"][l, 2], inp["ln_b"][l, 2], nsh, nsc)
        res = _run(_prog("ffn", build_ffn_prog), [{"xT": xT[c], "hT": hT[c], "w_up": inp["ffn_up"][l, 1],
                                                   "w_down": inp["ffn_down"][l, 1], "vec": v} for c in range(NCORES)])
        xT = [r["xT_out"] for r in res]
        hT = [r["hT_out"] for r in res]
        if _DBG:
            _DBG("x3", l, xT)
    out = np.concatenate([t.T for t in xT], axis=0)[None].astype(np.float32)
    return out
```

```python
import numpy as np
import ml_dtypes
from contextlib import ExitStack
import threading

import concourse.bass as bass
import concourse.mybir as mybir
from concourse.bass_utils import run_bass_kernel_spmd

F32 = mybir.dt.float32
BF16 = mybir.dt.bfloat16
AF = mybir.ActivationFunctionType
ALU = mybir.AluOpType
AX = mybir.AxisListType

NCORES = 8
D = 1024
SEQ = 16384
DEPTH = 4
TPC = SEQ // NCORES
DFF = 2816
NJ = DFF // 128
KC = D // 128
ALPHA = (2 * DEPTH) ** 0.25
LN_EPS = 1e-5
HD = 64


class Tok:
    __slots__ = ("sem", "sid", "val")

    def __init__(self, sem, sid, val):
        self.sem, self.sid, self.val = sem, sid, val


class Buf:
    __slots__ = ("name", "w", "r")

    def __init__(self, name=""):
        self.name = name
        self.w = None
        self.r = []


class Prog:
    def __init__(self, nc, es, n_dma_sems=12):
        self.nc = nc
        self.es = es
        self.engs = {"pe": nc.tensor, "act": nc.scalar, "dve": nc.vector,
                     "pool": nc.gpsimd, "sp": nc.sync}
        self.esem = {}
        self.ecnt = {}
        self.seen = {e: {} for e in self.engs}
        self._sid = 0
        for e in self.engs:
            self.esem[e] = (es.enter_context(nc.semaphore("sem_" + e)), self._newsid())
            self.ecnt[e] = 0
        self.dsem = {}
        self.dpos = {}
        for q in ("sp", "act", "pool"):
            ring = []
            for i in range(n_dma_sems):
                ring.append([es.enter_context(nc.semaphore("dma_%s_%d" % (q, i))), self._newsid(), 0, None])
            self.dsem[q] = ring
            self.dpos[q] = 0
        self.out_toks = []
        self._tl = threading.local()

    def _newsid(self):
        self._sid += 1
        return self._sid

    def buf(self, name=""):
        return Buf(name)

    def bufs(self, n, name=""):
        return [Buf(name + str(i)) for i in range(n)]

    def _wait(self, e, tok):
        if tok is None:
            return
        if e == "pe" and tok.sid == self.esem["pe"][1]:
            return
        if self.seen[e].get(tok.sid, 0) >= tok.val:
            return
        self.engs[e].wait_ge(tok.sem, tok.val)
        self.seen[e][tok.sid] = tok.val

    def _deps(self, e, reads, writes):
        for b in reads:
            if b.w is not None:
                self._wait(e, b.w)
        for b in writes:
            if b.w is not None:
                self._wait(e, b.w)
            for t in b.r:
                self._wait(e, t)

    def _commit(self, tok, reads, writes):
        for b in reads:
            b.r.append(tok)
            if len(b.r) > 64:
                b.r = b.r[-64:]
        for b in writes:
            b.w = tok
            b.r = []

    def _yield(self):
        h = getattr(self._tl, "hook", None)
        if h is not None:
            h()

    def interleave(self, fa, fb, ka=2, kb=1):
        cond = threading.Condition()
        st = {"turn": 0, "alive": [True, True], "err": None}

        def make_hook(i, k):
            cnt = [0]

            def hook():
                cnt[0] += 1
                if cnt[0] % k:
                    return
                with cond:
                    if st["alive"][1 - i]:
                        st["turn"] = 1 - i
                        cond.notify_all()
                        while st["turn"] != i and st["alive"][1 - i]:
                            cond.wait()
            return hook

        def runner(i, f, k):
            try:
                with cond:
                    while st["turn"] != i and st["alive"][1 - i]:
                        cond.wait()
                self._tl.hook = make_hook(i, k)
                f()
            except BaseException as ex:
                st["err"] = ex
            finally:
                self._tl.hook = None
                with cond:
                    st["alive"][i] = False
                    st["turn"] = 1 - i
                    cond.notify_all()

        ta = threading.Thread(target=runner, args=(0, fa, ka))
        tb = threading.Thread(target=runner, args=(1, fb, kb))
        ta.start()
        tb.start()
        ta.join()
        tb.join()
        if st["err"] is not None:
            raise st["err"]

    def op(self, e, fn, reads=(), writes=(), inc=True):
        self._yield()
        self._deps(e, reads, writes)
        ins = fn(self.engs[e])
        sem, sid = self.esem[e]
        if inc:
            self.ecnt[e] += 1
            ins.then_inc(sem, 1)
            tok = Tok(sem, sid, self.ecnt[e])
        else:
            assert e == "pe"
            tok = Tok(sem, sid, self.ecnt[e] + 1)
        self._commit(tok, reads, writes)
        return tok

    def dma(self, q, out, in_, reads=(), writes=(), is_output=False):
        self._yield()
        ring = self.dsem[q]
        slot = ring[self.dpos[q] % len(ring)]
        self.dpos[q] += 1
        if slot[3] is not None:
            self._wait(q, slot[3])
        self._deps(q, reads, writes)
        ins = self.engs[q].dma_start(out=out, in_=in_)
        slot[2] += 16
        ins.then_inc(slot[0], 16)
        tok = Tok(slot[0], slot[1], slot[2])
        slot[3] = tok
        self._commit(tok, reads, writes)
        if is_output:
            self.out_toks.append(tok)
        return tok

    def barrier(self):
        toks = []
        for e in self.engs:
            if self.ecnt[e] > 0:
                toks.append(Tok(self.esem[e][0], self.esem[e][1], self.ecnt[e]))
        for q in self.dsem:
            for slot in self.dsem[q]:
                if slot[3] is not None:
                    toks.append(slot[3])
        for e in self.engs:
            for t in toks:
                self._wait(e, t)

    def finish(self):
        for t in self.out_toks:
            self._wait("sp", t)
        self.barrier()


TT = 512
NTT = TPC // TT


def _mm(P, ps_ap, lhsT, rhs, start, stop, reads, writes, inc):
    return P.op("pe", lambda e: e.matmul(ps_ap, lhsT, rhs, start=start, stop=stop),
                reads=reads, writes=writes, inc=inc)


class TokCtx:
    def __init__(self, P, nc, es, T=TPC):
        self.P, self.nc = P, nc
        self.T = T
        self.ntt = T // TT
        NTT = self.ntt
        sb = lambda name, shape, dt: es.enter_context(nc.sbuf_tensor("sb_" + name, shape, dt))
        self.xT = sb("xT", [128, KC, T], F32)
        self.hT = sb("hT", [128, KC, T], BF16)
        self.xB = [[P.buf("x%d_%d" % (n, t)) for t in range(NTT)] for n in range(KC)]
        self.hB = [[P.buf("h%d_%d" % (n, t)) for t in range(NTT)] for n in range(KC)]
        self.ones = sb("ones", [128, 128], BF16)
        self.onesB = P.buf("ones")
        self.vec = sb("vec", [128, 40], F32)
        self.vecB = P.buf("vec")
        self.gs = sb("gs", [128, KC], F32)
        self.sc1 = sb("sc1", [128, KC], F32)
        self.gsB = P.buf("gs")
        self.xb = sb("xb16", [128, KC, TT], BF16)
        self.sq = sb("sq16", [128, KC, TT], BF16)
        self.xbB, self.sqB = P.buf("xb"), P.buf("sq")
        self.m = sb("ln_m", [128, TT], F32)
        self.var = sb("ln_var", [128, TT], F32)
        self.rstd = sb("ln_rstd", [128, TT], F32)
        self.mB, self.varB, self.rstdB = P.buf("m"), P.buf("var"), P.buf("rstd")
        self.t1 = [sb("ln_t%d" % i, [128, TT], F32) for i in range(2)]
        self.t1B = P.bufs(2, "t1")
        self.ps = [es.enter_context(nc.psum_tensor("ps%d" % i, [128, TT], F32)) for i in range(8)]
        self.psB = P.bufs(8, "ps")
        P.op("pool", lambda e: e.memset(self.ones[:], 1.0), writes=[self.onesB])

    def load_vec(self, vec_d, gmul):
        P = self.P
        P.dma("sp", self.vec[:], vec_d, writes=[self.vecB])
        P.op("dve", lambda e: e.tensor_scalar(out=self.gs[:], in0=self.vec[:, 0:8], scalar1=float(gmul),
                                              scalar2=None, op0=ALU.mult),
             reads=[self.vecB], writes=[self.gsB])
        P.op("dve", lambda e: e.tensor_scalar(out=self.sc1[:], in0=self.vec[:, 32:40], scalar1=1.0,
                                              scalar2=None, op0=ALU.add),
             reads=[self.vecB], writes=[self.gsB])

    def load_x(self, xT_d, off=0):
        xv = xT_d.rearrange("(n p) t -> p n t", p=128)
        for n in range(KC):
            self.P.dma("sp", self.xT[:, n, :], xv[:, n, off:off + self.T], writes=self.xB[n])

    def load_h(self, hT_d, off=0):
        hv = hT_d.rearrange("(n p) t -> p n t", p=128)
        for n in range(KC):
            self.P.dma("sp", self.hT[:, n, :], hv[:, n, off:off + self.T], writes=self.hB[n])

    def ln_tile(self, tt, eps, scale_ap, bias_ap, out_ap, out_bufs, extra_reads):
        P = self.P
        sl = slice(tt * TT, (tt + 1) * TT)
        xin = [self.xB[n][tt] for n in range(KC)]
        P.op("act", lambda e: e.activation(out=self.sq[:], in_=self.xT[:, :, sl], func=AF.Square),
             reads=xin, writes=[self.sqB])
        P.op("pool", lambda e: e.tensor_copy(out=self.xb[:], in_=self.xT[:, :, sl]),
             reads=xin, writes=[self.xbB])
        s1, s2 = self.ps[6], self.ps[7]
        for n in range(KC):
            _mm(P, s1[:], self.ones[:], self.xb[:, n, :], n == 0, n == KC - 1,
                [self.onesB, self.xbB], [self.psB[6]], n == KC - 1)
        for n in range(KC):
            _mm(P, s2[:], self.ones[:], self.sq[:, n, :], n == 0, n == KC - 1,
                [self.onesB, self.sqB], [self.psB[7]], n == KC - 1)
        P.op("act", lambda e: e.activation(out=self.m[:], in_=s1[:], func=AF.Copy, scale=1.0 / D),
             reads=[self.psB[6]], writes=[self.mB])
        P.op("dve", lambda e: e.tensor_tensor(out=self.var[:], in0=self.m[:], in1=self.m[:], op=ALU.mult),
             reads=[self.mB], writes=[self.varB])
        P.op("dve", lambda e: e.scalar_tensor_tensor(out=self.var[:], in0=s2[:], scalar=1.0 / D, in1=self.var[:],
                                                     op0=ALU.mult, op1=ALU.subtract),
             reads=[self.psB[7], self.varB], writes=[self.varB])
        P.op("dve", lambda e: e.tensor_scalar(out=self.var[:], in0=self.var[:], scalar1=float(eps), scalar2=None,
                                              op0=ALU.add),
             reads=[self.varB], writes=[self.varB])
        P.op("act", lambda e: e.activation(out=self.var[:], in_=self.var[:], func=AF.Sqrt),
             reads=[self.varB], writes=[self.varB])
        P.op("dve", lambda e: e.reciprocal(out=self.rstd[:], in_=self.var[:]),
             reads=[self.varB], writes=[self.rstdB])
        for n in range(KC):
            k = n % 2
            t1, t1B = self.t1[k], self.t1B[k]
            P.op("dve", lambda e: e.tensor_tensor(out=t1[:], in0=self.xT[:, n, sl], in1=self.m[:], op=ALU.subtract),
                 reads=[self.xB[n][tt], self.mB], writes=[t1B])
            P.op("dve", lambda e: e.tensor_tensor(out=t1[:], in0=t1[:], in1=self.rstd[:], op=ALU.mult),
                 reads=[t1B, self.rstdB], writes=[t1B])
            P.op("dve", lambda e: e.tensor_scalar(out=out_ap(n), in0=t1[:], scalar1=scale_ap(n), scalar2=bias_ap(n),
                                                  op0=ALU.mult, op1=ALU.add),
                 reads=[t1B] + extra_reads, writes=[out_bufs(n)])

    def modulate_only(self, hT_out_d, off=0):
        P = self.P
        ho = hT_out_d.rearrange("(n p) t -> p n t", p=128)
        for tt in range(self.ntt):
            sl = slice(tt * TT, (tt + 1) * TT)
            osl = slice(off + tt * TT, off + (tt + 1) * TT)
            self.ln_tile(tt, LN_EPS,
                         lambda n: self.sc1[:, n:n + 1], lambda n: self.vec[:, 24 + n:25 + n],
                         lambda n: self.hT[:, n, sl], lambda n: self.hB[n][tt], [self.vecB, self.gsB])
            for n in range(KC):
                P.dma("sp", ho[:, n, osl], self.hT[:, n, sl], reads=[self.hB[n][tt]], is_output=True)

    def postnorm_and_modulate(self, xT_out_d, hT_out_d, off=0):
        P = self.P
        xo = xT_out_d.rearrange("(n p) t -> p n t", p=128)
        ho = hT_out_d.rearrange("(n p) t -> p n t", p=128) if hT_out_d is not None else None
        for tt in range(self.ntt):
            sl = slice(tt * TT, (tt + 1) * TT)
            osl = slice(off + tt * TT, off + (tt + 1) * TT)
            self.ln_tile(tt, LN_EPS / (ALPHA * ALPHA),
                         lambda n: self.vec[:, 8 + n:9 + n], lambda n: self.vec[:, 16 + n:17 + n],
                         lambda n: self.xT[:, n, sl], lambda n: self.xB[n][tt], [self.vecB])
            for n in range(KC):
                P.dma("sp", xo[:, n, osl], self.xT[:, n, sl], reads=[self.xB[n][tt]], is_output=True)
            if ho is not None:
                self.ln_tile(tt, LN_EPS,
                             lambda n: self.sc1[:, n:n + 1], lambda n: self.vec[:, 24 + n:25 + n],
                             lambda n: self.hT[:, n, sl], lambda n: self.hB[n][tt], [self.vecB, self.gsB])
                for n in range(KC):
                    P.dma("sp", ho[:, n, osl], self.hT[:, n, sl], reads=[self.hB[n][tt]], is_output=True)


def emit_ffn(C, es, w_up_d, w_down_d):
    P, nc = C.P, C.nc
    sb = lambda name, shape, dt: es.enter_context(nc.sbuf_tensor("sb_" + name, shape, dt))
    NH = NJ // 2
    aT = sb("aT", [128, NH, TPC], BF16)
    aB = [[P.buf() for t in range(NTT)] for j in range(NH)]
    NWB = 3
    wup = [sb("wup%d" % i, [128, KC, 256], BF16) for i in range(NWB)]
    wupB = P.bufs(NWB, "wup")
    wdn = [sb("wdn%d" % i, [128, NH, 128], BF16) for i in range(2)]
    wdnB = P.bufs(2, "wdn")
    st = [sb("silu%d" % i, [128, TT], F32) for i in range(2)]
    stB = P.bufs(2, "silu")
    wu_v = w_up_d.rearrange("(kc p) n -> p kc n", p=128)
    wd_v = w_down_d.rearrange("(j p) n -> p j n", p=128)
    it = 0
    for hf in range(2):
        for jj in range(NH):
            j = hf * NH + jj
            wb = (hf * NH + jj) % NWB
            P.dma("pool", wup[wb][:, :, 0:128], wu_v[:, :, j * 128:(j + 1) * 128], writes=[wupB[wb]])
            P.dma("pool", wup[wb][:, :, 128:256], wu_v[:, :, DFF + j * 128:DFF + (j + 1) * 128], writes=[wupB[wb]])
            for tt in range(NTT):
                sl = slice(tt * TT, (tt + 1) * TT)
                b = it % 2
                it += 1
                pu, pg = C.ps[b], C.ps[2 + b]
                for kc in range(KC):
                    _mm(P, pu[:], wup[wb][:, kc, 0:128], C.hT[:, kc, sl], kc == 0, kc == KC - 1,
                        [wupB[wb], C.hB[kc][tt]], [C.psB[b]], kc == KC - 1)
                for kc in range(KC):
                    _mm(P, pg[:], wup[wb][:, kc, 128:256], C.hT[:, kc, sl], kc == 0, kc == KC - 1,
                        [wupB[wb], C.hB[kc][tt]], [C.psB[2 + b]], kc == KC - 1)
                P.op("act", lambda e: e.activation(out=st[b][:], in_=pu[:], func=AF.Silu),
                     reads=[C.psB[b]], writes=[stB[b]])
                P.op("dve", lambda e: e.tensor_tensor(out=aT[:, jj, sl], in0=pg[:], in1=st[b][:], op=ALU.mult),
                     reads=[C.psB[2 + b], stB[b]], writes=[aB[jj][tt]])
        for n in range(KC):
            wb = n % 2
            P.dma("pool", wdn[wb][:], wd_v[:, hf * NH:(hf + 1) * NH, n * 128:(n + 1) * 128], writes=[wdnB[wb]])
            for tt in range(NTT):
                sl = slice(tt * TT, (tt + 1) * TT)
                b = (n * NTT + tt) % 2
                py = C.ps[4 + b]
                for jj in range(NH):
                    _mm(P, py[:], wdn[wb][:, jj, :], aT[:, jj, sl], jj == 0, jj == NH - 1,
                        [wdnB[wb], aB[jj][tt]], [C.psB[4 + b]], jj == NH - 1)
                P.op("dve", lambda e: e.scalar_tensor_tensor(out=C.xT[:, n, sl], in0=py[:], scalar=C.gs[:, n:n + 1],
                                                             in1=C.xT[:, n, sl], op0=ALU.mult, op1=ALU.add),
                     reads=[C.psB[4 + b], C.gsB, C.xB[n][tt]], writes=[C.xB[n][tt]])


def build_ffn_prog(final=False):
    nc = bass.Bass("TRN2", target_bir_lowering=False)
    xT_d = nc.dram_tensor("xT", [D, TPC], F32, kind="ExternalInput").ap()
    hT_d = nc.dram_tensor("hT", [D, TPC], BF16, kind="ExternalInput").ap()
    wu_d = nc.dram_tensor("w_up", [D, 2 * DFF], F32, kind="ExternalInput").ap()
    wd_d = nc.dram_tensor("w_down", [DFF, D], F32, kind="ExternalInput").ap()
    vec_d = nc.dram_tensor("vec", [128, 40], F32, kind="ExternalInput").ap()
    xo_d = nc.dram_tensor("xT_out", [D, TPC], F32, kind="ExternalOutput").ap()
    ho_d = None if final else nc.dram_tensor("hT_out", [D, TPC], BF16, kind="ExternalOutput").ap()
    with ExitStack() as es:
        P = Prog(nc, es)
        C = TokCtx(P, nc, es)
        C.load_vec(vec_d, 0.5 / ALPHA)
        C.load_x(xT_d)
        C.load_h(hT_d)
        with ExitStack() as es2:
            emit_ffn(C, es2, wu_d, wd_d)
            C.postnorm_and_modulate(xo_d, ho_d)
            P.finish()
    return nc


def build_mod_prog():
    nc = bass.Bass("TRN2", target_bir_lowering=False)
    xT_d = nc.dram_tensor("xT", [D, TPC], F32, kind="ExternalInput").ap()
    vec_d = nc.dram_tensor("vec", [128, 40], F32, kind="ExternalInput").ap()
    ho_d = nc.dram_tensor("hT_out", [D, TPC], BF16, kind="ExternalOutput").ap()
    with ExitStack() as es:
        P = Prog(nc, es)
        C = TokCtx(P, nc, es)
        C.load_vec(vec_d, 1.0)
        C.load_x(xT_d)
        C.modulate_only(ho_d)
        P.finish()
    return nc


def build_mixpost_prog():
    nc = bass.Bass("TRN2", target_bir_lowering=False)
    dt = lambda name, shape, dty, kind="ExternalInput": nc.dram_tensor(name, shape, dty, kind=kind).ap()
    xT_d = dt("xT", [D, TPC], F32)
    hT_d = dt("hT", [D, TPC], BF16)
    br_d = dt("brT", [2048, TPC], BF16)
    wg_d = dt("wg", [D, 3072], F32)
    wp_d = dt("wp", [2048, D], F32)
    wo_d = dt("wo", [D, D], F32)
    vec_d = dt("vec", [128, 40], F32)
    xo_d = dt("xT_out", [D, TPC], F32, "ExternalOutput")
    ho_d = dt("hT_out", [D, TPC], BF16, "ExternalOutput")
    with ExitStack() as es:
        P = Prog(nc, es)
        C = TokCtx(P, nc, es, T=TT)
        sb = lambda name, shape, dty: es.enter_context(nc.sbuf_tensor("sb_" + name, shape, dty))
        C.load_vec(vec_d, 1.0 / ALPHA)
        wg = sb("wg", [128, KC, 3072], BF16)
        wp = sb("wp", [128, 16, D], BF16)
        wo = sb("wo", [128, KC, D], BF16)
        wB = P.buf()
        wgv = wg_d.rearrange("(kc p) n -> p kc n", p=128)
        for kc in range(KC):
            P.dma("pool", wg[:, kc, :], wgv[:, kc, :], writes=[wB])
        wpv = wp_d.rearrange("(kc p) n -> p kc n", p=128)
        for kc in range(16):
            P.dma("pool", wp[:, kc, :], wpv[:, kc, :], writes=[wB])
        P.dma("pool", wo[:], wo_d.rearrange("(kc p) n -> p kc n", p=128), writes=[wB])
        br = sb("br", [128, 16, TT], BF16)
        brB = P.buf()
        mT = sb("mT", [128, KC, TT], BF16)
        mB = P.bufs(KC)
        sg = [sb("sg%d" % i, [128, TT], F32) for i in range(3)]
        sgB = P.bufs(3)
        ta = [sb("ta%d" % i, [128, TT], F32) for i in range(2)]
        taB = P.bufs(2)
        brv = br_d.rearrange("(kc p) t -> p kc t", p=128)
        for tk in range(TPC // TT):
            off = tk * TT
            C.load_x(xT_d, off)
            C.load_h(hT_d, off)
            P.dma("sp", br[:], brv[:, :, off:off + TT], writes=[brB])
            for n in range(KC):
                ns = slice(n * 128, (n + 1) * 128)
                for g in range(3):
                    for kc in range(KC):
                        _mm(P, C.ps[g][:], wg[:, kc, g * 1024 + n * 128:g * 1024 + (n + 1) * 128], C.hT[:, kc, :], kc == 0, kc == KC - 1,
                            [wB, C.hB[kc][0]], [C.psB[g]], kc == KC - 1)
                for g, (k0, nk) in enumerate(((0, 4), (4, 8), (12, 4))):
                    for kc in range(nk):
                        _mm(P, C.ps[3 + g][:], wp[:, k0 + kc, ns], br[:, k0 + kc, :], kc == 0, kc == nk - 1,
                            [wB, brB], [C.psB[3 + g]], kc == nk - 1)
                for g in range(3):
                    P.op("act", lambda e: e.activation(out=sg[g][:], in_=C.ps[g][:], func=AF.Sigmoid), reads=[C.psB[g]], writes=[sgB[g]])
                P.op("dve", lambda e: e.tensor_tensor(out=ta[0][:], in0=C.ps[3][:], in1=sg[0][:], op=ALU.mult),
                     reads=[C.psB[3], sgB[0]], writes=[taB[0]])
                P.op("dve", lambda e: e.tensor_tensor(out=ta[1][:], in0=C.ps[4][:], in1=sg[1][:], op=ALU.mult),
                     reads=[C.psB[4], sgB[1]], writes=[taB[1]])
                P.op("dve", lambda e: e.tensor_tensor(out=ta[0][:], in0=ta[0][:], in1=ta[1][:], op=ALU.add),
                     reads=[taB[0], taB[1]], writes=[taB[0]])
                P.op("dve", lambda e: e.tensor_tensor(out=ta[1][:], in0=C.ps[5][:], in1=sg[2][:], op=ALU.mult),
                     reads=[C.psB[5], sgB[2]], writes=[taB[1]])
                P.op("dve", lambda e: e.tensor_tensor(out=mT[:, n, :], in0=ta[0][:], in1=ta[1][:], op=ALU.add),
                     reads=[taB[0], taB[1]], writes=[mB[n]])
            for n2 in range(KC):
                b = n2 % 2
                for n in range(KC):
                    _mm(P, C.ps[b][:], wo[:, n, n2 * 128:(n2 + 1) * 128], mT[:, n, :], n == 0, n == KC - 1,
                        [wB, mB[n]], [C.psB[b]], n == KC - 1)
                P.op("dve", lambda e: e.scalar_tensor_tensor(out=C.xT[:, n2, :], in0=C.ps[b][:], scalar=C.gs[:, n2:n2 + 1],
                                                             in1=C.xT[:, n2, :], op0=ALU.mult, op1=ALU.add),
                     reads=[C.psB[b], C.gsB, C.xB[n2][0]], writes=[C.xB[n2][0]])
            C.postnorm_and_modulate(xo_d, ho_d, off)
        P.finish()
    return nc


def build_ada_prog():
    nc = bass.Bass("TRN2", target_bir_lowering=False)
    dt = lambda name, shape, dty, kind="ExternalInput": nc.dram_tensor(name, shape, dty, kind=kind).ap()
    c_d = dt("c", [128, KC], F32)
    w_d = dt("ada_w", [DEPTH, D, 1152], F32)
    b_d = dt("ada_b", [128, DEPTH * 9], F32)
    o_d = dt("ada_out", [128, DEPTH * 9], F32, "ExternalOutput")
    with ExitStack() as es:
        P = Prog(nc, es)
        sb = lambda name, shape, dty: es.enter_context(nc.sbuf_tensor("sb_" + name, shape, dty))
        ct = sb("c", [128, KC], F32)
        bt = sb("b", [128, DEPTH * 9], F32)
        ot = sb("o", [128, DEPTH * 9], F32)
        cB, bB, oB = P.buf(), P.buf(), P.buf()
        P.dma("sp", ct[:], c_d, writes=[cB])
        P.dma("sp", bt[:], b_d, writes=[bB])
        P.op("act", lambda e: e.activation(out=ct[:], in_=ct[:], func=AF.Silu), reads=[cB], writes=[cB])
        ps = es.enter_context(nc.psum_tensor("ps", [128, 64], F32))
        psB = P.buf()
        wt = [sb("w%d" % i, [128, KC, 1152], F32) for i in range(2)]
        wtB = P.bufs(2)
        for l in range(DEPTH):
            w, wB = wt[l % 2], wtB[l % 2]
            wv = w_d[l].rearrange("(kc p) n -> p kc n", p=128)
            for kc in range(KC):
                P.dma("sp", w[:, kc, :], wv[:, kc, :], writes=[wB])
            for ch in range(9):
                col = l * 9 + ch
                for kc in range(KC):
                    _mm(P, ps[:, col:col + 1], w[:, kc, ch * 128:(ch + 1) * 128], ct[:, kc:kc + 1], kc == 0, kc == KC - 1,
                        [wB, cB], [psB], kc == KC - 1)
        P.op("dve", lambda e: e.tensor_tensor(out=ot[:], in0=ps[:, 0:DEPTH * 9], in1=bt[:], op=ALU.add), reads=[psB, bB], writes=[oB])
        P.dma("sp", o_d, ot[:], reads=[oB], is_output=True)
        P.finish()
    return nc


NQT = SEQ // TT
NKB = SEQ // 128
MASKNEG = -30000.0


class MixCtx:
    def __init__(self, P, nc, es, hT_d, parent=None, tag=""):
        self.P, self.nc = P, nc
        self.sb = lambda name, shape, dt: es.enter_context(nc.sbuf_tensor("sb_" + tag + name, shape, dt))
        self.hv = hT_d.rearrange("(n p) t -> p n t", p=128)
        if parent is None:
            self.psall = es.enter_context(nc.psum_tensor("psall", [128, 8 * TT], F32))
            self.psB = P.bufs(8, "ps")
        else:
            self.psall, self.psB = parent.psall, parent.psB
        self.ps = [self.psall[:, i * TT:(i + 1) * TT] for i in range(8)]
        self.hbuf = [self.sb("hbuf%d" % i, [128, KC, TT], BF16) for i in range(2)]
        self.hbufB = P.bufs(2, "hbuf")
        self.hcnt = 0

    def load_h_tile(self, tt):
        i = self.hcnt % 2
        self.hcnt += 1
        self.P.dma("sp", self.hbuf[i][:], self.hv[:, :, tt * TT:(tt + 1) * TT], writes=[self.hbufB[i]])
        return self.hbuf[i], self.hbufB[i]

    def load_w(self, name, w_d, ncols):
        t = self.sb(name, [128, KC, ncols], BF16)
        b = self.P.buf(name)
        self.P.dma("pool", t[:], w_d.rearrange("(kc p) n -> p kc n", p=128), writes=[b])
        return t, b


def emit_attention(M, es, w_att_d, avec_d, cmask_d, attT_out_d, npb=3):
    P, nc = M.P, M.nc
    sb = lambda name, shape, dt: es.enter_context(nc.sbuf_tensor("sb_" + name, shape, dt))
    W, WB = M.load_w("w_att", w_att_d, 193)
    Qx = sb("Qx", [70, SEQ], BF16)
    Kx = sb("Kx", [70, SEQ], BF16)
    Vx = sb("Vx", [128, NKB, 65], BF16)
    QB = [P.buf() for _ in range(NQT)]
    KB = [P.buf() for _ in range(NQT)]
    VB = [P.buf() for _ in range(NQT)]
    avec = sb("avec", [128, 4], F32)
    avecB = P.buf()
    P.dma("sp", avec[:], avec_d, writes=[avecB])
    cmask = sb("cmask", [128, 4, TT], BF16)
    ident = sb("identb", [128, 128], BF16)
    cB = P.buf()
    P.dma("sp", cmask[:], cmask_d[:, 0:4 * TT].rearrange("p (a t) -> p a t", a=4), writes=[cB])
    P.dma("sp", ident[:], cmask_d[:, 4 * TT:4 * TT + 128], writes=[cB])
    sel = sb("sel", [128, 8, 70], BF16)
    onesr = sb("onesr", [128, TT], BF16)
    onesf = sb("onesf", [128, TT], F32)
    selB = P.buf()
    P.op("pool", lambda e: e.memset(sel[:], 0.0), writes=[selB])
    P.op("pool", lambda e: e.memset(onesr[:], 1.0), writes=[selB])
    P.op("pool", lambda e: e.memset(onesf[:], 1.0), writes=[selB])
    for i, (c0, c1, v) in enumerate([(64, 67, 1.0), (67, 68, 1.0), (68, 69, 1.0), (69, 70, 1.0),
                                     (67, 70, -1.0), (64, 65, 1.0), (65, 66, 1.0), (66, 67, 1.0)]):
        P.op("pool", lambda e: e.memset(sel[64:65, i, c0:c1], v), writes=[selB])
    P.op("pool", lambda e: e.memset(Vx[:, :, 64:65], 1.0), writes=VB)
    nfb = sb("nfb", [128, 1], F32)
    P.op("dve", lambda e: e.tensor_scalar(out=nfb[:], in0=avec[:, 0:1], scalar1=-1.0, scalar2=None, op0=ALU.mult),
         reads=[avecB], writes=[avecB])
    fl = sb("fl", [128, TT], F32)
    fr = [sb("fr%d" % i, [128, TT], F32) for i in range(2)]
    f16 = [sb("f16_%d" % i, [128, TT], BF16) for i in range(3)]
    flB, frB, f16B = P.buf(), P.bufs(2), P.bufs(3)
    carry = sb("fcarry", [128, 1], F32)
    carryB = P.buf()
    P.op("dve", lambda e: e.memset(carry[:], 0.0), writes=[carryB])
    r64 = slice(64, 65)
    for tt in range(NQT):
        sl = slice(tt * TT, (tt + 1) * TT)
        hb, hbB = M.load_h_tile(tt)
        pq, pk = M.ps[0], M.ps[1]
        for kc in range(KC):
            _mm(P, pq[0:64, :], W[:, kc, 0:64], hb[:, kc, :], kc == 0, kc == KC - 1, [WB, hbB], [M.psB[0]], kc == KC - 1)
        for kc in range(KC):
            _mm(P, pk[0:65, :], W[:, kc, 64:129], hb[:, kc, :], kc == 0, kc == KC - 1, [WB, hbB], [M.psB[1]], kc == KC - 1)
        P.op("act", lambda e: e.activation(out=Qx[0:64, sl], in_=pq[0:64, :], func=AF.Copy, scale=0.125),
             reads=[M.psB[0]], writes=[QB[tt]])
        P.op("dve", lambda e: e.tensor_copy(out=Kx[0:64, sl], in_=pk[0:64, :]), reads=[M.psB[1]], writes=[KB[tt]])
        P.op("act", lambda e: e.activation(out=fl[r64, :], in_=pk[r64, :], func=AF.Exp, scale=-1.0, bias=nfb[r64, :]),
             reads=[M.psB[1], avecB], writes=[flB])
        P.op("act", lambda e: e.activation(out=fl[r64, :], in_=fl[r64, :], func=AF.Ln, bias=1.0),
             reads=[flB], writes=[flB])
        P.op("dve", lambda e: e.tensor_tensor_scan(out=fr[0][r64, :], data0=onesf[r64, :], data1=fl[r64, :],
                                                   initial=carry[r64, :], op0=ALU.mult, op1=ALU.add),
             reads=[flB, carryB, selB], writes=[frB[0]])
        P.op("dve", lambda e: e.tensor_copy(out=carry[r64, :], in_=fr[0][r64, TT - 1:TT]), reads=[frB[0]], writes=[carryB])
        P.op("dve", lambda e: e.tensor_copy(out=f16[0][r64, :], in_=fr[0][r64, :]), reads=[frB[0]], writes=[f16B[0]])
        P.op("dve", lambda e: e.tensor_tensor(out=fr[1][r64, :], in0=fr[0][r64, :], in1=f16[0][r64, :], op=ALU.subtract),
             reads=[frB[0], f16B[0]], writes=[frB[1]])
        P.op("dve", lambda e: e.tensor_copy(out=f16[1][r64, :], in_=fr[1][r64, :]), reads=[frB[1]], writes=[f16B[1]])
        P.op("dve", lambda e: e.tensor_tensor(out=fr[0][r64, :], in0=fr[1][r64, :], in1=f16[1][r64, :], op=ALU.subtract),
             reads=[frB[1], f16B[1]], writes=[frB[0]])
        P.op("dve", lambda e: e.tensor_copy(out=f16[2][r64, :], in_=fr[0][r64, :]), reads=[frB[0]], writes=[f16B[2]])
        pa, pb = M.ps[2], M.ps[3]
        srcs = [onesr, f16[0], f16[1], f16[2]]
        srcB = [selB, f16B[0], f16B[1], f16B[2]]
        for i in range(4):
            _mm(P, pa[0:70, :], sel[r64, i, :], srcs[i][r64, :], i == 0, i == 3, [selB, srcB[i]], [M.psB[2]], i == 3)
        for i in range(4):
            _mm(P, pb[0:70, :], sel[r64, 4 + i, :], srcs[i][r64, :], i == 0, i == 3, [selB, srcB[i]], [M.psB[3]], i == 3)
        P.op("act", lambda e: e.activation(out=Qx[64:70, sl], in_=pa[64:70, :], func=AF.Copy),
             reads=[M.psB[2]], writes=[QB[tt]])
        P.op("dve", lambda e: e.tensor_copy(out=Kx[64:70, sl], in_=pb[64:70, :]), reads=[M.psB[3]], writes=[KB[tt]])
        pvi = 6 + tt % 2
        pv = M.ps[pvi]
        for bk in range(4):
            for kc in range(KC):
                _mm(P, pv[:, bk * 64:(bk + 1) * 64], hb[:, kc, bk * 128:(bk + 1) * 128], W[:, kc, 129:193],
                    kc == 0, kc == KC - 1, [WB, hbB], [M.psB[pvi]], kc == KC - 1)
        P.op("dve", lambda e: e.tensor_copy(out=Vx[:, tt * 4:(tt + 1) * 4, 0:64],
                                            in_=pv[:, 0:256].rearrange("p (b d) -> p b d", b=4)),
             reads=[M.psB[pvi]], writes=[VB[tt]])
    NPB = npb
    pT = [sb("pT%d" % i, [128, 2 * TT], BF16) for i in range(NPB)]
    pTB = P.bufs(NPB)
    den = sb("den", [128, TT], F32)
    bc = sb("bc", [64, TT], F32)
    ao = [sb("ao%d" % i, [64, TT], BF16) for i in range(2)]
    denB, bcB, aoB = P.buf(), P.buf(), P.bufs(2)
    pairs = [(I, J) for I in range(NQT) for J in range(0, 4 * I + 4, 2)]
    LOOK = NPB - 1
    slot = [0]
    slots = {}

    def issue_S(n):
        I, J0 = pairs[n]
        b = slot[0] % NPB
        slot[0] += 1
        slots[n] = b
        for k in range(2):
            J = J0 + k
            pS, pSB = M.ps[2 * b + k], M.psB[2 * b + k]
            diag = J >= 4 * I
            _mm(P, pS[:], Kx[:, J * 128:(J + 1) * 128], Qx[:, I * TT:(I + 1) * TT], True, not diag,
                [KB[J // 4], QB[I]], [pSB], not diag)
            if diag:
                _mm(P, pS[:], ident[:], cmask[:, J - 4 * I, :], False, True, [cB], [pSB], True)

    for n in range(min(LOOK, len(pairs))):
        issue_S(n)
    for n, (I, J0) in enumerate(pairs):
        b = slots.pop(n)
        qs = slice(I * TT, (I + 1) * TT)
        po, poB = M.ps[6 + I % 2], M.psB[6 + I % 2]
        nJ = 4 * I + 4
        P.op("act", lambda e: e.activation(out=pT[b][:], in_=M.psall[:, 2 * b * TT:(2 * b + 2) * TT], func=AF.Exp),
             reads=[M.psB[2 * b], M.psB[2 * b + 1]], writes=[pTB[b]])
        if n + LOOK < len(pairs):
            issue_S(n + LOOK)
        for k in range(2):
            J = J0 + k
            _mm(P, po[0:65, :], Vx[:, J, :], pT[b][:, k * TT:(k + 1) * TT], J == 0, J == nJ - 1, [VB[J // 4], pTB[b]], [poB],
                k == 1)
        if J0 + 2 == nJ:
            P.op("dve", lambda e: e.reciprocal(out=den[r64, :], in_=po[r64, :]), reads=[poB], writes=[denB])
            bb_ = slot[0] % NPB
            slot[0] += 1
            pbc, pbcB = M.ps[2 * bb_], M.psB[2 * bb_]
            _mm(P, pbc[0:64, :], onesf[r64, 0:64], den[r64, :], True, True, [selB, denB], [pbcB], True)
            P.op("dve", lambda e: e.tensor_copy(out=bc[:], in_=pbc[0:64, :]), reads=[pbcB], writes=[bcB])
            P.op("dve", lambda e: e.tensor_tensor(out=ao[I % 2][:], in0=po[0:64, :], in1=bc[:], op=ALU.mult),
                 reads=[poB, bcB], writes=[aoB[I % 2]])
            P.dma("sp", attT_out_d[:, qs], ao[I % 2][:], reads=[aoB[I % 2]], is_output=True)


def emit_lru(M, es, w_lru_d, gab_d, lvec_d, lruT_out_d, banks=(0, 1, 2, 3, 4, 5, 6, 7)):
    P, nc = M.P, M.nc
    sb = lambda name, shape, dt: es.enter_context(nc.sbuf_tensor("sb_" + name, shape, dt))
    W, WB = M.load_w("w_lru", w_lru_d, 256)
    gab = sb("gab", [128, 256], BF16)
    gabB = P.buf()
    P.dma("pool", gab[:], gab_d, writes=[gabB])
    lv = sb("lvec", [128, 8], F32)
    lvB = P.buf()
    P.dma("sp", lv[:], lvec_d, writes=[lvB])
    cv = sb("lconst", [128, 4], F32)
    cvB = P.buf()
    P.op("act", lambda e: e.activation(out=cv[:, 3:4], in_=lv[:, 7:8], func=AF.Exp, scale=-1.0), reads=[lvB], writes=[cvB])
    P.op("act", lambda e: e.activation(out=cv[:, 3:4], in_=cv[:, 3:4], func=AF.Ln, bias=1.0), reads=[cvB], writes=[cvB])
    P.op("dve", lambda e: e.tensor_scalar(out=cv[:, 0:1], in0=cv[:, 3:4], scalar1=-4.0, scalar2=None, op0=ALU.mult),
         reads=[cvB], writes=[cvB])
    P.op("dve", lambda e: e.tensor_scalar(out=cv[:, 1:3], in0=lv[:, 5:7], scalar1=0.5, scalar2=None, op0=ALU.mult),
         reads=[lvB, cvB], writes=[cvB])
    xbuf = sb("xbuf", [128, 3 + TT], F32)
    xbufB = P.buf()
    P.op("dve", lambda e: e.memset(xbuf[:, 0:3], 0.0), writes=[xbufB])
    names = ["xc", "tr", "a", "ti", "om", "u", "hc", "y2", "gl"]
    T = {n: sb("l_" + n, [128, TT], F32) for n in names}
    B = {n: P.buf(n) for n in names}
    xc16 = sb("xc16", [128, TT], BF16)
    xc16B = P.buf()
    ob = [sb("lo%d" % i, [128, TT], BF16) for i in range(2)]
    obB = P.bufs(2)
    hcar = sb("hcar", [128, 1], F32)
    hcarB = P.buf()
    P.op("dve", lambda e: e.memset(hcar[:], 0.0), writes=[hcarB])
    for tt in range(NQT):
        sl = slice(tt * TT, (tt + 1) * TT)
        hb, hbB = M.load_h_tile(tt)
        if len(banks) >= 8:
            bx, by, br_, bi_ = banks[0 + tt % 2], banks[2 + tt % 2], banks[4 + tt % 2], banks[6 + tt % 2]
        else:
            bx, by, br_, bi_ = banks[0], banks[1], banks[0], banks[1]
        px, py = M.ps[bx], M.ps[by]
        pxB, pyB = M.psB[bx], M.psB[by]
        for kc in range(KC):
            _mm(P, px[:], W[:, kc, 0:128], hb[:, kc, :], kc == 0, kc == KC - 1, [WB, hbB], [pxB], kc == KC - 1)
        for kc in range(KC):
            _mm(P, py[:], W[:, kc, 128:256], hb[:, kc, :], kc == 0, kc == KC - 1, [WB, hbB], [pyB], kc == KC - 1)
        P.op("act", lambda e: e.activation(out=xbuf[:, 3:3 + TT], in_=px[:], func=AF.Copy), reads=[pxB], writes=[xbufB])
        P.op("dve", lambda e: e.tensor_scalar(out=T["xc"][:], in0=xbuf[:, 3:3 + TT], scalar1=lv[:, 3:4], scalar2=lv[:, 4:5],
                                              op0=ALU.mult, op1=ALU.add), reads=[xbufB, lvB], writes=[B["xc"]])
        for k in (2, 1, 0):
            P.op("dve", lambda e: e.scalar_tensor_tensor(out=T["xc"][:], in0=xbuf[:, k:k + TT], scalar=lv[:, k:k + 1],
                                                         in1=T["xc"][:], op0=ALU.mult, op1=ALU.add),
                 reads=[xbufB, lvB, B["xc"]], writes=[B["xc"]])
        P.op("dve", lambda e: e.tensor_copy(out=xbuf[:, 0:3], in_=xbuf[:, TT:TT + 3]), reads=[xbufB], writes=[xbufB])
        P.op("pool", lambda e: e.tensor_copy(out=xc16[:], in_=T["xc"][:]), reads=[B["xc"]], writes=[xc16B])
        P.op("act", lambda e: e.activation(out=T["y2"][:], in_=py[:], func=AF.Square), reads=[pyB], writes=[B["y2"]])
        P.op("dve", lambda e: e.tensor_scalar(out=T["y2"][:], in0=T["y2"][:], scalar1=0.044715, scalar2=1.0,
                                              op0=ALU.mult, op1=ALU.add), reads=[B["y2"]], writes=[B["y2"]])
        P.op("dve", lambda e: e.tensor_tensor(out=T["y2"][:], in0=py[:], in1=T["y2"][:], op=ALU.mult),
             reads=[pyB, B["y2"]], writes=[B["y2"]])
        P.op("act", lambda e: e.activation(out=T["gl"][:], in_=T["y2"][:], func=AF.Tanh, scale=0.7978845608028654),
             reads=[B["y2"]], writes=[B["gl"]])
        P.op("dve", lambda e: e.scalar_tensor_tensor(out=T["gl"][:], in0=T["gl"][:], scalar=1.0, in1=py[:],
                                                     op0=ALU.add, op1=ALU.mult), reads=[B["gl"], pyB], writes=[B["gl"]])
        pr, pi = M.ps[br_], M.ps[bi_]
        prB, piB = M.psB[br_], M.psB[bi_]
        _mm(P, pr[:], gab[:, 0:128], xc16[:], True, True, [gabB, xc16B], [prB], True)
        _mm(P, pi[:], gab[:, 128:256], xc16[:], True, True, [gabB, xc16B], [piB], True)
        P.op("act", lambda e: e.activation(out=T["tr"][:], in_=pr[:], func=AF.Tanh, scale=0.5, bias=cv[:, 1:2]),
             reads=[prB, cvB], writes=[B["tr"]])
        P.op("act", lambda e: e.activation(out=T["a"][:], in_=T["tr"][:], func=AF.Exp, scale=cv[:, 0:1], bias=cv[:, 0:1]),
             reads=[B["tr"], cvB], writes=[B["a"]])
        P.op("act", lambda e: e.activation(out=T["ti"][:], in_=pi[:], func=AF.Tanh, scale=0.5, bias=cv[:, 2:3]),
             reads=[piB, cvB], writes=[B["ti"]])
        P.op("dve", lambda e: e.tensor_tensor(out=T["om"][:], in0=T["a"][:], in1=T["a"][:], op=ALU.mult),
             reads=[B["a"]], writes=[B["om"]])
        P.op("dve", lambda e: e.tensor_scalar(out=T["om"][:], in0=T["om"][:], scalar1=-1.0, scalar2=1.0,
                                              op0=ALU.mult, op1=ALU.add), reads=[B["om"]], writes=[B["om"]])
        P.op("act", lambda e: e.activation(out=T["om"][:], in_=T["om"][:], func=AF.Sqrt), reads=[B["om"]], writes=[B["om"]])
        P.op("dve", lambda e: e.scalar_tensor_tensor(out=T["u"][:], in0=T["ti"][:], scalar=1.0, in1=T["xc"][:],
                                                     op0=ALU.add, op1=ALU.mult), reads=[B["ti"], B["xc"]], writes=[B["u"]])
        P.op("dve", lambda e: e.scalar_tensor_tensor(out=T["u"][:], in0=T["u"][:], scalar=0.5, in1=T["om"][:],
                                                     op0=ALU.mult, op1=ALU.mult), reads=[B["u"], B["om"]], writes=[B["u"]])
        P.op("dve", lambda e: e.tensor_tensor_scan(out=T["hc"][:], data0=T["a"][:], data1=T["u"][:], initial=hcar[:],
                                                   op0=ALU.mult, op1=ALU.add), reads=[B["a"], B["u"], hcarB], writes=[B["hc"]])
        P.op("dve", lambda e: e.tensor_copy(out=hcar[:], in_=T["hc"][:, TT - 1:TT]), reads=[B["hc"]], writes=[hcarB])
        o = ob[tt % 2]
        P.op("dve", lambda e: e.scalar_tensor_tensor(out=o[:], in0=T["hc"][:], scalar=0.5, in1=T["gl"][:],
                                                     op0=ALU.mult, op1=ALU.mult), reads=[B["hc"], B["gl"]], writes=[obB[tt % 2]])
        P.dma("sp", lruT_out_d[:, sl], o[:], reads=[obB[tt % 2]], is_output=True)


def emit_rwkv(M, es, w_rw_d, rvec_d, rmat_d, rgn_d, rconst_d, rwT_out_d):
    P, nc = M.P, M.nc
    sb = lambda name, shape, dt: es.enter_context(nc.sbuf_tensor("sb_" + name, shape, dt))
    W, WB = M.load_w("w_rw", w_rw_d, 448)
    rv = sb("rvec", [128, 16], F32)
    rm16 = sb("rmat16", [128, 192], BF16)
    rgn = sb("rgn", [128, 512], F32)
    rc = sb("rconst", [128, 10 * 512], F32)
    cB = P.buf()
    P.dma("sp", rv[:], rvec_d, writes=[cB])
    P.dma("pool", rm16[:], rmat_d, writes=[cB])
    P.dma("sp", rgn[:], rgn_d, writes=[cB])
    P.dma("sp", rc[:], rconst_d, writes=[cB])
    ident = rc[:, 0:128]
    m_su = rc[:, 512:1024].rearrange("p (c t) -> p c t", c=4)
    m_iu = rc[:, 1024:1536].rearrange("p (c t) -> p c t", c=4)
    m_sl = rc[:, 1536:2048].rearrange("p (c t) -> p c t", c=4)
    cmask = rc[:, 2048:2560]
    m_d16 = rc[:, 2560:3072]
    m_o16 = rc[:, 3072:3584]
    m_o32 = rc[:, 3584:4096]
    m_o64 = rc[:, 4096:4608]
    identx4 = rc[:, 4608:5120]
    onesc = rc[:, 128:129]
    ident16 = sb("ident16", [128, 128], BF16)
    P.op("dve", lambda e: e.tensor_copy(out=ident16[:], in_=rc[:, 0:128]), reads=[cB], writes=[cB])
    ones64 = sb("ones64", [128, 64], F32)
    P.op("pool", lambda e: e.memset(ones64[:], 1.0), writes=[cB])
    dv = sb("rdv", [128, 16], F32)
    P.op("dve", lambda e: e.tensor_scalar(out=dv[:, 0:10], in0=rv[:, 0:10], scalar1=-1.0, scalar2=1.0, op0=ALU.mult, op1=ALU.add),
         reads=[cB], writes=[cB])
    P.op("dve", lambda e: e.tensor_scalar(out=dv[:, 10:12], in0=rv[:, 3:5], scalar1=0.5, scalar2=None, op0=ALU.mult),
         reads=[cB], writes=[cB])
    P.op("dve", lambda e: e.tensor_scalar(out=dv[:, 12:13], in0=rv[:, 6:7], scalar1=0.5, scalar2=None, op0=ALU.mult),
         reads=[cB], writes=[cB])
    P.op("dve", lambda e: e.tensor_scalar(out=dv[:, 13:14], in0=rv[:, 6:7], scalar1=-0.5, scalar2=1.0, op0=ALU.mult, op1=ALU.add),
         reads=[cB], writes=[cB])
    def t64(name, dt=F32, n=TT):
        return sb("r_" + name, [64, n], dt), P.buf(name)
    def t128(name, dt=F32, n=TT):
        return sb("r_" + name, [128, n], dt), P.buf(name)
    zr, zrB = t64("zr", n=TT + 1); zk, zkB = t64("zk", n=TT + 1); zv, zvB = t64("zv", n=TT + 1)
    zwa, zwaB = t128("zwa", n=TT + 1); zg, zgB = t128("zg", n=TT + 1)
    for z, zB in ((zr, zrB), (zk, zkB), (zv, zvB), (zwa, zwaB), (zg, zgB)):
        P.op("dve", lambda e: e.memset(z[:, 0:1], 0.0), writes=[zB])
    rm, rmB = t64("rm"); km, kmB = t64("km"); vm, vmB = t64("vm", BF16)
    wam, wamB = t128("wam"); gm, gmB = t128("gm")
    tmp, tmpB = t128("tmp")
    twl, twlB = t128("twl", BF16); sg16, sg16B = t128("sg16", BF16)
    lw, lwB = t64("lw"); ta, taB = t64("ta"); kk, kkB = t64("kk"); kf, kfB = t64("kf"); bb, bbB = t64("bb")
    cw, cwB = t64("cw"); cwx, cwxB = t64("cwx"); dd, ddB = t64("dd")
    E2, E2B = t64("E2"); E3, E3B = t64("E3"); E4, E4B = t64("E4")
    BhT, BhTB = t64("BhT", BF16); KhT, KhTB = t64("KhT", BF16)
    SH = [{"E1": t64("E1_%d" % i), "AR": t64("AR_%d" % i, BF16, n=4 * 256), "bT": t64("bT_%d" % i, BF16), "kT": t64("kT_%d" % i, BF16),
           "Vt": t128("Vt_%d" % i, BF16, n=256), "Bh": t128("Bh_%d" % i, BF16, n=256), "Kh": t128("Kh_%d" % i, BF16, n=256),
           "gtm": t128("gtm_%d" % i, n=256), "sbon": t128("sbon_%d" % i, n=4)} for i in range(2)]
    prod, prodB = t64("prod")
    IDT = BF16
    X = [t128("X%d" % i, IDT) for i in range(2)]
    XT = [t128("XT%d" % i, IDT) for i in range(2)]
    Yrb, YrbB = t128("Yrb", BF16); Xak, XakB = t128("Xak", BF16); Yrk, YrkB = t128("Yrk", BF16)
    Z = [t128("Z0", IDT), t128("Z1", BF16)]
    iv = {n: t128("iv_" + n, IDT) for n in ("S0", "S0T", "S1", "S1T", "S2", "S2T", "S3", "S3T", "J", "JT", "Fa", "FaT", "Fb", "FbT",
                                       "U", "L", "W1", "V1", "Ta", "Tb", "Na", "Nb")}
    RpT, RpTB = t64("RpT"); Yl, YlB = t128("Yl", n=256); MT, MTB = t64("MT", n=256); Psi, PsiB = t64("Psi", n=256)
    Tst = [t64("T%d" % i, n=64) for i in range(2)]
    P.op("dve", lambda e: e.memset(Tst[0][0][:], 0.0), writes=[Tst[0][1]])
    Yo, YoB = t128("Yo", n=256); Yo16, Yo16B = t128("Yo16", BF16, n=256)
    st6, st6B = t128("st6", n=24); mv, mvB = t128("mv", n=8); rs, rsB = t128("rs", n=4)
    oT = [t64("oT%d" % i, BF16) for i in range(2)]
    v3 = lambda ap, c: ap.rearrange("p (c t) -> p c t", c=c)
    ps, psB = M.ps, M.psB
    big = M.psall
    tstate = [0]

    def prep(tt):
        ps = [M.ps[0], M.ps[1]] * 4
        psB = [M.psB[0], M.psB[1]] * 4
        sh = SH[tt % 2]
        (E1, E1B), (AR, ARB), (bT, bTB), (kT, kTB) = sh["E1"], sh["AR"], sh["bT"], sh["kT"]
        (Vt, VtB), (Bh, BhB), (Kh, KhB), (gtm, gtmB), (sbon, sbonB) = sh["Vt"], sh["Bh"], sh["Kh"], sh["gtm"], sh["sbon"]
        ARv = AR[:].rearrange("p (c t) -> p c t", c=4)
        hb, hbB = M.load_h_tile(tt)
        for gi, (c0, c1, z, zB) in enumerate(((0, 64, zr, zrB), (64, 128, zk, zkB), (128, 192, zv, zvB),
                                               (192, 320, zwa, zwaB), (320, 448, zg, zgB))):
            n = c1 - c0
            for kc in range(KC):
                _mm(P, ps[gi][0:n, :], W[:, kc, c0:c1], hb[:, kc, :], kc == 0, kc == KC - 1, [WB, hbB], [psB[gi]], kc == KC - 1)
            P.op("act", lambda e: e.activation(out=z[:, 1:TT + 1], in_=ps[gi][0:n, :], func=AF.Copy), reads=[psB[gi]], writes=[zB])
        for (z, zB, o, oB, col, n) in ((zr, zrB, rm, rmB, 0, 64), (zk, zkB, km, kmB, 1, 64), (zv, zvB, vm, vmB, 2, 64),
                                       (zwa, zwaB, wam, wamB, 8, 128), (zg, zgB, gm, gmB, 9, 128)):
            P.op("pool", lambda e: e.tensor_scalar(out=tmp[0:n, :], in0=z[:, 1:TT + 1], scalar1=dv[0:n, col:col + 1], scalar2=0.0,
                                                   op0=ALU.mult, op1=ALU.add), reads=[zB, cB], writes=[tmpB])
            P.op("dve", lambda e: e.scalar_tensor_tensor(out=o[:], in0=z[:, 0:TT], scalar=rv[0:n, col:col + 1], in1=tmp[0:n, :],
                                                         op0=ALU.mult, op1=ALU.add), reads=[zB, cB, tmpB], writes=[oB])
            P.op("dve", lambda e: e.tensor_copy(out=z[:, 0:1], in_=z[:, TT:TT + 1]), reads=[zB], writes=[zB])
        P.op("act", lambda e: e.activation(out=twl[0:64, :], in_=wam[0:64, :], func=AF.Tanh), reads=[wamB], writes=[twlB])
        P.op("pool", lambda e: e.tensor_copy(out=twl[64:128, :], in_=wam[64:128, :]), reads=[wamB], writes=[twlB])
        _mm(P, ps[0][0:64, :], rm16[0:64, 0:64], twl[0:64, :], True, True, [cB, twlB], [psB[0]], True)
        _mm(P, ps[1][0:64, :], rm16[64:128, 64:128], twl[64:128, :], True, True, [cB, twlB], [psB[1]], True)
        P.op("act", lambda e: e.activation(out=lw[:], in_=ps[0][0:64, :], func=AF.Tanh, scale=0.5, bias=dv[0:64, 10:11]),
             reads=[psB[0], cB], writes=[lwB])
        P.op("dve", lambda e: e.tensor_scalar(out=lw[:], in0=lw[:], scalar1=-0.3032653298563167, scalar2=-0.3032653298563167,
                                              op0=ALU.mult, op1=ALU.add), reads=[lwB], writes=[lwB])
        P.op("act", lambda e: e.activation(out=ta[:], in_=ps[1][0:64, :], func=AF.Tanh, scale=0.5, bias=dv[0:64, 11:12]),
             reads=[psB[1], cB], writes=[taB])
        P.op("act", lambda e: e.activation(out=tmp[:], in_=gm[:], func=AF.Tanh, scale=0.5), reads=[gmB], writes=[tmpB])
        P.op("dve", lambda e: e.tensor_scalar(out=sg16[:], in0=tmp[:], scalar1=0.5, scalar2=0.5, op0=ALU.mult, op1=ALU.add),
             reads=[tmpB], writes=[sg16B])
        for c in range(4):
            _mm(P, ps[2][:, c * 64:(c + 1) * 64], sg16[:, c * 128:(c + 1) * 128], rm16[:, 128:192], True, True,
                [sg16B, cB], [psB[2]], c == 3)
        P.op("act", lambda e: e.activation(out=gtm[:], in_=ps[2][:, 0:256], func=AF.Copy), reads=[psB[2]], writes=[gtmB])
        P.op("pool", lambda e: e.tensor_scalar(out=kk[:], in0=km[:], scalar1=rv[0:64, 5:6], scalar2=0.0, op0=ALU.mult, op1=ALU.add),
             reads=[kmB, cB], writes=[kkB])
        P.op("pool", lambda e: e.tensor_tensor(out=tmp[0:64, :], in0=kk[:], in1=kk[:], op=ALU.mult), reads=[kkB], writes=[tmpB])
        _mm(P, ps[3][0:64, :], ones64[0:64, :], tmp[0:64, :], True, True, [cB, tmpB], [psB[3]], True)
        P.op("act", lambda e: e.activation(out=tmp[0:64, :], in_=ps[3][0:64, :], func=AF.Sqrt), reads=[psB[3]], writes=[tmpB])
        P.op("dve", lambda e: e.tensor_scalar(out=tmp[0:64, :], in0=tmp[0:64, :], scalar1=1e-12, scalar2=None, op0=ALU.max),
             reads=[tmpB], writes=[tmpB])
        P.op("dve", lambda e: e.reciprocal(out=tmp[0:64, :], in_=tmp[0:64, :]), reads=[tmpB], writes=[tmpB])
        P.op("dve", lambda e: e.tensor_tensor(out=kk[:], in0=kk[:], in1=tmp[0:64, :], op=ALU.mult), reads=[kkB, tmpB], writes=[kkB])
        P.op("pool", lambda e: e.tensor_scalar(out=kf[:], in0=ta[:], scalar1=dv[0:64, 12:13], scalar2=dv[0:64, 13:14],
                                               op0=ALU.mult, op1=ALU.add), reads=[taB, cB], writes=[kfB])
        P.op("pool", lambda e: e.tensor_tensor(out=kf[:], in0=kf[:], in1=km[:], op=ALU.mult), reads=[kfB, kmB], writes=[kfB])
        P.op("pool", lambda e: e.tensor_scalar(out=bb[:], in0=ta[:], scalar1=0.5, scalar2=0.5, op0=ALU.mult, op1=ALU.add),
             reads=[taB], writes=[bbB])
        P.op("pool", lambda e: e.tensor_tensor(out=bb[:], in0=bb[:], in1=kk[:], op=ALU.mult), reads=[bbB, kkB], writes=[bbB])
        P.op("dve", lambda e: e.tensor_tensor_scan(out=cw[:], data0=cmask[0:64, :], data1=lw[:], initial=0.0,
                                                   op0=ALU.mult, op1=ALU.add), reads=[lwB, cB], writes=[cwB])
        P.op("pool", lambda e: e.tensor_tensor(out=cwx[:], in0=cw[:], in1=lw[:], op=ALU.subtract), reads=[cwB, lwB], writes=[cwxB])
        P.op("dve", lambda e: e.tensor_tensor(out=v3(dd[:], 4), in0=v3(cw[:], 4)[:, :, 127:128].to_broadcast([64, 4, 128]),
                                              in1=v3(cw[:], 4), op=ALU.subtract), reads=[cwB], writes=[ddB])
        P.op("act", lambda e: e.activation(out=E1[:], in_=cw[:], func=AF.Exp), reads=[cwB], writes=[E1B])
        P.op("act", lambda e: e.activation(out=E2[:], in_=cw[:], func=AF.Exp, scale=-1.0), reads=[cwB], writes=[E2B])
        P.op("act", lambda e: e.activation(out=E3[:], in_=cwx[:], func=AF.Exp), reads=[cwxB], writes=[E3B])
        P.op("act", lambda e: e.activation(out=E4[:], in_=dd[:], func=AF.Exp), reads=[ddB], writes=[E4B])
        P.op("dve", lambda e: e.scalar_tensor_tensor(out=ARv[:, :, 0:128], in0=v3(kk[:], 4), scalar=-1.0, in1=v3(E3[:], 4),
                                                     op0=ALU.mult, op1=ALU.mult), reads=[kkB, E3B], writes=[ARB])
        P.op("pool", lambda e: e.tensor_tensor(out=ARv[:, :, 128:256], in0=v3(rm[:], 4), in1=v3(E1[:], 4), op=ALU.mult),
             reads=[rmB, E1B], writes=[ARB])
        P.op("pool", lambda e: e.tensor_tensor(out=bT[:], in0=bb[:], in1=E2[:], op=ALU.mult), reads=[bbB, E2B], writes=[bTB])
        P.op("pool", lambda e: e.tensor_tensor(out=kT[:], in0=kf[:], in1=E2[:], op=ALU.mult), reads=[kfB, E2B], writes=[kTB])
        P.op("pool", lambda e: e.tensor_tensor(out=BhT[:], in0=bb[:], in1=E4[:], op=ALU.mult), reads=[bbB, E4B], writes=[BhTB])
        P.op("pool", lambda e: e.tensor_tensor(out=KhT[:], in0=kf[:], in1=E4[:], op=ALU.mult), reads=[kfB, E4B], writes=[KhTB])
        P.op("dve", lambda e: e.scalar_tensor_tensor(out=prod[:], in0=rm[:], scalar=rv[0:64, 7:8], in1=kf[:],
                                                     op0=ALU.mult, op1=ALU.mult), reads=[rmB, cB, kfB], writes=[prodB])
        for c in range(4):
            _mm(P, ps[4][:, c:c + 1], prod[:, c * 128:(c + 1) * 128], ones64[0:64, 0:1], True, True, [prodB, cB], [psB[4]], c == 3)
        P.op("act", lambda e: e.activation(out=sbon[:], in_=ps[4][:, 0:4], func=AF.Copy), reads=[psB[4]], writes=[sbonB])
        for (src, srcB, dst, dstB, pi) in ((vm, vmB, Vt, VtB, 5), (BhT, BhTB, Bh, BhB, 6), (KhT, KhTB, Kh, KhB, 7)):
            for c in range(4):
                _mm(P, ps[pi][:, c * 64:(c + 1) * 64], src[:, c * 128:(c + 1) * 128], ident16[0:64, 0:64], True, True,
                    [srcB, cB], [psB[pi]], c == 3)
            P.op("act" if pi != 6 else "dve", (lambda e: e.activation(out=dst[:], in_=ps[pi][:, 0:256], func=AF.Copy)) if pi != 6 else
                 (lambda e: e.tensor_copy(out=dst[:], in_=ps[pi][:, 0:256])), reads=[psB[pi]], writes=[dstB])
    def matrix(tt):
        sl = slice(tt * TT, (tt + 1) * TT)
        ps = [M.ps[2 + (i % 6)] for i in range(8)]
        psB = [M.psB[2 + (i % 6)] for i in range(8)]
        big = M.psall[:, 2 * TT:6 * TT]
        tcur = tstate[0]
        sh = SH[tt % 2]
        (E1, E1B), (AR, ARB), (bT, bTB), (kT, kTB) = sh["E1"], sh["AR"], sh["bT"], sh["kT"]
        (Vt, VtB), (Bh, BhB), (Kh, KhB), (gtm, gtmB), (sbon, sbonB) = sh["Vt"], sh["Bh"], sh["Kh"], sh["gtm"], sh["sbon"]
        ARv = AR[:].rearrange("p (c t) -> p c t", c=4)
        for c in range(4):
            _mm(P, big[:, c * 256:(c + 1) * 256], bT[:, c * 128:(c + 1) * 128], ARv[:, c, :], True, True,
                [bTB, ARB], [psB[0], psB[1]], c == 3)
        for c in range(4):
            _mm(P, big[:, 1024 + c * 256:1024 + (c + 1) * 256], kT[:, c * 128:(c + 1) * 128], ARv[:, c, :], True, True,
                [kTB, ARB], [psB[2], psB[3]], c == 3)
        for c in range(4):
            _mm(P, ps[4][:, c * 128:(c + 1) * 128], ARv[:, c, 0:128], bT[:, c * 128:(c + 1) * 128], True, True,
                [ARB, bTB], [psB[4]], c == 3)
        a1 = big[:, 0:1024].rearrange("p (c t) -> p c t", c=4)
        a2 = big[:, 1024:2048].rearrange("p (c t) -> p c t", c=4)
        X0, X0B = X[0]
        XT0, XT0B = XT[0]
        P.op("dve", lambda e: e.tensor_tensor(out=v3(X0[:], 4), in0=a1[:, :, 0:128], in1=m_su, op=ALU.mult),
             reads=[psB[0], psB[1], cB], writes=[X0B])
        P.op("dve", lambda e: e.tensor_tensor(out=v3(Yrb[:], 4), in0=a1[:, :, 128:256], in1=m_iu, op=ALU.mult),
             reads=[psB[0], psB[1], cB], writes=[YrbB])
        P.op("dve", lambda e: e.tensor_tensor(out=v3(Xak[:], 4), in0=a2[:, :, 0:128], in1=m_su, op=ALU.mult),
             reads=[psB[2], psB[3], cB], writes=[XakB])
        P.op("dve", lambda e: e.tensor_tensor(out=v3(Yrk[:], 4), in0=a2[:, :, 128:256], in1=m_iu, op=ALU.mult),
             reads=[psB[2], psB[3], cB], writes=[YrkB])
        P.op("dve", lambda e: e.tensor_tensor(out=v3(XT0[:], 4), in0=v3(ps[4], 4), in1=m_sl, op=ALU.mult),
             reads=[psB[4], cB], writes=[XT0B])
        for c in range(4):
            _mm(P, ps[5][:, c * 128:c * 128 + 64], Xak[:, c * 128:(c + 1) * 128], Vt[:, c * 64:(c + 1) * 64], True, True,
                [XakB, VtB], [psB[5]], False)
            _mm(P, ps[5][:, c * 128 + 64:(c + 1) * 128], ARv[:, c, 0:128], ident16[0:64, 0:64], True, True,
                [ARB, cB], [psB[5]], c == 3)
        Zc, ZcB = Z[0]
        P.op("act", lambda e: e.activation(out=Zc[:], in_=ps[5], func=AF.Copy), reads=[psB[5]], writes=[ZcB])
        X0, X0B = X[0]
        XT0, XT0B = XT[0]
        bank = [0]

        def mm4(lhs, lhsB, rhs, rhsB):
            bi = bank[0] % 8
            bank[0] += 1
            for c in range(4):
                _mm(P, ps[bi][:, c * 128:(c + 1) * 128], lhs[:, c * 128:(c + 1) * 128], rhs[:, c * 128:(c + 1) * 128],
                    True, True, [lhsB, rhsB], [psB[bi]], c == 3)
            return ps[bi], psB[bi]

        def evac(eng, dst, dstB, src, srcB):
            if eng == "act":
                P.op("act", lambda e: e.activation(out=dst[:], in_=src, func=AF.Copy), reads=[srcB], writes=[dstB])
            else:
                P.op(eng, lambda e: e.tensor_copy(out=dst[:], in_=src), reads=[srcB], writes=[dstB])

        def addto(dst, dstB, psrc, psrcB, other, otherB):
            P.op("dve", lambda e: e.tensor_tensor(out=dst[:], in0=psrc, in1=other[:], op=ALU.add),
                 reads=[psrcB, otherB], writes=[dstB])

        def masked(dst, dstB, src, srcB, mk):
            P.op("pool", lambda e: e.tensor_tensor(out=dst[:], in0=src[:], in1=mk, op=ALU.mult), reads=[srcB, cB], writes=[dstB])

        S, SB = iv["S0"]
        ST, STB = iv["S0T"]
        masked(S, SB, X0, X0B, m_d16)
        masked(ST, STB, XT0, XT0B, m_d16)
        J, JB = iv["J"]
        JT, JTB = iv["JT"]
        P.op("dve", lambda e: e.tensor_tensor(out=J[:], in0=S[:], in1=identx4, op=ALU.add), reads=[SB, cB], writes=[JB])
        P.op("dve", lambda e: e.tensor_tensor(out=JT[:], in0=ST[:], in1=identx4, op=ALU.add), reads=[STB, cB], writes=[JTB])
        F = FT = None
        for lev in range(3):
            Sn, SnB = iv["S%d" % (lev + 1)]
            STn, STnB = iv["S%dT" % (lev + 1)]
            p1, p1B = mm4(ST, STB, S, SB)
            p2, p2B = mm4(S, SB, ST, STB)
            evac("act", Sn, SnB, p1, p1B)
            evac("dve", STn, STnB, p2, p2B)
            if lev == 0:
                P.op("dve", lambda e: e.tensor_tensor(out=J[:], in0=J[:], in1=Sn[:], op=ALU.add), reads=[JB, SnB], writes=[JB])
                P.op("dve", lambda e: e.tensor_tensor(out=JT[:], in0=JT[:], in1=STn[:], op=ALU.add), reads=[JTB, STnB], writes=[JTB])
                q1, q1B = mm4(ST, STB, Sn, SnB)
                q2, q2B = mm4(S, SB, STn, STnB)
                F, FB = iv["Fa"]
                FT, FTB = iv["FaT"]
                addto(F, FB, q1, q1B, J, JB)
                addto(FT, FTB, q2, q2B, JT, JTB)
            else:
                q1, q1B = mm4(FT, FTB, Sn, SnB)
                q2, q2B = mm4(F, FB, STn, STnB)
                Fn, FnB = iv["Fb" if lev == 1 else "Fa"]
                FTn, FTnB = iv["FbT" if lev == 1 else "FaT"]
                addto(Fn, FnB, q1, q1B, F, FB)
                addto(FTn, FTnB, q2, q2B, FT, FTB)
                F, FB, FT, FTB = Fn, FnB, FTn, FTnB
            S, SB, ST, STB = Sn, SnB, STn, STnB
        Tk, TkB, Nk, NkB = F, FB, FT, FTB
        for li, mk in enumerate((m_o16, m_o32, m_o64)):
            U, UB = iv["U"]
            Lm, LB = iv["L"]
            masked(Lm, LB, XT0, XT0B, mk)
            last = li == 2
            if not last:
                masked(U, UB, X0, X0B, mk)
            w1, w1B = mm4(Lm, LB, Tk, TkB)
            W1, W1B = iv["W1"]
            evac("act", W1, W1B, w1, w1B)
            if not last:
                v1, v1B = mm4(U, UB, Nk, NkB)
                V1, V1B = iv["V1"]
                evac("dve", V1, V1B, v1, v1B)
            t2, t2B = mm4(Nk, NkB, W1, W1B)
            Tn, TnB = iv["Ta" if li % 2 == 0 else "Tb"]
            addto(Tn, TnB, t2, t2B, Tk, TkB)
            if not last:
                n2, n2B = mm4(Tk, TkB, V1, V1B)
                Nn, NnB = iv["Na" if li % 2 == 0 else "Nb"]
                addto(Nn, NnB, n2, n2B, Nk, NkB)
                Nk, NkB = Nn, NnB
            Tk, TkB = Tn, TnB
        zp, zpB = mm4(Tk, TkB, Z[0][0], Z[0][1])
        Zf, ZfB = Z[1]
        evac("act", Zf, ZfB, zp, zpB)
        Zv = v3(Zf[:], 4)
        for c in range(4):
            _mm(P, ps[0][0:64, c * 128:(c + 1) * 128], Zv[:, c, 64:128], Yrb[:, c * 128:(c + 1) * 128], True, True,
                [ZfB, YrbB], [psB[0]], c == 3)
        P.op("dve", lambda e: e.tensor_tensor(out=v3(RpT[:], 4), in0=v3(ps[0][0:64, :], 4), in1=ARv[:, :, 128:256], op=ALU.add),
             reads=[psB[0], ARB], writes=[RpTB])
        for c in range(4):
            _mm(P, ps[1][:, c * 64:(c + 1) * 64], Yrb[:, c * 128:(c + 1) * 128], Zv[:, c, 0:64], True, False,
                [YrbB, ZfB], [psB[1]], False)
            _mm(P, ps[1][:, c * 64:(c + 1) * 64], Yrk[:, c * 128:(c + 1) * 128], Vt[:, c * 64:(c + 1) * 64], False, True,
                [YrkB, VtB], [psB[1]], c == 3)
        P.op("act", lambda e: e.activation(out=Yl[:], in_=ps[1][:, 0:256], func=AF.Copy), reads=[psB[1]], writes=[YlB])
        for c in range(4):
            _mm(P, ps[2][0:64, c * 64:(c + 1) * 64], Zv[:, c, 64:128], Bh[:, c * 64:(c + 1) * 64], True, True,
                [ZfB, BhB], [psB[2]], c == 3)
        P.op("act", lambda e: e.activation(out=MT[:], in_=ps[2][0:64, 0:256], func=AF.Copy), reads=[psB[2]], writes=[MTB])
        for c in range(4):
            _mm(P, ps[3][0:64, c * 64:(c + 1) * 64], Kh[:, c * 64:(c + 1) * 64], Vt[:, c * 64:(c + 1) * 64], True, False,
                [KhB, VtB], [psB[3]], False)
            _mm(P, ps[3][0:64, c * 64:(c + 1) * 64], Bh[:, c * 64:(c + 1) * 64], Zv[:, c, 0:64], False, True,
                [BhB, ZfB], [psB[3]], c == 3)
        P.op("dve", lambda e: e.tensor_copy(out=Psi[:], in_=ps[3][0:64, 0:256]), reads=[psB[3]], writes=[PsiB])
        for c in range(4):
            Tc, TcB = Tst[tcur]
            Tn, TnB = Tst[1 - tcur]
            _mm(P, ps[4][:, c * 64:(c + 1) * 64], RpT[:, c * 128:(c + 1) * 128], Tc[:], True, True, [RpTB, TcB], [psB[4]], True)
            _mm(P, ps[5][0:64, c * 64:(c + 1) * 64], MT[:, c * 64:(c + 1) * 64], Tc[:], True, True, [MTB, TcB], [psB[5]], True)
            wc = E1[:, c * 128 + 127:c * 128 + 128]
            P.op("dve", lambda e: e.scalar_tensor_tensor(out=Tn[:], in0=Tc[:], scalar=wc, in1=Psi[:, c * 64:(c + 1) * 64],
                                                         op0=ALU.mult, op1=ALU.add), reads=[TcB, E1B, PsiB], writes=[TnB])
            P.op("dve", lambda e: e.tensor_tensor(out=Tn[:], in0=ps[5][0:64, c * 64:(c + 1) * 64], in1=Tn[:], op=ALU.add),
                 reads=[psB[5], TnB], writes=[TnB])
            tcur = 1 - tcur
        tstate[0] = tcur
        P.op("dve", lambda e: e.tensor_tensor(out=Yo[:], in0=ps[4][:, 0:256], in1=Yl[:], op=ALU.add), reads=[psB[4], YlB], writes=[YoB])
        Yov = v3(Yo[:], 4)
        for c in range(4):
            P.op("dve", lambda e: e.bn_stats(out=st6[:, c * 6:(c + 1) * 6], in_=Yov[:, c, :]), reads=[YoB], writes=[st6B])
        for c in range(4):
            P.op("dve", lambda e: e.bn_aggr(out=mv[:, c * 2:(c + 1) * 2], in_=st6[:, c * 6:(c + 1) * 6]), reads=[st6B], writes=[mvB])
        mvv = v3(mv[:], 4)
        P.op("dve", lambda e: e.tensor_scalar(out=rs[:], in0=mvv[:, :, 1], scalar1=64e-5, scalar2=None, op0=ALU.add),
             reads=[mvB], writes=[rsB])
        P.op("act", lambda e: e.activation(out=rs[:], in_=rs[:], func=AF.Sqrt), reads=[rsB], writes=[rsB])
        P.op("dve", lambda e: e.reciprocal(out=rs[:], in_=rs[:]), reads=[rsB], writes=[rsB])
        for c in range(4):
            P.op("dve", lambda e: e.tensor_scalar(out=Yov[:, c, :], in0=Yov[:, c, :], scalar1=mv[:, 2 * c:2 * c + 1],
                                                  scalar2=rs[:, c:c + 1], op0=ALU.subtract, op1=ALU.mult),
                 reads=[YoB, mvB, rsB], writes=[YoB])
        P.op("pool", lambda e: e.tensor_tensor(out=Yo[:], in0=Yo[:], in1=rgn[:, 0:256], op=ALU.mult), reads=[YoB, cB], writes=[YoB])
        P.op("pool", lambda e: e.tensor_tensor(out=Yo[:], in0=Yo[:], in1=rgn[:, 256:512], op=ALU.add), reads=[YoB, cB], writes=[YoB])
        for c in range(4):
            P.op("dve", lambda e: e.scalar_tensor_tensor(out=Yov[:, c, :], in0=Vt[:, c * 64:(c + 1) * 64], scalar=sbon[:, c:c + 1],
                                                         in1=Yov[:, c, :], op0=ALU.mult, op1=ALU.add),
                 reads=[VtB, sbonB, YoB], writes=[YoB])
        P.op("dve", lambda e: e.tensor_tensor(out=Yo16[:], in0=Yo[:], in1=gtm[:], op=ALU.mult), reads=[YoB, gtmB], writes=[Yo16B])
        Yo16v = v3(Yo16[:], 4)
        for c in range(4):
            _mm(P, ps[6][0:64, c * 128:(c + 1) * 128], Yo16v[:, c, :], ident16[:], True, True, [Yo16B, cB], [psB[6]], c == 3)
        o, oB = oT[tt % 2]
        P.op("act", lambda e: e.activation(out=o[:], in_=ps[6][0:64, :], func=AF.Copy), reads=[psB[6]], writes=[oB])
        P.dma("sp", rwT_out_d[:, sl], o[:], reads=[oB], is_output=True)

    prep(0)
    for tt in range(NQT):
        if tt + 1 < NQT:
            P.interleave(lambda: matrix(tt), lambda: prep(tt + 1), 2, 1)
        else:
            matrix(tt)


def build_mix_prog(parts=("att", "lru", "rwkv")):
    nc = bass.Bass("TRN2", target_bir_lowering=False)
    dt = lambda name, shape, dty, kind="ExternalInput": nc.dram_tensor(name, shape, dty, kind=kind).ap()
    hT_d = dt("hT", [D, SEQ], BF16)
    with ExitStack() as es:
        P = Prog(nc, es)
        M = MixCtx(P, nc, es, hT_d)
        if "att" in parts and "lru" in parts:
            w_att_d = dt("w_att", [D, 193], F32)
            avec_d = dt("avec", [128, 4], F32)
            cmask_d = dt("cmask", [128, 4 * TT + 128], BF16)
            att_o = dt("attT", [64, SEQ], BF16, "ExternalOutput")
            w_lru_d = dt("w_lru", [D, 256], F32)
            gab_d = dt("gab", [128, 256], F32)
            lvec_d = dt("lvec", [128, 8], F32)
            lru_o = dt("lruT", [128, SEQ], BF16, "ExternalOutput")
            with ExitStack() as es2:
                es3 = es2
                M2 = MixCtx(P, nc, es3, hT_d, parent=M, tag="l_")
                Ma = MixCtx(P, nc, es3, hT_d, parent=M, tag="a_")
                P.interleave(lambda: emit_attention(Ma, es2, w_att_d, avec_d, cmask_d, att_o, npb=2),
                             lambda: emit_lru(M2, es3, w_lru_d, gab_d, lvec_d, lru_o, banks=(4, 5)), 6, 1)
                P.barrier()
        elif "att" in parts:
            w_att_d = dt("w_att", [D, 193], F32)
            avec_d = dt("avec", [128, 4], F32)
            cmask_d = dt("cmask", [128, 4 * TT + 128], BF16)
            att_o = dt("attT", [64, SEQ], BF16, "ExternalOutput")
            with ExitStack() as es2:
                emit_attention(M, es2, w_att_d, avec_d, cmask_d, att_o)
                P.barrier()
        elif "lru" in parts:
            w_lru_d = dt("w_lru", [D, 256], F32)
            gab_d = dt("gab", [128, 256], F32)
            lvec_d = dt("lvec", [128, 8], F32)
            lru_o = dt("lruT", [128, SEQ], BF16, "ExternalOutput")
            with ExitStack() as es2:
                emit_lru(M, es2, w_lru_d, gab_d, lvec_d, lru_o)
                P.barrier()
        if "rwkv" in parts:
            w_rw_d = dt("w_rw", [D, 448], F32)
            rvec_d = dt("rvec", [128, 16], F32)
            rmat_d = dt("rmat", [128, 192], F32)
            rgn_d = dt("rgn", [128, 512], F32)
            rconst_d = dt("rconst", [128, 5120], F32)
            rw_o = dt("rwkvT", [64, SEQ], BF16, "ExternalOutput")
            with ExitStack() as es2:
                emit_rwkv(M, es2, w_rw_d, rvec_d, rmat_d, rgn_d, rconst_d, rw_o)
                P.barrier()
        P.finish()
    return nc


ATT_COLS = 1544
LRU_BASE = 1544
RWKV_BASE = 3592
GATE_BASE = 5384
_BF = ml_dtypes.bfloat16


def _cmask_const():
    p = np.arange(128)[:, None, None]
    a = np.arange(4)[None, :, None]
    c = np.arange(TT)[None, None, :]
    m = np.where(c - p - 128 * a >= 0, 0.0, MASKNEG).astype(np.float32).reshape(128, 4 * TT)
    return np.concatenate([m, np.eye(128, dtype=np.float32)], axis=1).astype(_BF)


def mix_inputs(inp, l, hT_full, parts=("att", "lru", "rwkv")):
    w_in = inp["w_in"][l]
    cm = _cmask_const()
    maps = []
    for j in range(NCORES):
        m = {"hT": hT_full}
        if "att" in parts:
            m["w_att"] = np.ascontiguousarray(np.concatenate(
                [w_in[:, 64 * j:64 * j + 64], w_in[:, 512 + 64 * j:512 + 64 * j + 64],
                 w_in[:, 1536 + j:1537 + j], w_in[:, 1024 + 64 * j:1024 + 64 * j + 64]], axis=1))
            av = np.zeros((128, 4), np.float32)
            av[:, 0] = inp["fox_f_bias"][l][j]
            m["avec"] = av
            m["cmask"] = cm
        if "lru" in parts:
            c0 = LRU_BASE + 128 * j
            m["w_lru"] = np.ascontiguousarray(np.concatenate([w_in[:, c0:c0 + 128], w_in[:, c0 + 1024:c0 + 1152]], axis=1))
            gab = np.zeros((128, 256), np.float32)
            for b in range(2):
                gab[64 * b:64 * b + 64, 64 * b:64 * b + 64] = inp["lru_ga_w"][l][2 * j + b]
                gab[64 * b:64 * b + 64, 128 + 64 * b:128 + 64 * b + 64] = inp["lru_gx_w"][l][2 * j + b]
            m["gab"] = gab
            ch = slice(128 * j, 128 * j + 128)
            m["lvec"] = np.ascontiguousarray(np.stack(
                [inp["lru_conv_w"][l][k][ch] for k in range(4)] +
                [inp["lru_conv_b"][l][ch], inp["lru_ga_b"][l][ch], inp["lru_gx_b"][l][ch], inp["lru_lambda"][l][ch]],
                axis=1).astype(np.float32))
        if "rwkv" in parts:
            m.update(rwkv_inputs(inp, l, j))
        maps.append(m)
    return maps


def _rconst():
    r = np.arange(128)[:, None]
    c = np.arange(128)[None, :]
    ident = np.eye(128, dtype=np.float32)
    su = (r < c).astype(np.float32)
    iu = (r <= c).astype(np.float32)
    sl = (c < r).astype(np.float32)
    cm = np.ones((128, 512), np.float32)
    cm[:, ::128] = 0.0
    first = np.concatenate([ident, np.zeros((128, 384), np.float32)], axis=1)
    blk = lambda k: (r // k) == (c // k)
    d16 = blk(16).astype(np.float32)
    off = lambda k: (blk(2 * k) & ~blk(k)).astype(np.float32)
    t4 = lambda m: np.tile(m, (1, 4))
    return np.ascontiguousarray(np.concatenate([first, t4(su), t4(iu), t4(sl), cm, t4(d16), t4(off(16)), t4(off(32)), t4(off(64)),
                                                t4(ident)], axis=1))


def rwkv_inputs(inp, l, j):
    w_in = inp["w_in"][l]
    b = RWKV_BASE
    hs = slice(64 * j, 64 * j + 64)
    cols = [w_in[:, b + 64 * j:b + 64 * j + 64], w_in[:, b + 512 + 64 * j:b + 512 + 64 * j + 64],
            w_in[:, b + 1024 + 64 * j:b + 1024 + 64 * j + 64], w_in[:, b + 1536:b + 1792]]
    mu = inp["rwkv_mu"][l]
    rvec = np.zeros((128, 16), np.float32)
    rvec[0:64, 0] = mu[64 * j:64 * j + 64]
    rvec[0:64, 1] = mu[512 + 64 * j:512 + 64 * j + 64]
    rvec[0:64, 2] = mu[1024 + 64 * j:1024 + 64 * j + 64]
    rvec[0:64, 3] = inp["rwkv_w0"][l][hs]
    rvec[0:64, 4] = inp["rwkv_a0"][l][hs]
    rvec[0:64, 5] = inp["rwkv_k_k"][l][hs]
    rvec[0:64, 6] = inp["rwkv_k_a"][l][hs]
    rvec[0:64, 7] = inp["rwkv_r_k"][l][j]
    rvec[:, 8] = mu[1536:1664]
    rvec[:, 9] = mu[1664:1792]
    rmat = np.zeros((128, 192), np.float32)
    rmat[0:64, 0:64] = inp["rwkv_w2"][l][:, hs]
    rmat[64:128, 64:128] = inp["rwkv_a2"][l][:, hs]
    rmat[:, 128:192] = inp["rwkv_g2"][l][:, hs]
    rgn = np.concatenate([np.tile(inp["rwkv_gn_w"][l][hs][None, :], (128, 4)),
                          np.tile(inp["rwkv_gn_b"][l][hs][None, :], (128, 4))], axis=1).astype(np.float32)
    return {"w_rw": np.ascontiguousarray(np.concatenate(cols, axis=1)), "rvec": rvec, "rmat": rmat,
            "rgn": np.ascontiguousarray(rgn), "rconst": _rconst()}


_PROGS = {}
_DBG = None


def _prog(name, builder):
    if name not in _PROGS:
        _PROGS[name] = builder()
    return _PROGS[name]


def _run(nc, maps):
    return run_bass_kernel_spmd(nc, maps, core_ids=list(range(NCORES))).results


def _pcol(v):
    return np.ascontiguousarray(np.asarray(v, np.float32).reshape(KC, 128).T)


def _vec(gate, lng, lnb, sh, sc):
    return np.ascontiguousarray(np.concatenate([_pcol(v) for v in (gate, lng, lnb, sh, sc)], axis=1))


def kernel(**inp):
    inp = {k: np.asarray(v) for k, v in inp.items()}
    x = inp["x"][0]
    c128 = _pcol(inp["c"][0])
    maps = []
    for c in range(NCORES):
        cs = slice(c * 1152, (c + 1) * 1152)
        bb = np.stack([inp["ada_b"][l][cs].reshape(9, 128).T for l in range(DEPTH)], axis=1).reshape(128, DEPTH * 9)
        maps.append({"c": c128, "ada_w": np.ascontiguousarray(inp["ada_w"][:, :, cs]), "ada_b": np.ascontiguousarray(bb.astype(np.float32))})
    res = _run(_prog("ada", build_ada_prog), maps)
    ada = np.zeros((DEPTH, 9 * D), np.float32)
    for c in range(NCORES):
        o = res[c]["ada_out"].reshape(128, DEPTH, 9)
        for l in range(DEPTH):
            ada[l, c * 1152:(c + 1) * 1152] = o[:, l, :].T.reshape(-1)
    adas = [np.split(ada[l], 9) for l in range(DEPTH)]
    if _DBG:
        _DBG("ada", 0, ada)
    zeros = np.zeros(D, np.float32)
    xT = [np.ascontiguousarray(x[c * TPC:(c + 1) * TPC].T) for c in range(NCORES)]
    v0 = _vec(zeros, zeros, zeros, adas[0][0], adas[0][1])
    res = _run(_prog("mod", build_mod_prog), [{"xT": xT[c], "vec": v0} for c in range(NCORES)])
    hT = [r["hT_out"] for r in res]
    if _DBG:
        _DBG("h1", 0, hT)
    for l in range(DEPTH):
        sh1, sc1, g1, sh2, sc2, g2, sh3, sc3, g3 = adas[l]
        v = _vec(g1, inp["ln_g"][l, 0], inp["ln_b"][l, 0], sh2, sc2)
        res = _run(_prog("ffn", build_ffn_prog), [{"xT": xT[c], "hT": hT[c], "w_up": inp["ffn_up"][l, 0],
                                                   "w_down": inp["ffn_down"][l, 0], "vec": v} for c in range(NCORES)])
        xT = [r["xT_out"] for r in res]
        hT = [r["hT_out"] for r in res]
        if _DBG:
            _DBG("x1", l, xT)
            _DBG("h2", l, hT)
        hfull = np.ascontiguousarray(np.concatenate(hT, axis=1))
        res = _run(_prog("mix", build_mix_prog), mix_inputs(inp, l, hfull))
        att = np.concatenate([r["attT"] for r in res], axis=0)
        lru = np.concatenate([r["lruT"] for r in res], axis=0)
        rwk = np.concatenate([r["rwkvT"] for r in res], axis=0)
        br = np.concatenate([att, lru, rwk], axis=0)
        if _DBG:
            _DBG("att", l, att)
            _DBG("lru", l, lru)
            _DBG("rwkv", l, rwk)
        wg = np.ascontiguousarray(inp["w_in"][l][:, GATE_BASE:GATE_BASE + 3072])
        wp = np.ascontiguousarray(np.concatenate([inp["w_proj_a"][l], inp["w_proj_b"][l], inp["w_proj_c"][l]], axis=0))
        v = _vec(g2, inp["ln_g"][l, 1], inp["ln_b"][l, 1], sh3, sc3)
        res = _run(_prog("mixpost", build_mixpost_prog),
                   [{"xT": xT[c], "hT": hT[c], "brT": np.ascontiguousarray(br[:, c * TPC:(c + 1) * TPC]), "wg": wg, "wp": wp,
                     "wo": inp["w_out"][l], "vec": v} for c in range(NCORES)])
        xT = [r["xT_out"] for r in res]
        hT = [r["hT_out"] for r in res]
        if _DBG:
            _DBG("x2", l, xT)
            _DBG("h3", l, hT)
        nsh, nsc = (adas[l + 1][0], adas[l + 1][1]) if l + 1 < DEPTH else (zeros, zeros)
        v = _vec(g3, inp["ln_g"][l, 2], inp["ln_b"][l, 2], nsh, nsc)
        res = _run(_prog("ffn", build_ffn_prog), [{"xT": xT[c], "hT": hT[c], "w_up": inp["ffn_up"][l, 1],
                                                   "w_down": inp["ffn_down"][l, 1], "vec": v} for c in range(NCORES)])
        xT = [r["xT_out"] for r in res]
        hT = [r["hT_out"] for r in res]
        if _DBG:
            _DBG("x3", l, xT)
    out = np.concatenate([t.T for t in xT], axis=0)[None].astype(np.float32)
    return out
```

```python
import numpy as np
import ml_dtypes
from contextlib import ExitStack
import threading

import concourse.bass as bass
import concourse.mybir as mybir
from concourse.bass_utils import run_bass_kernel_spmd

F32 = mybir.dt.float32
BF16 = mybir.dt.bfloat16
AF = mybir.ActivationFunctionType
ALU = mybir.AluOpType
AX = mybir.AxisListType

NCORES = 8
D = 1024
SEQ = 16384
DEPTH = 4
TPC = SEQ // NCORES
DFF = 2816
NJ = DFF // 128
KC = D // 128
ALPHA = (2 * DEPTH) ** 0.25
LN_EPS = 1e-5
HD = 64


class Tok:
    __slots__ = ("sem", "sid", "val")

    def __init__(self, sem, sid, val):
        self.sem, self.sid, self.val = sem, sid, val


class Buf:
    __slots__ = ("name", "w", "r")

    def __init__(self, name=""):
        self.name = name
        self.w = None
        self.r = []


class Prog:
    def __init__(self, nc, es, n_dma_sems=12):
        self.nc = nc
        self.es = es
        self.engs = {"pe": nc.tensor, "act": nc.scalar, "dve": nc.vector,
                     "pool": nc.gpsimd, "sp": nc.sync}
        self.esem = {}
        self.ecnt = {}
        self.seen = {e: {} for e in self.engs}
        self._sid = 0
        for e in self.engs:
            self.esem[e] = (es.enter_context(nc.semaphore("sem_" + e)), self._newsid())
            self.ecnt[e] = 0
        self.dsem = {}
        self.dpos = {}
        for q in ("sp", "act", "pool"):
            ring = []
            for i in range(n_dma_sems):
                ring.append([es.enter_context(nc.semaphore("dma_%s_%d" % (q, i))), self._newsid(), 0, None])
            self.dsem[q] = ring
            self.dpos[q] = 0
        self.out_toks = []
        self._tl = threading.local()

    def _newsid(self):
        self._sid += 1
        return self._sid

    def buf(self, name=""):
        return Buf(name)

    def bufs(self, n, name=""):
        return [Buf(name + str(i)) for i in range(n)]

    def _wait(self, e, tok):
        if tok is None:
            return
        if e == "pe" and tok.sid == self.esem["pe"][1]:
            return
        if self.seen[e].get(tok.sid, 0) >= tok.val:
            return
        self.engs[e].wait_ge(tok.sem, tok.val)
        self.seen[e][tok.sid] = tok.val

    def _deps(self, e, reads, writes):
        for b in reads:
            if b.w is not None:
                self._wait(e, b.w)
        for b in writes:
            if b.w is not None:
                self._wait(e, b.w)
            for t in b.r:
                self._wait(e, t)

    def _commit(self, tok, reads, writes):
        for b in reads:
            b.r.append(tok)
            if len(b.r) > 64:
                b.r = b.r[-64:]
        for b in writes:
            b.w = tok
            b.r = []

    def _yield(self):
        h = getattr(self._tl, "hook", None)
        if h is not None:
            h()

    def interleave(self, fa, fb, ka=2, kb=1):
        cond = threading.Condition()
        st = {"turn": 0, "alive": [True, True], "err": None}

        def make_hook(i, k):
            cnt = [0]

            def hook():
                cnt[0] += 1
                if cnt[0] % k:
                    return
                with cond:
                    if st["alive"][1 - i]:
                        st["turn"] = 1 - i
                        cond.notify_all()
                        while st["turn"] != i and st["alive"][1 - i]:
                            cond.wait()
            return hook

        def runner(i, f, k):
            try:
                with cond:
                    while st["turn"] != i and st["alive"][1 - i]:
                        cond.wait()
                self._tl.hook = make_hook(i, k)
                f()
            except BaseException as ex:
                st["err"] = ex
            finally:
                self._tl.hook = None
                with cond:
                    st["alive"][i] = False
                    st["turn"] = 1 - i
                    cond.notify_all()

        ta = threading.Thread(target=runner, args=(0, fa, ka))
        tb = threading.Thread(target=runner, args=(1, fb, kb))
        ta.start()
        tb.start()
        ta.join()
        tb.join()
        if st["err"] is not None:
            raise st["err"]

    def op(self, e, fn, reads=(), writes=(), inc=True):
        self._yield()
        self._deps(e, reads, writes)
        ins = fn(self.engs[e])
        sem, sid = self.esem[e]
        if inc:
            self.ecnt[e] += 1
            ins.then_inc(sem, 1)
            tok = Tok(sem, sid, self.ecnt[e])
        else:
            assert e == "pe"
            tok = Tok(sem, sid, self.ecnt[e] + 1)
        self._commit(tok, reads, writes)
        return tok

    def dma(self, q, out, in_, reads=(), writes=(), is_output=False):
        self._yield()
        ring = self.dsem[q]
        slot = ring[self.dpos[q] % len(ring)]
        self.dpos[q] += 1
        if slot[3] is not None:
            self._wait(q, slot[3])
        self._deps(q, reads, writes)
        ins = self.engs[q].dma_start(out=out, in_=in_)
        slot[2] += 16
        ins.then_inc(slot[0], 16)
        tok = Tok(slot[0], slot[1], slot[2])
        slot[3] = tok
        self._commit(tok, reads, writes)
        if is_output:
            self.out_toks.append(tok)
        return tok

    def barrier(self):
        toks = []
        for e in self.engs:
            if self.ecnt[e] > 0:
                toks.append(Tok(self.esem[e][0], self.esem[e][1], self.ecnt[e]))
        for q in self.dsem:
            for slot in self.dsem[q]:
                if slot[3] is not None:
                    toks.append(slot[3])
        for e in self.engs:
            for t in toks:
                self._wait(e, t)

    def finish(self):
        for t in self.out_toks:
            self._wait("sp", t)
        self.barrier()


TT = 512
NTT = TPC // TT


def _mm(P, ps_ap, lhsT, rhs, start, stop, reads, writes, inc):
    return P.op("pe", lambda e: e.matmul(ps_ap, lhsT, rhs, start=start, stop=stop),
                reads=reads, writes=writes, inc=inc)


class TokCtx:
    def __init__(self, P, nc, es, T=TPC):
        self.P, self.nc = P, nc
        self.T = T
        self.ntt = T // TT
        NTT = self.ntt
        sb = lambda name, shape, dt: es.enter_context(nc.sbuf_tensor("sb_" + name, shape, dt))
        self.xT = sb("xT", [128, KC, T], F32)
        self.hT = sb("hT", [128, KC, T], BF16)
        self.xB = [[P.buf("x%d_%d" % (n, t)) for t in range(NTT)] for n in range(KC)]
        self.hB = [[P.buf("h%d_%d" % (n, t)) for t in range(NTT)] for n in range(KC)]
        self.ones = sb("ones", [128, 128], BF16)
        self.onesB = P.buf("ones")
        self.vec = sb("vec", [128, 40], F32)
        self.vecB = P.buf("vec")
        self.gs = sb("gs", [128, KC], F32)
        self.sc1 = sb("sc1", [128, KC], F32)
        self.gsB = P.buf("gs")
        self.xb = sb("xb16", [128, KC, TT], BF16)
        self.sq = sb("sq16", [128, KC, TT], BF16)
        self.xbB, self.sqB = P.buf("xb"), P.buf("sq")
        self.m = sb("ln_m", [128, TT], F32)
        self.var = sb("ln_var", [128, TT], F32)
        self.rstd = sb("ln_rstd", [128, TT], F32)
        self.mB, self.varB, self.rstdB = P.buf("m"), P.buf("var"), P.buf("rstd")
        self.t1 = [sb("ln_t%d" % i, [128, TT], F32) for i in range(3)]
        self.t1B = P.bufs(3, "t1")
        self.ps = [es.enter_context(nc.psum_tensor("ps%d" % i, [128, TT], F32)) for i in range(8)]
        self.psB = P.bufs(8, "ps")
        P.op("pool", lambda e: e.memset(self.ones[:], 1.0), writes=[self.onesB])

    def load_vec(self, vec_d, gmul):
        P = self.P
        P.dma("sp", self.vec[:], vec_d, writes=[self.vecB])
        P.op("dve", lambda e: e.tensor_scalar(out=self.gs[:], in0=self.vec[:, 0:8], scalar1=float(gmul),
                                              scalar2=None, op0=ALU.mult),
             reads=[self.vecB], writes=[self.gsB])
        P.op("dve", lambda e: e.tensor_scalar(out=self.sc1[:], in0=self.vec[:, 32:40], scalar1=1.0,
                                              scalar2=None, op0=ALU.add),
             reads=[self.vecB], writes=[self.gsB])

    def load_x(self, xT_d, off=0):
        xv = xT_d.rearrange("(n p) t -> p n t", p=128)
        for n in range(KC):
            self.P.dma("sp", self.xT[:, n, :], xv[:, n, off:off + self.T], writes=self.xB[n])

    def load_h(self, hT_d, off=0):
        hv = hT_d.rearrange("(n p) t -> p n t", p=128)
        for n in range(KC):
            self.P.dma("sp", self.hT[:, n, :], hv[:, n, off:off + self.T], writes=self.hB[n])

    def ln_tile(self, tt, eps, scale_ap, bias_ap, out_ap, out_bufs, extra_reads):
        P = self.P
        sl = slice(tt * TT, (tt + 1) * TT)
        xin = [self.xB[n][tt] for n in range(KC)]
        P.op("act", lambda e: e.activation(out=self.sq[:], in_=self.xT[:, :, sl], func=AF.Square),
             reads=xin, writes=[self.sqB])
        P.op("pool", lambda e: e.tensor_copy(out=self.xb[:], in_=self.xT[:, :, sl]),
             reads=xin, writes=[self.xbB])
        s1, s2 = self.ps[6], self.ps[7]
        for n in range(KC):
            _mm(P, s1[:], self.ones[:], self.xb[:, n, :], n == 0, n == KC - 1,
                [self.onesB, self.xbB], [self.psB[6]], n == KC - 1)
        for n in range(KC):
            _mm(P, s2[:], self.ones[:], self.sq[:, n, :], n == 0, n == KC - 1,
                [self.onesB, self.sqB], [self.psB[7]], n == KC - 1)
        P.op("act", lambda e: e.activation(out=self.m[:], in_=s1[:], func=AF.Copy, scale=1.0 / D),
             reads=[self.psB[6]], writes=[self.mB])
        P.op("dve", lambda e: e.tensor_tensor(out=self.var[:], in0=self.m[:], in1=self.m[:], op=ALU.mult),
             reads=[self.mB], writes=[self.varB])
        P.op("dve", lambda e: e.scalar_tensor_tensor(out=self.var[:], in0=s2[:], scalar=1.0 / D, in1=self.var[:],
                                                     op0=ALU.mult, op1=ALU.subtract),
             reads=[self.psB[7], self.varB], writes=[self.varB])
        P.op("dve", lambda e: e.tensor_scalar(out=self.var[:], in0=self.var[:], scalar1=float(eps), scalar2=None,
                                              op0=ALU.add),
             reads=[self.varB], writes=[self.varB])
        P.op("act", lambda e: e.activation(out=self.var[:], in_=self.var[:], func=AF.Sqrt),
             reads=[self.varB], writes=[self.varB])
        P.op("dve", lambda e: e.reciprocal(out=self.rstd[:], in_=self.var[:]),
             reads=[self.varB], writes=[self.rstdB])
        for n in range(KC):
            k = n % 3
            t1, t1B = self.t1[k], self.t1B[k]
            P.op("dve", lambda e: e.tensor_tensor(out=t1[:], in0=self.xT[:, n, sl], in1=self.m[:], op=ALU.subtract),
                 reads=[self.xB[n][tt], self.mB], writes=[t1B])
            P.op("dve", lambda e: e.tensor_tensor(out=t1[:], in0=t1[:], in1=self.rstd[:], op=ALU.mult),
                 reads=[t1B, self.rstdB], writes=[t1B])
            P.op("act", lambda e: e.activation(out=out_ap(n), in_=t1[:], func=AF.Identity, scale=scale_ap(n), bias=bias_ap(n)),
                 reads=[t1B] + extra_reads, writes=[out_bufs(n)])

    def modulate_only(self, hT_out_d, off=0):
        P = self.P
        ho = hT_out_d.rearrange("(n p) t -> p n t", p=128)
        for tt in range(self.ntt):
            sl = slice(tt * TT, (tt + 1) * TT)
            osl = slice(off + tt * TT, off + (tt + 1) * TT)
            self.ln_tile(tt, LN_EPS,
                         lambda n: self.sc1[:, n:n + 1], lambda n: self.vec[:, 24 + n:25 + n],
                         lambda n: self.hT[:, n, sl], lambda n: self.hB[n][tt], [self.vecB, self.gsB])
            for n in range(KC):
                P.dma("sp", ho[:, n, osl], self.hT[:, n, sl], reads=[self.hB[n][tt]], is_output=True)

    def postnorm_and_modulate(self, xT_out_d, hT_out_d, off=0):
        P = self.P
        xo = xT_out_d.rearrange("(n p) t -> p n t", p=128)
        ho = hT_out_d.rearrange("(n p) t -> p n t", p=128) if hT_out_d is not None else None
        for tt in range(self.ntt):
            sl = slice(tt * TT, (tt + 1) * TT)
            osl = slice(off + tt * TT, off + (tt + 1) * TT)
            self.ln_tile(tt, LN_EPS / (ALPHA * ALPHA),
                         lambda n: self.vec[:, 8 + n:9 + n], lambda n: self.vec[:, 16 + n:17 + n],
                         lambda n: self.xT[:, n, sl], lambda n: self.xB[n][tt], [self.vecB])
            for n in range(KC):
                P.dma("sp", xo[:, n, osl], self.xT[:, n, sl], reads=[self.xB[n][tt]], is_output=True)
            if ho is not None:
                self.ln_tile(tt, LN_EPS,
                             lambda n: self.sc1[:, n:n + 1], lambda n: self.vec[:, 24 + n:25 + n],
                             lambda n: self.hT[:, n, sl], lambda n: self.hB[n][tt], [self.vecB, self.gsB])
                for n in range(KC):
                    P.dma("sp", ho[:, n, osl], self.hT[:, n, sl], reads=[self.hB[n][tt]], is_output=True)


def emit_ffn(C, es, w_up_d, w_down_d):
    P, nc = C.P, C.nc
    sb = lambda name, shape, dt: es.enter_context(nc.sbuf_tensor("sb_" + name, shape, dt))
    NH = NJ // 2
    aT = sb("aT", [128, NH, TPC], BF16)
    aB = [[P.buf() for t in range(NTT)] for j in range(NH)]
    NWB = 3
    wup = [sb("wup%d" % i, [128, KC, 256], BF16) for i in range(NWB)]
    wupB = P.bufs(NWB, "wup")
    wdn = [sb("wdn%d" % i, [128, NH, 128], BF16) for i in range(2)]
    wdnB = P.bufs(2, "wdn")
    st = [sb("silu%d" % i, [128, TT], F32) for i in range(2)]
    stB = P.bufs(2, "silu")
    wu_v = w_up_d.rearrange("(kc p) n -> p kc n", p=128)
    wd_v = w_down_d.rearrange("(j p) n -> p j n", p=128)
    it = 0
    for hf in range(2):
        for jj in range(NH):
            j = hf * NH + jj
            wb = (hf * NH + jj) % NWB
            P.dma("pool", wup[wb][:, :, 0:128], wu_v[:, :, j * 128:(j + 1) * 128], writes=[wupB[wb]])
            P.dma("pool", wup[wb][:, :, 128:256], wu_v[:, :, DFF + j * 128:DFF + (j + 1) * 128], writes=[wupB[wb]])
            for tt in range(NTT):
                sl = slice(tt * TT, (tt + 1) * TT)
                b = it % 2
                it += 1
                pu, pg = C.ps[b], C.ps[2 + b]
                for kc in range(KC):
                    _mm(P, pu[:], wup[wb][:, kc, 0:128], C.hT[:, kc, sl], kc == 0, kc == KC - 1,
                        [wupB[wb], C.hB[kc][tt]], [C.psB[b]], kc == KC - 1)
                for kc in range(KC):
                    _mm(P, pg[:], wup[wb][:, kc, 128:256], C.hT[:, kc, sl], kc == 0, kc == KC - 1,
                        [wupB[wb], C.hB[kc][tt]], [C.psB[2 + b]], kc == KC - 1)
                P.op("act", lambda e: e.activation(out=st[b][:], in_=pu[:], func=AF.Silu),
                     reads=[C.psB[b]], writes=[stB[b]])
                P.op("dve", lambda e: e.tensor_tensor(out=aT[:, jj, sl], in0=pg[:], in1=st[b][:], op=ALU.mult),
                     reads=[C.psB[2 + b], stB[b]], writes=[aB[jj][tt]])
        for n in range(KC):
            wb = n % 2
            P.dma("pool", wdn[wb][:], wd_v[:, hf * NH:(hf + 1) * NH, n * 128:(n + 1) * 128], writes=[wdnB[wb]])
            for tt in range(NTT):
                sl = slice(tt * TT, (tt + 1) * TT)
                b = (n * NTT + tt) % 2
                py = C.ps[4 + b]
                for jj in range(NH):
                    _mm(P, py[:], wdn[wb][:, jj, :], aT[:, jj, sl], jj == 0, jj == NH - 1,
                        [wdnB[wb], aB[jj][tt]], [C.psB[4 + b]], jj == NH - 1)
                P.op("dve", lambda e: e.scalar_tensor_tensor(out=C.xT[:, n, sl], in0=py[:], scalar=C.gs[:, n:n + 1],
                                                             in1=C.xT[:, n, sl], op0=ALU.mult, op1=ALU.add),
                     reads=[C.psB[4 + b], C.gsB, C.xB[n][tt]], writes=[C.xB[n][tt]])


def build_ffn_prog(final=False):
    nc = bass.Bass("TRN2", target_bir_lowering=False)
    xT_d = nc.dram_tensor("xT", [D, TPC], F32, kind="ExternalInput").ap()
    hT_d = nc.dram_tensor("hT", [D, TPC], BF16, kind="ExternalInput").ap()
    wu_d = nc.dram_tensor("w_up", [D, 2 * DFF], F32, kind="ExternalInput").ap()
    wd_d = nc.dram_tensor("w_down", [DFF, D], F32, kind="ExternalInput").ap()
    vec_d = nc.dram_tensor("vec", [128, 40], F32, kind="ExternalInput").ap()
    xo_d = nc.dram_tensor("xT_out", [D, TPC], F32, kind="ExternalOutput").ap()
    ho_d = None if final else nc.dram_tensor("hT_out", [D, TPC], BF16, kind="ExternalOutput").ap()
    with ExitStack() as es:
        P = Prog(nc, es)
        C = TokCtx(P, nc, es)
        C.load_vec(vec_d, 0.5 / ALPHA)
        C.load_x(xT_d)
        C.load_h(hT_d)
        with ExitStack() as es2:
            emit_ffn(C, es2, wu_d, wd_d)
            C.postnorm_and_modulate(xo_d, ho_d)
            P.finish()
    return nc


def build_mod_prog():
    nc = bass.Bass("TRN2", target_bir_lowering=False)
    xT_d = nc.dram_tensor("xT", [D, TPC], F32, kind="ExternalInput").ap()
    vec_d = nc.dram_tensor("vec", [128, 40], F32, kind="ExternalInput").ap()
    ho_d = nc.dram_tensor("hT_out", [D, TPC], BF16, kind="ExternalOutput").ap()
    with ExitStack() as es:
        P = Prog(nc, es)
        C = TokCtx(P, nc, es)
        C.load_vec(vec_d, 1.0)
        C.load_x(xT_d)
        C.modulate_only(ho_d)
        P.finish()
    return nc


def build_mixpost_prog():
    nc = bass.Bass("TRN2", target_bir_lowering=False)
    dt = lambda name, shape, dty, kind="ExternalInput": nc.dram_tensor(name, shape, dty, kind=kind).ap()
    xT_d = dt("xT", [D, TPC], F32)
    hT_d = dt("hT", [D, TPC], BF16)
    br_d = dt("brT", [2048, TPC], BF16)
    wg_d = dt("wg", [D, 3072], F32)
    wp_d = dt("wp", [2048, D], F32)
    wo_d = dt("wo", [D, D], F32)
    vec_d = dt("vec", [128, 40], F32)
    xo_d = dt("xT_out", [D, TPC], F32, "ExternalOutput")
    ho_d = dt("hT_out", [D, TPC], BF16, "ExternalOutput")
    with ExitStack() as es:
        P = Prog(nc, es)
        C = TokCtx(P, nc, es, T=TT)
        sb = lambda name, shape, dty: es.enter_context(nc.sbuf_tensor("sb_" + name, shape, dty))
        C.load_vec(vec_d, 1.0 / ALPHA)
        wg = sb("wg", [128, KC, 3072], BF16)
        wp = sb("wp", [128, 16, D], BF16)
        wo = sb("wo", [128, KC, D], BF16)
        wB = P.buf()
        wgv = wg_d.rearrange("(kc p) n -> p kc n", p=128)
        for kc in range(KC):
            P.dma("pool", wg[:, kc, :], wgv[:, kc, :], writes=[wB])
        wpv = wp_d.rearrange("(kc p) n -> p kc n", p=128)
        for kc in range(16):
            P.dma("pool", wp[:, kc, :], wpv[:, kc, :], writes=[wB])
        P.dma("pool", wo[:], wo_d.rearrange("(kc p) n -> p kc n", p=128), writes=[wB])
        br = sb("br", [128, 16, TT], BF16)
        brB = P.buf()
        mT = sb("mT", [128, KC, TT], BF16)
        mB = P.bufs(KC)
        sg = [sb("sg%d" % i, [128, TT], F32) for i in range(3)]
        sgB = P.bufs(3)
        ta = [sb("ta%d" % i, [128, TT], F32) for i in range(2)]
        taB = P.bufs(2)
        brv = br_d.rearrange("(kc p) t -> p kc t", p=128)
        for tk in range(TPC // TT):
            off = tk * TT
            C.load_x(xT_d, off)
            C.load_h(hT_d, off)
            P.dma("sp", br[:], brv[:, :, off:off + TT], writes=[brB])
            for n in range(KC):
                ns = slice(n * 128, (n + 1) * 128)
                for g in range(3):
                    for kc in range(KC):
                        _mm(P, C.ps[g][:], wg[:, kc, g * 1024 + n * 128:g * 1024 + (n + 1) * 128], C.hT[:, kc, :], kc == 0, kc == KC - 1,
                            [wB, C.hB[kc][0]], [C.psB[g]], kc == KC - 1)
                for g, (k0, nk) in enumerate(((0, 4), (4, 8), (12, 4))):
                    for kc in range(nk):
                        _mm(P, C.ps[3 + g][:], wp[:, k0 + kc, ns], br[:, k0 + kc, :], kc == 0, kc == nk - 1,
                            [wB, brB], [C.psB[3 + g]], kc == nk - 1)
                for g in range(3):
                    P.op("act", lambda e: e.activation(out=sg[g][:], in_=C.ps[g][:], func=AF.Sigmoid), reads=[C.psB[g]], writes=[sgB[g]])
                P.op("dve", lambda e: e.tensor_tensor(out=ta[0][:], in0=C.ps[3][:], in1=sg[0][:], op=ALU.mult),
                     reads=[C.psB[3], sgB[0]], writes=[taB[0]])
                P.op("dve", lambda e: e.tensor_tensor(out=ta[1][:], in0=C.ps[4][:], in1=sg[1][:], op=ALU.mult),
                     reads=[C.psB[4], sgB[1]], writes=[taB[1]])
                P.op("dve", lambda e: e.tensor_tensor(out=ta[0][:], in0=ta[0][:], in1=ta[1][:], op=ALU.add),
                     reads=[taB[0], taB[1]], writes=[taB[0]])
                P.op("dve", lambda e: e.tensor_tensor(out=ta[1][:], in0=C.ps[5][:], in1=sg[2][:], op=ALU.mult),
                     reads=[C.psB[5], sgB[2]], writes=[taB[1]])
                P.op("dve", lambda e: e.tensor_tensor(out=mT[:, n, :], in0=ta[0][:], in1=ta[1][:], op=ALU.add),
                     reads=[taB[0], taB[1]], writes=[mB[n]])
            for n2 in range(KC):
                b = n2 % 2
                for n in range(KC):
                    _mm(P, C.ps[b][:], wo[:, n, n2 * 128:(n2 + 1) * 128], mT[:, n, :], n == 0, n == KC - 1,
                        [wB, mB[n]], [C.psB[b]], n == KC - 1)
                P.op("dve", lambda e: e.scalar_tensor_tensor(out=C.xT[:, n2, :], in0=C.ps[b][:], scalar=C.gs[:, n2:n2 + 1],
                                                             in1=C.xT[:, n2, :], op0=ALU.mult, op1=ALU.add),
                     reads=[C.psB[b], C.gsB, C.xB[n2][0]], writes=[C.xB[n2][0]])
            C.postnorm_and_modulate(xo_d, ho_d, off)
        P.finish()
    return nc


def build_ada_prog():
    nc = bass.Bass("TRN2", target_bir_lowering=False)
    dt = lambda name, shape, dty, kind="ExternalInput": nc.dram_tensor(name, shape, dty, kind=kind).ap()
    c_d = dt("c", [128, KC], F32)
    w_d = dt("ada_w", [DEPTH, D, 1152], F32)
    b_d = dt("ada_b", [128, DEPTH * 9], F32)
    o_d = dt("ada_out", [128, DEPTH * 9], F32, "ExternalOutput")
    with ExitStack() as es:
        P = Prog(nc, es)
        sb = lambda name, shape, dty: es.enter_context(nc.sbuf_tensor("sb_" + name, shape, dty))
        ct = sb("c", [128, KC], F32)
        bt = sb("b", [128, DEPTH * 9], F32)
        ot = sb("o", [128, DEPTH * 9], F32)
        cB, bB, oB = P.buf(), P.buf(), P.buf()
        P.dma("sp", ct[:], c_d, writes=[cB])
        P.dma("sp", bt[:], b_d, writes=[bB])
        P.op("act", lambda e: e.activation(out=ct[:], in_=ct[:], func=AF.Silu), reads=[cB], writes=[cB])
        ps = es.enter_context(nc.psum_tensor("ps", [128, 64], F32))
        psB = P.buf()
        wt = [sb("w%d" % i, [128, KC, 1152], F32) for i in range(2)]
        wtB = P.bufs(2)
        for l in range(DEPTH):
            w, wB = wt[l % 2], wtB[l % 2]
            wv = w_d[l].rearrange("(kc p) n -> p kc n", p=128)
            for kc in range(KC):
                P.dma("sp", w[:, kc, :], wv[:, kc, :], writes=[wB])
            for ch in range(9):
                col = l * 9 + ch
                for kc in range(KC):
                    _mm(P, ps[:, col:col + 1], w[:, kc, ch * 128:(ch + 1) * 128], ct[:, kc:kc + 1], kc == 0, kc == KC - 1,
                        [wB, cB], [psB], kc == KC - 1)
        P.op("dve", lambda e: e.tensor_tensor(out=ot[:], in0=ps[:, 0:DEPTH * 9], in1=bt[:], op=ALU.add), reads=[psB, bB], writes=[oB])
        P.dma("sp", o_d, ot[:], reads=[oB], is_output=True)
        P.finish()
    return nc


NQT = SEQ // TT
NKB = SEQ // 128
MASKNEG = -30000.0


class MixCtx:
    def __init__(self, P, nc, es, hT_d, parent=None, tag=""):
        self.P, self.nc = P, nc
        self.sb = lambda name, shape, dt: es.enter_context(nc.sbuf_tensor("sb_" + tag + name, shape, dt))
        self.hv = hT_d.rearrange("(n p) t -> p n t", p=128)
        if parent is None:
            self.psall = es.enter_context(nc.psum_tensor("psall", [128, 8 * TT], F32))
            self.psB = P.bufs(8, "ps")
        else:
            self.psall, self.psB = parent.psall, parent.psB
        self.ps = [self.psall[:, i * TT:(i + 1) * TT] for i in range(8)]
        self.hbuf = [self.sb("hbuf%d" % i, [128, KC, TT], BF16) for i in range(2)]
        self.hbufB = P.bufs(2, "hbuf")
        self.hcnt = 0

    def load_h_tile(self, tt):
        i = self.hcnt % 2
        self.hcnt += 1
        self.P.dma("sp", self.hbuf[i][:], self.hv[:, :, tt * TT:(tt + 1) * TT], writes=[self.hbufB[i]])
        return self.hbuf[i], self.hbufB[i]

    def load_w(self, name, w_d, ncols):
        t = self.sb(name, [128, KC, ncols], BF16)
        b = self.P.buf(name)
        self.P.dma("pool", t[:], w_d.rearrange("(kc p) n -> p kc n", p=128), writes=[b])
        return t, b


def emit_attention(M, es, w_att_d, avec_d, cmask_d, attT_out_d, npb=3):
    P, nc = M.P, M.nc
    sb = lambda name, shape, dt: es.enter_context(nc.sbuf_tensor("sb_" + name, shape, dt))
    W, WB = M.load_w("w_att", w_att_d, 193)
    Qx = sb("Qx", [70, SEQ], BF16)
    Kx = sb("Kx", [70, SEQ], BF16)
    Vx = sb("Vx", [128, NKB, 65], BF16)
    QB = [P.buf() for _ in range(NQT)]
    KB = [P.buf() for _ in range(NQT)]
    VB = [P.buf() for _ in range(NQT)]
    avec = sb("avec", [128, 4], F32)
    avecB = P.buf()
    P.dma("sp", avec[:], avec_d, writes=[avecB])
    cmask = sb("cmask", [128, 4, TT], BF16)
    ident = sb("identb", [128, 128], BF16)
    cB = P.buf()
    P.dma("sp", cmask[:], cmask_d[:, 0:4 * TT].rearrange("p (a t) -> p a t", a=4), writes=[cB])
    P.dma("sp", ident[:], cmask_d[:, 4 * TT:4 * TT + 128], writes=[cB])
    sel = sb("sel", [128, 8, 70], BF16)
    onesr = sb("onesr", [128, TT], BF16)
    onesf = sb("onesf", [128, TT], F32)
    selB = P.buf()
    P.op("pool", lambda e: e.memset(sel[:], 0.0), writes=[selB])
    P.op("pool", lambda e: e.memset(onesr[:], 1.0), writes=[selB])
    P.op("pool", lambda e: e.memset(onesf[:], 1.0), writes=[selB])
    for i, (c0, c1, v) in enumerate([(64, 67, 1.0), (67, 68, 1.0), (68, 69, 1.0), (69, 70, 1.0),
                                     (67, 70, -1.0), (64, 65, 1.0), (65, 66, 1.0), (66, 67, 1.0)]):
        P.op("pool", lambda e: e.memset(sel[64:65, i, c0:c1], v), writes=[selB])
    P.op("pool", lambda e: e.memset(Vx[:, :, 64:65], 1.0), writes=VB)
    nfb = sb("nfb", [128, 1], F32)
    P.op("dve", lambda e: e.tensor_scalar(out=nfb[:], in0=avec[:, 0:1], scalar1=-1.0, scalar2=None, op0=ALU.mult),
         reads=[avecB], writes=[avecB])
    fl = sb("fl", [128, TT], F32)
    fr = [sb("fr%d" % i, [128, TT], F32) for i in range(2)]
    f16 = [sb("f16_%d" % i, [128, TT], BF16) for i in range(3)]
    flB, frB, f16B = P.buf(), P.bufs(2), P.bufs(3)
    carry = sb("fcarry", [128, 1], F32)
    carryB = P.buf()
    P.op("dve", lambda e: e.memset(carry[:], 0.0), writes=[carryB])
    r64 = slice(64, 65)
    for tt in range(NQT):
        sl = slice(tt * TT, (tt + 1) * TT)
        hb, hbB = M.load_h_tile(tt)
        pq, pk = M.ps[0], M.ps[1]
        for kc in range(KC):
            _mm(P, pq[0:64, :], W[:, kc, 0:64], hb[:, kc, :], kc == 0, kc == KC - 1, [WB, hbB], [M.psB[0]], kc == KC - 1)
        for kc in range(KC):
            _mm(P, pk[0:65, :], W[:, kc, 64:129], hb[:, kc, :], kc == 0, kc == KC - 1, [WB, hbB], [M.psB[1]], kc == KC - 1)
        P.op("act", lambda e: e.activation(out=Qx[0:64, sl], in_=pq[0:64, :], func=AF.Copy, scale=0.125),
             reads=[M.psB[0]], writes=[QB[tt]])
        P.op("dve", lambda e: e.tensor_copy(out=Kx[0:64, sl], in_=pk[0:64, :]), reads=[M.psB[1]], writes=[KB[tt]])
        P.op("act", lambda e: e.activation(out=fl[r64, :], in_=pk[r64, :], func=AF.Exp, scale=-1.0, bias=nfb[r64, :]),
             reads=[M.psB[1], avecB], writes=[flB])
        P.op("act", lambda e: e.activation(out=fl[r64, :], in_=fl[r64, :], func=AF.Ln, bias=1.0),
             reads=[flB], writes=[flB])
        P.op("dve", lambda e: e.tensor_tensor_scan(out=fr[0][r64, :], data0=onesf[r64, :], data1=fl[r64, :],
                                                   initial=carry[r64, :], op0=ALU.mult, op1=ALU.add),
             reads=[flB, carryB, selB], writes=[frB[0]])
        P.op("dve", lambda e: e.tensor_copy(out=carry[r64, :], in_=fr[0][r64, TT - 1:TT]), reads=[frB[0]], writes=[carryB])
        P.op("dve", lambda e: e.tensor_copy(out=f16[0][r64, :], in_=fr[0][r64, :]), reads=[frB[0]], writes=[f16B[0]])
        P.op("dve", lambda e: e.tensor_tensor(out=fr[1][r64, :], in0=fr[0][r64, :], in1=f16[0][r64, :], op=ALU.subtract),
             reads=[frB[0], f16B[0]], writes=[frB[1]])
        P.op("dve", lambda e: e.tensor_copy(out=f16[1][r64, :], in_=fr[1][r64, :]), reads=[frB[1]], writes=[f16B[1]])
        P.op("dve", lambda e: e.tensor_tensor(out=fr[0][r64, :], in0=fr[1][r64, :], in1=f16[1][r64, :], op=ALU.subtract),
             reads=[frB[1], f16B[1]], writes=[frB[0]])
        P.op("dve", lambda e: e.tensor_copy(out=f16[2][r64, :], in_=fr[0][r64, :]), reads=[frB[0]], writes=[f16B[2]])
        pa, pb = M.ps[2], M.ps[3]
        srcs = [onesr, f16[0], f16[1], f16[2]]
        srcB = [selB, f16B[0], f16B[1], f16B[2]]
        for i in range(4):
            _mm(P, pa[0:70, :], sel[r64, i, :], srcs[i][r64, :], i == 0, i == 3, [selB, srcB[i]], [M.psB[2]], i == 3)
        for i in range(4):
            _mm(P, pb[0:70, :], sel[r64, 4 + i, :], srcs[i][r64, :], i == 0, i == 3, [selB, srcB[i]], [M.psB[3]], i == 3)
        P.op("act", lambda e: e.activation(out=Qx[64:70, sl], in_=pa[64:70, :], func=AF.Copy),
             reads=[M.psB[2]], writes=[QB[tt]])
        P.op("dve", lambda e: e.tensor_copy(out=Kx[64:70, sl], in_=pb[64:70, :]), reads=[M.psB[3]], writes=[KB[tt]])
        pvi = 6 + tt % 2
        pv = M.ps[pvi]
        for bk in range(4):
            for kc in range(KC):
                _mm(P, pv[:, bk * 64:(bk + 1) * 64], hb[:, kc, bk * 128:(bk + 1) * 128], W[:, kc, 129:193],
                    kc == 0, kc == KC - 1, [WB, hbB], [M.psB[pvi]], kc == KC - 1)
        P.op("dve", lambda e: e.tensor_copy(out=Vx[:, tt * 4:(tt + 1) * 4, 0:64],
                                            in_=pv[:, 0:256].rearrange("p (b d) -> p b d", b=4)),
             reads=[M.psB[pvi]], writes=[VB[tt]])
    NPB = npb
    pT = [sb("pT%d" % i, [128, 2 * TT], BF16) for i in range(NPB)]
    pTB = P.bufs(NPB)
    den = sb("den", [128, TT], F32)
    bc = sb("bc", [64, TT], F32)
    ao = [sb("ao%d" % i, [64, TT], BF16) for i in range(2)]
    denB, bcB, aoB = P.buf(), P.buf(), P.bufs(2)
    pairs = [(I, J) for I in range(NQT) for J in range(0, 4 * I + 4, 2)]
    LOOK = NPB - 1
    slot = [0]
    slots = {}

    def issue_S(n):
        I, J0 = pairs[n]
        b = slot[0] % NPB
        slot[0] += 1
        slots[n] = b
        for k in range(2):
            J = J0 + k
            pS, pSB = M.ps[2 * b + k], M.psB[2 * b + k]
            diag = J >= 4 * I
            _mm(P, pS[:], Kx[:, J * 128:(J + 1) * 128], Qx[:, I * TT:(I + 1) * TT], True, not diag,
                [KB[J // 4], QB[I]], [pSB], not diag)
            if diag:
                _mm(P, pS[:], ident[:], cmask[:, J - 4 * I, :], False, True, [cB], [pSB], True)

    for n in range(min(LOOK, len(pairs))):
        issue_S(n)
    for n, (I, J0) in enumerate(pairs):
        b = slots.pop(n)
        qs = slice(I * TT, (I + 1) * TT)
        po, poB = M.ps[6 + I % 2], M.psB[6 + I % 2]
        nJ = 4 * I + 4
        P.op("act", lambda e: e.activation(out=pT[b][:], in_=M.psall[:, 2 * b * TT:(2 * b + 2) * TT], func=AF.Exp),
             reads=[M.psB[2 * b], M.psB[2 * b + 1]], writes=[pTB[b]])
        if n + LOOK < len(pairs):
            issue_S(n + LOOK)
        for k in range(2):
            J = J0 + k
            _mm(P, po[0:65, :], Vx[:, J, :], pT[b][:, k * TT:(k + 1) * TT], J == 0, J == nJ - 1, [VB[J // 4], pTB[b]], [poB],
                k == 1)
        if J0 + 2 == nJ:
            P.op("dve", lambda e: e.reciprocal(out=den[r64, :], in_=po[r64, :]), reads=[poB], writes=[denB])
            bb_ = slot[0] % NPB
            slot[0] += 1
            pbc, pbcB = M.ps[2 * bb_], M.psB[2 * bb_]
            _mm(P, pbc[0:64, :], onesf[r64, 0:64], den[r64, :], True, True, [selB, denB], [pbcB], True)
            P.op("dve", lambda e: e.tensor_copy(out=bc[:], in_=pbc[0:64, :]), reads=[pbcB], writes=[bcB])
            P.op("dve", lambda e: e.tensor_tensor(out=ao[I % 2][:], in0=po[0:64, :], in1=bc[:], op=ALU.mult),
                 reads=[poB, bcB], writes=[aoB[I % 2]])
            P.dma("sp", attT_out_d[:, qs], ao[I % 2][:], reads=[aoB[I % 2]], is_output=True)


def emit_lru(M, es, w_lru_d, gab_d, lvec_d, lruT_out_d, banks=(0, 1, 2, 3, 4, 5, 6, 7)):
    P, nc = M.P, M.nc
    sb = lambda name, shape, dt: es.enter_context(nc.sbuf_tensor("sb_" + name, shape, dt))
    W, WB = M.load_w("w_lru", w_lru_d, 256)
    gab = sb("gab", [128, 256], BF16)
    gabB = P.buf()
    P.dma("pool", gab[:], gab_d, writes=[gabB])
    lv = sb("lvec", [128, 8], F32)
    lvB = P.buf()
    P.dma("sp", lv[:], lvec_d, writes=[lvB])
    cv = sb("lconst", [128, 4], F32)
    cvB = P.buf()
    P.op("act", lambda e: e.activation(out=cv[:, 3:4], in_=lv[:, 7:8], func=AF.Exp, scale=-1.0), reads=[lvB], writes=[cvB])
    P.op("act", lambda e: e.activation(out=cv[:, 3:4], in_=cv[:, 3:4], func=AF.Ln, bias=1.0), reads=[cvB], writes=[cvB])
    P.op("dve", lambda e: e.tensor_scalar(out=cv[:, 0:1], in0=cv[:, 3:4], scalar1=-4.0, scalar2=None, op0=ALU.mult),
         reads=[cvB], writes=[cvB])
    P.op("dve", lambda e: e.tensor_scalar(out=cv[:, 1:3], in0=lv[:, 5:7], scalar1=0.5, scalar2=None, op0=ALU.mult),
         reads=[lvB, cvB], writes=[cvB])
    xbuf = sb("xbuf", [128, 3 + TT], F32)
    xbufB = P.buf()
    P.op("dve", lambda e: e.memset(xbuf[:, 0:3], 0.0), writes=[xbufB])
    names = ["xc", "tr", "a", "ti", "om", "u", "hc", "y2", "gl"]
    T = {n: sb("l_" + n, [128, TT], F32) for n in names}
    B = {n: P.buf(n) for n in names}
    xc16 = sb("xc16", [128, TT], BF16)
    xc16B = P.buf()
    ob = [sb("lo%d" % i, [128, TT], BF16) for i in range(2)]
    obB = P.bufs(2)
    hcar = sb("hcar", [128, 1], F32)
    hcarB = P.buf()
    P.op("dve", lambda e: e.memset(hcar[:], 0.0), writes=[hcarB])
    for tt in range(NQT):
        sl = slice(tt * TT, (tt + 1) * TT)
        hb, hbB = M.load_h_tile(tt)
        if len(banks) >= 8:
            bx, by, br_, bi_ = banks[0 + tt % 2], banks[2 + tt % 2], banks[4 + tt % 2], banks[6 + tt % 2]
        else:
            bx, by, br_, bi_ = banks[0], banks[1], banks[0], banks[1]
        px, py = M.ps[bx], M.ps[by]
        pxB, pyB = M.psB[bx], M.psB[by]
        for kc in range(KC):
            _mm(P, px[:], W[:, kc, 0:128], hb[:, kc, :], kc == 0, kc == KC - 1, [WB, hbB], [pxB], kc == KC - 1)
        for kc in range(KC):
            _mm(P, py[:], W[:, kc, 128:256], hb[:, kc, :], kc == 0, kc == KC - 1, [WB, hbB], [pyB], kc == KC - 1)
        P.op("act", lambda e: e.activation(out=xbuf[:, 3:3 + TT], in_=px[:], func=AF.Copy), reads=[pxB], writes=[xbufB])
        P.op("dve", lambda e: e.tensor_scalar(out=T["xc"][:], in0=xbuf[:, 3:3 + TT], scalar1=lv[:, 3:4], scalar2=lv[:, 4:5],
                                              op0=ALU.mult, op1=ALU.add), reads=[xbufB, lvB], writes=[B["xc"]])
        for k in (2, 1, 0):
            P.op("dve", lambda e: e.scalar_tensor_tensor(out=T["xc"][:], in0=xbuf[:, k:k + TT], scalar=lv[:, k:k + 1],
                                                         in1=T["xc"][:], op0=ALU.mult, op1=ALU.add),
                 reads=[xbufB, lvB, B["xc"]], writes=[B["xc"]])
        P.op("dve", lambda e: e.tensor_copy(out=xbuf[:, 0:3], in_=xbuf[:, TT:TT + 3]), reads=[xbufB], writes=[xbufB])
        P.op("pool", lambda e: e.tensor_copy(out=xc16[:], in_=T["xc"][:]), reads=[B["xc"]], writes=[xc16B])
        P.op("act", lambda e: e.activation(out=T["y2"][:], in_=py[:], func=AF.Square), reads=[pyB], writes=[B["y2"]])
        P.op("dve", lambda e: e.tensor_scalar(out=T["y2"][:], in0=T["y2"][:], scalar1=0.044715, scalar2=1.0,
                                              op0=ALU.mult, op1=ALU.add), reads=[B["y2"]], writes=[B["y2"]])
        P.op("dve", lambda e: e.tensor_tensor(out=T["y2"][:], in0=py[:], in1=T["y2"][:], op=ALU.mult),
             reads=[pyB, B["y2"]], writes=[B["y2"]])
        P.op("act", lambda e: e.activation(out=T["gl"][:], in_=T["y2"][:], func=AF.Tanh, scale=0.7978845608028654),
             reads=[B["y2"]], writes=[B["gl"]])
        P.op("dve", lambda e: e.scalar_tensor_tensor(out=T["gl"][:], in0=T["gl"][:], scalar=1.0, in1=py[:],
                                                     op0=ALU.add, op1=ALU.mult), reads=[B["gl"], pyB], writes=[B["gl"]])
        pr, pi = M.ps[br_], M.ps[bi_]
        prB, piB = M.psB[br_], M.psB[bi_]
        _mm(P, pr[:], gab[:, 0:128], xc16[:], True, True, [gabB, xc16B], [prB], True)
        _mm(P, pi[:], gab[:, 128:256], xc16[:], True, True, [gabB, xc16B], [piB], True)
        P.op("act", lambda e: e.activation(out=T["tr"][:], in_=pr[:], func=AF.Tanh, scale=0.5, bias=cv[:, 1:2]),
             reads=[prB, cvB], writes=[B["tr"]])
        P.op("act", lambda e: e.activation(out=T["a"][:], in_=T["tr"][:], func=AF.Exp, scale=cv[:, 0:1], bias=cv[:, 0:1]),
             reads=[B["tr"], cvB], writes=[B["a"]])
        P.op("act", lambda e: e.activation(out=T["ti"][:], in_=pi[:], func=AF.Tanh, scale=0.5, bias=cv[:, 2:3]),
             reads=[piB, cvB], writes=[B["ti"]])
        P.op("dve", lambda e: e.tensor_tensor(out=T["om"][:], in0=T["a"][:], in1=T["a"][:], op=ALU.mult),
             reads=[B["a"]], writes=[B["om"]])
        P.op("dve", lambda e: e.tensor_scalar(out=T["om"][:], in0=T["om"][:], scalar1=-1.0, scalar2=1.0,
                                              op0=ALU.mult, op1=ALU.add), reads=[B["om"]], writes=[B["om"]])
        P.op("act", lambda e: e.activation(out=T["om"][:], in_=T["om"][:], func=AF.Sqrt), reads=[B["om"]], writes=[B["om"]])
        P.op("dve", lambda e: e.scalar_tensor_tensor(out=T["u"][:], in0=T["ti"][:], scalar=1.0, in1=T["xc"][:],
                                                     op0=ALU.add, op1=ALU.mult), reads=[B["ti"], B["xc"]], writes=[B["u"]])
        P.op("dve", lambda e: e.scalar_tensor_tensor(out=T["u"][:], in0=T["u"][:], scalar=0.5, in1=T["om"][:],
                                                     op0=ALU.mult, op1=ALU.mult), reads=[B["u"], B["om"]], writes=[B["u"]])
        P.op("dve", lambda e: e.tensor_tensor_scan(out=T["hc"][:], data0=T["a"][:], data1=T["u"][:], initial=hcar[:],
                                                   op0=ALU.mult, op1=ALU.add), reads=[B["a"], B["u"], hcarB], writes=[B["hc"]])
        P.op("dve", lambda e: e.tensor_copy(out=hcar[:], in_=T["hc"][:, TT - 1:TT]), reads=[B["hc"]], writes=[hcarB])
        o = ob[tt % 2]
        P.op("dve", lambda e: e.scalar_tensor_tensor(out=o[:], in0=T["hc"][:], scalar=0.5, in1=T["gl"][:],
                                                     op0=ALU.mult, op1=ALU.mult), reads=[B["hc"], B["gl"]], writes=[obB[tt % 2]])
        P.dma("sp", lruT_out_d[:, sl], o[:], reads=[obB[tt % 2]], is_output=True)


def emit_rwkv(M, es, w_rw_d, rvec_d, rmat_d, rgn_d, rconst_d, rwT_out_d):
    P, nc = M.P, M.nc
    sb = lambda name, shape, dt: es.enter_context(nc.sbuf_tensor("sb_" + name, shape, dt))
    W, WB = M.load_w("w_rw", w_rw_d, 448)
    rv = sb("rvec", [128, 16], F32)
    rm16 = sb("rmat16", [128, 192], BF16)
    rgn = sb("rgn", [128, 512], F32)
    rc = sb("rconst", [128, 10 * 512], F32)
    cB = P.buf()
    P.dma("sp", rv[:], rvec_d, writes=[cB])
    P.dma("pool", rm16[:], rmat_d, writes=[cB])
    P.dma("sp", rgn[:], rgn_d, writes=[cB])
    P.dma("sp", rc[:], rconst_d, writes=[cB])
    ident = rc[:, 0:128]
    m_su = rc[:, 512:1024].rearrange("p (c t) -> p c t", c=4)
    m_iu = rc[:, 1024:1536].rearrange("p (c t) -> p c t", c=4)
    m_sl = rc[:, 1536:2048].rearrange("p (c t) -> p c t", c=4)
    cmask = rc[:, 2048:2560]
    m_d16 = rc[:, 2560:3072]
    m_o16 = rc[:, 3072:3584]
    m_o32 = rc[:, 3584:4096]
    m_o64 = rc[:, 4096:4608]
    identx4 = rc[:, 4608:5120]
    onesc = rc[:, 128:129]
    ident16 = sb("ident16", [128, 128], BF16)
    P.op("dve", lambda e: e.tensor_copy(out=ident16[:], in_=rc[:, 0:128]), reads=[cB], writes=[cB])
    ones64 = sb("ones64", [128, 64], F32)
    P.op("pool", lambda e: e.memset(ones64[:], 1.0), writes=[cB])
    dv = sb("rdv", [128, 16], F32)
    P.op("dve", lambda e: e.tensor_scalar(out=dv[:, 0:10], in0=rv[:, 0:10], scalar1=-1.0, scalar2=1.0, op0=ALU.mult, op1=ALU.add),
         reads=[cB], writes=[cB])
    P.op("dve", lambda e: e.tensor_scalar(out=dv[:, 10:12], in0=rv[:, 3:5], scalar1=0.5, scalar2=None, op0=ALU.mult),
         reads=[cB], writes=[cB])
    P.op("dve", lambda e: e.tensor_scalar(out=dv[:, 12:13], in0=rv[:, 6:7], scalar1=0.5, scalar2=None, op0=ALU.mult),
         reads=[cB], writes=[cB])
    P.op("dve", lambda e: e.tensor_scalar(out=dv[:, 13:14], in0=rv[:, 6:7], scalar1=-0.5, scalar2=1.0, op0=ALU.mult, op1=ALU.add),
         reads=[cB], writes=[cB])
    def t64(name, dt=F32, n=TT):
        return sb("r_" + name, [64, n], dt), P.buf(name)
    def t128(name, dt=F32, n=TT):
        return sb("r_" + name, [128, n], dt), P.buf(name)
    zr, zrB = t64("zr", n=TT + 1); zk, zkB = t64("zk", n=TT + 1); zv, zvB = t64("zv", n=TT + 1)
    zwa, zwaB = t128("zwa", n=TT + 1); zg, zgB = t128("zg", n=TT + 1)
    for z, zB in ((zr, zrB), (zk, zkB), (zv, zvB), (zwa, zwaB), (zg, zgB)):
        P.op("dve", lambda e: e.memset(z[:, 0:1], 0.0), writes=[zB])
    rm, rmB = t64("rm"); km, kmB = t64("km"); vm, vmB = t64("vm", BF16)
    wam, wamB = t128("wam"); gm, gmB = t128("gm")
    tmp, tmpB = t128("tmp")
    twl, twlB = t128("twl", BF16); sg16, sg16B = t128("sg16", BF16)
    lw, lwB = t64("lw"); ta, taB = t64("ta"); kk, kkB = t64("kk"); kf, kfB = t64("kf"); bb, bbB = t64("bb")
    cw, cwB = t64("cw"); cwx, cwxB = t64("cwx"); dd, ddB = t64("dd")
    E2, E2B = t64("E2"); E3, E3B = t64("E3"); E4, E4B = t64("E4")
    BhT, BhTB = t64("BhT", BF16); KhT, KhTB = t64("KhT", BF16)
    SH = [{"E1": t64("E1_%d" % i), "AR": t64("AR_%d" % i, BF16, n=4 * 256), "bT": t64("bT_%d" % i, BF16), "kT": t64("kT_%d" % i, BF16),
           "Vt": t128("Vt_%d" % i, BF16, n=256), "Bh": t128("Bh_%d" % i, BF16, n=256), "Kh": t128("Kh_%d" % i, BF16, n=256),
           "gtm": t128("gtm_%d" % i, n=256), "sbon": t128("sbon_%d" % i, n=4)} for i in range(2)]
    prod, prodB = t64("prod")
    IDT = BF16
    X = [t128("X%d" % i, IDT) for i in range(2)]
    XT = [t128("XT%d" % i, IDT) for i in range(2)]
    Yrb, YrbB = t128("Yrb", BF16); Xak, XakB = t128("Xak", BF16); Yrk, YrkB = t128("Yrk", BF16)
    Z = [t128("Z0", IDT), t128("Z1", BF16)]
    iv = {n: t128("iv_" + n, IDT) for n in ("S0", "S0T", "S1", "S1T", "S2", "S2T", "S3", "S3T", "J", "JT", "Fa", "FaT", "Fb", "FbT",
                                       "U", "L", "W1", "V1", "Ta", "Tb", "Na", "Nb")}
    RpT, RpTB = t64("RpT"); Yl, YlB = t128("Yl", n=256); MT, MTB = t64("MT", n=256); Psi, PsiB = t64("Psi", n=256)
    Tst = [t64("T%d" % i, n=64) for i in range(2)]
    P.op("dve", lambda e: e.memset(Tst[0][0][:], 0.0), writes=[Tst[0][1]])
    Yo, YoB = t128("Yo", n=256); Yo16, Yo16B = t128("Yo16", BF16, n=256)
    st6, st6B = t128("st6", n=24); mv, mvB = t128("mv", n=8); rs, rsB = t128("rs", n=4)
    oT = [t64("oT%d" % i, BF16) for i in range(2)]
    v3 = lambda ap, c: ap.rearrange("p (c t) -> p c t", c=c)
    ps, psB = M.ps, M.psB
    big = M.psall
    tstate = [0]

    def prep(tt):
        ps = [M.ps[0], M.ps[1]] * 4
        psB = [M.psB[0], M.psB[1]] * 4
        sh = SH[tt % 2]
        (E1, E1B), (AR, ARB), (bT, bTB), (kT, kTB) = sh["E1"], sh["AR"], sh["bT"], sh["kT"]
        (Vt, VtB), (Bh, BhB), (Kh, KhB), (gtm, gtmB), (sbon, sbonB) = sh["Vt"], sh["Bh"], sh["Kh"], sh["gtm"], sh["sbon"]
        ARv = AR[:].rearrange("p (c t) -> p c t", c=4)
        hb, hbB = M.load_h_tile(tt)
        for gi, (c0, c1, z, zB) in enumerate(((0, 64, zr, zrB), (64, 128, zk, zkB), (128, 192, zv, zvB),
                                               (192, 320, zwa, zwaB), (320, 448, zg, zgB))):
            n = c1 - c0
            for kc in range(KC):
                _mm(P, ps[gi][0:n, :], W[:, kc, c0:c1], hb[:, kc, :], kc == 0, kc == KC - 1, [WB, hbB], [psB[gi]], kc == KC - 1)
            P.op("act", lambda e: e.activation(out=z[:, 1:TT + 1], in_=ps[gi][0:n, :], func=AF.Copy), reads=[psB[gi]], writes=[zB])
        for (z, zB, o, oB, col, n) in ((zr, zrB, rm, rmB, 0, 64), (zk, zkB, km, kmB, 1, 64), (zv, zvB, vm, vmB, 2, 64),
                                       (zwa, zwaB, wam, wamB, 8, 128), (zg, zgB, gm, gmB, 9, 128)):
            P.op("pool", lambda e: e.tensor_scalar(out=tmp[0:n, :], in0=z[:, 1:TT + 1], scalar1=dv[0:n, col:col + 1], scalar2=0.0,
                                                   op0=ALU.mult, op1=ALU.add), reads=[zB, cB], writes=[tmpB])
            P.op("dve", lambda e: e.scalar_tensor_tensor(out=o[:], in0=z[:, 0:TT], scalar=rv[0:n, col:col + 1], in1=tmp[0:n, :],
                                                         op0=ALU.mult, op1=ALU.add), reads=[zB, cB, tmpB], writes=[oB])
            P.op("dve", lambda e: e.tensor_copy(out=z[:, 0:1], in_=z[:, TT:TT + 1]), reads=[zB], writes=[zB])
        P.op("act", lambda e: e.activation(out=twl[0:64, :], in_=wam[0:64, :], func=AF.Tanh), reads=[wamB], writes=[twlB])
        P.op("pool", lambda e: e.tensor_copy(out=twl[64:128, :], in_=wam[64:128, :]), reads=[wamB], writes=[twlB])
        _mm(P, ps[0][0:64, :], rm16[0:64, 0:64], twl[0:64, :], True, True, [cB, twlB], [psB[0]], True)
        _mm(P, ps[1][0:64, :], rm16[64:128, 64:128], twl[64:128, :], True, True, [cB, twlB], [psB[1]], True)
        P.op("act", lambda e: e.activation(out=lw[:], in_=ps[0][0:64, :], func=AF.Tanh, scale=0.5, bias=dv[0:64, 10:11]),
             reads=[psB[0], cB], writes=[lwB])
        P.op("dve", lambda e: e.tensor_scalar(out=lw[:], in0=lw[:], scalar1=-0.3032653298563167, scalar2=-0.3032653298563167,
                                              op0=ALU.mult, op1=ALU.add), reads=[lwB], writes=[lwB])
        P.op("act", lambda e: e.activation(out=ta[:], in_=ps[1][0:64, :], func=AF.Tanh, scale=0.5, bias=dv[0:64, 11:12]),
             reads=[psB[1], cB], writes=[taB])
        P.op("act", lambda e: e.activation(out=tmp[:], in_=gm[:], func=AF.Tanh, scale=0.5), reads=[gmB], writes=[tmpB])
        P.op("dve", lambda e: e.tensor_scalar(out=sg16[:], in0=tmp[:], scalar1=0.5, scalar2=0.5, op0=ALU.mult, op1=ALU.add),
             reads=[tmpB], writes=[sg16B])
        for c in range(4):
            _mm(P, ps[2][:, c * 64:(c + 1) * 64], sg16[:, c * 128:(c + 1) * 128], rm16[:, 128:192], True, True,
                [sg16B, cB], [psB[2]], c == 3)
        P.op("act", lambda e: e.activation(out=gtm[:], in_=ps[2][:, 0:256], func=AF.Copy), reads=[psB[2]], writes=[gtmB])
        P.op("pool", lambda e: e.tensor_scalar(out=kk[:], in0=km[:], scalar1=rv[0:64, 5:6], scalar2=0.0, op0=ALU.mult, op1=ALU.add),
             reads=[kmB, cB], writes=[kkB])
        P.op("pool", lambda e: e.tensor_tensor(out=tmp[0:64, :], in0=kk[:], in1=kk[:], op=ALU.mult), reads=[kkB], writes=[tmpB])
        _mm(P, ps[3][0:64, :], ones64[0:64, :], tmp[0:64, :], True, True, [cB, tmpB], [psB[3]], True)
        P.op("act", lambda e: e.activation(out=tmp[0:64, :], in_=ps[3][0:64, :], func=AF.Sqrt), reads=[psB[3]], writes=[tmpB])
        P.op("dve", lambda e: e.tensor_scalar(out=tmp[0:64, :], in0=tmp[0:64, :], scalar1=1e-12, scalar2=None, op0=ALU.max),
             reads=[tmpB], writes=[tmpB])
        P.op("dve", lambda e: e.reciprocal(out=tmp[0:64, :], in_=tmp[0:64, :]), reads=[tmpB], writes=[tmpB])
        P.op("dve", lambda e: e.tensor_tensor(out=kk[:], in0=kk[:], in1=tmp[0:64, :], op=ALU.mult), reads=[kkB, tmpB], writes=[kkB])
        P.op("pool", lambda e: e.tensor_scalar(out=kf[:], in0=ta[:], scalar1=dv[0:64, 12:13], scalar2=dv[0:64, 13:14],
                                               op0=ALU.mult, op1=ALU.add), reads=[taB, cB], writes=[kfB])
        P.op("pool", lambda e: e.tensor_tensor(out=kf[:], in0=kf[:], in1=km[:], op=ALU.mult), reads=[kfB, kmB], writes=[kfB])
        P.op("pool", lambda e: e.tensor_scalar(out=bb[:], in0=ta[:], scalar1=0.5, scalar2=0.5, op0=ALU.mult, op1=ALU.add),
             reads=[taB], writes=[bbB])
        P.op("pool", lambda e: e.tensor_tensor(out=bb[:], in0=bb[:], in1=kk[:], op=ALU.mult), reads=[bbB, kkB], writes=[bbB])
        P.op("dve", lambda e: e.tensor_tensor_scan(out=cw[:], data0=cmask[0:64, :], data1=lw[:], initial=0.0,
                                                   op0=ALU.mult, op1=ALU.add), reads=[lwB, cB], writes=[cwB])
        P.op("pool", lambda e: e.tensor_tensor(out=cwx[:], in0=cw[:], in1=lw[:], op=ALU.subtract), reads=[cwB, lwB], writes=[cwxB])
        P.op("dve", lambda e: e.tensor_tensor(out=v3(dd[:], 4), in0=v3(cw[:], 4)[:, :, 127:128].to_broadcast([64, 4, 128]),
                                              in1=v3(cw[:], 4), op=ALU.subtract), reads=[cwB], writes=[ddB])
        P.op("act", lambda e: e.activation(out=E1[:], in_=cw[:], func=AF.Exp), reads=[cwB], writes=[E1B])
        P.op("act", lambda e: e.activation(out=E2[:], in_=cw[:], func=AF.Exp, scale=-1.0), reads=[cwB], writes=[E2B])
        P.op("act", lambda e: e.activation(out=E3[:], in_=cwx[:], func=AF.Exp), reads=[cwxB], writes=[E3B])
        P.op("act", lambda e: e.activation(out=E4[:], in_=dd[:], func=AF.Exp), reads=[ddB], writes=[E4B])
        P.op("dve", lambda e: e.scalar_tensor_tensor(out=ARv[:, :, 0:128], in0=v3(kk[:], 4), scalar=-1.0, in1=v3(E3[:], 4),
                                                     op0=ALU.mult, op1=ALU.mult), reads=[kkB, E3B], writes=[ARB])
        P.op("pool", lambda e: e.tensor_tensor(out=ARv[:, :, 128:256], in0=v3(rm[:], 4), in1=v3(E1[:], 4), op=ALU.mult),
             reads=[rmB, E1B], writes=[ARB])
        P.op("pool", lambda e: e.tensor_tensor(out=bT[:], in0=bb[:], in1=E2[:], op=ALU.mult), reads=[bbB, E2B], writes=[bTB])
        P.op("pool", lambda e: e.tensor_tensor(out=kT[:], in0=kf[:], in1=E2[:], op=ALU.mult), reads=[kfB, E2B], writes=[kTB])
        P.op("pool", lambda e: e.tensor_tensor(out=BhT[:], in0=bb[:], in1=E4[:], op=ALU.mult), reads=[bbB, E4B], writes=[BhTB])
        P.op("pool", lambda e: e.tensor_tensor(out=KhT[:], in0=kf[:], in1=E4[:], op=ALU.mult), reads=[kfB, E4B], writes=[KhTB])
        P.op("dve", lambda e: e.scalar_tensor_tensor(out=prod[:], in0=rm[:], scalar=rv[0:64, 7:8], in1=kf[:],
                                                     op0=ALU.mult, op1=ALU.mult), reads=[rmB, cB, kfB], writes=[prodB])
        for c in range(4):
            _mm(P, ps[4][:, c:c + 1], prod[:, c * 128:(c + 1) * 128], ones64[0:64, 0:1], True, True, [prodB, cB], [psB[4]], c == 3)
        P.op("act", lambda e: e.activation(out=sbon[:], in_=ps[4][:, 0:4], func=AF.Copy), reads=[psB[4]], writes=[sbonB])
        for (src, srcB, dst, dstB, pi) in ((vm, vmB, Vt, VtB, 5), (BhT, BhTB, Bh, BhB, 6), (KhT, KhTB, Kh, KhB, 7)):
            for c in range(4):
                _mm(P, ps[pi][:, c * 64:(c + 1) * 64], src[:, c * 128:(c + 1) * 128], ident16[0:64, 0:64], True, True,
                    [srcB, cB], [psB[pi]], c == 3)
            P.op("act" if pi != 6 else "dve", (lambda e: e.activation(out=dst[:], in_=ps[pi][:, 0:256], func=AF.Copy)) if pi != 6 else
                 (lambda e: e.tensor_copy(out=dst[:], in_=ps[pi][:, 0:256])), reads=[psB[pi]], writes=[dstB])
    def matrix(tt):
        sl = slice(tt * TT, (tt + 1) * TT)
        ps = [M.ps[2 + (i % 6)] for i in range(8)]
        psB = [M.psB[2 + (i % 6)] for i in range(8)]
        big = M.psall[:, 2 * TT:6 * TT]
        tcur = tstate[0]
        sh = SH[tt % 2]
        (E1, E1B), (AR, ARB), (bT, bTB), (kT, kTB) = sh["E1"], sh["AR"], sh["bT"], sh["kT"]
        (Vt, VtB), (Bh, BhB), (Kh, KhB), (gtm, gtmB), (sbon, sbonB) = sh["Vt"], sh["Bh"], sh["Kh"], sh["gtm"], sh["sbon"]
        ARv = AR[:].rearrange("p (c t) -> p c t", c=4)
        for c in range(4):
            _mm(P, big[:, c * 256:(c + 1) * 256], bT[:, c * 128:(c + 1) * 128], ARv[:, c, :], True, True,
                [bTB, ARB], [psB[0], psB[1]], c == 3)
        for c in range(4):
            _mm(P, big[:, 1024 + c * 256:1024 + (c + 1) * 256], kT[:, c * 128:(c + 1) * 128], ARv[:, c, :], True, True,
                [kTB, ARB], [psB[2], psB[3]], c == 3)
        for c in range(4):
            _mm(P, ps[4][:, c * 128:(c + 1) * 128], ARv[:, c, 0:128], bT[:, c * 128:(c + 1) * 128], True, True,
                [ARB, bTB], [psB[4]], c == 3)
        a1 = big[:, 0:1024].rearrange("p (c t) -> p c t", c=4)
        a2 = big[:, 1024:2048].rearrange("p (c t) -> p c t", c=4)
        X0, X0B = X[0]
        XT0, XT0B = XT[0]
        P.op("dve", lambda e: e.tensor_tensor(out=v3(X0[:], 4), in0=a1[:, :, 0:128], in1=m_su, op=ALU.mult),
             reads=[psB[0], psB[1], cB], writes=[X0B])
        P.op("dve", lambda e: e.tensor_tensor(out=v3(Yrb[:], 4), in0=a1[:, :, 128:256], in1=m_iu, op=ALU.mult),
             reads=[psB[0], psB[1], cB], writes=[YrbB])
        P.op("dve", lambda e: e.tensor_tensor(out=v3(Xak[:], 4), in0=a2[:, :, 0:128], in1=m_su, op=ALU.mult),
             reads=[psB[2], psB[3], cB], writes=[XakB])
        P.op("dve", lambda e: e.tensor_tensor(out=v3(Yrk[:], 4), in0=a2[:, :, 128:256], in1=m_iu, op=ALU.mult),
             reads=[psB[2], psB[3], cB], writes=[YrkB])
        P.op("dve", lambda e: e.tensor_tensor(out=v3(XT0[:], 4), in0=v3(ps[4], 4), in1=m_sl, op=ALU.mult),
             reads=[psB[4], cB], writes=[XT0B])
        for c in range(4):
            _mm(P, ps[5][:, c * 128:c * 128 + 64], Xak[:, c * 128:(c + 1) * 128], Vt[:, c * 64:(c + 1) * 64], True, True,
                [XakB, VtB], [psB[5]], False)
            _mm(P, ps[5][:, c * 128 + 64:(c + 1) * 128], ARv[:, c, 0:128], ident16[0:64, 0:64], True, True,
                [ARB, cB], [psB[5]], c == 3)
        Zc, ZcB = Z[0]
        P.op("act", lambda e: e.activation(out=Zc[:], in_=ps[5], func=AF.Copy), reads=[psB[5]], writes=[ZcB])
        X0, X0B = X[0]
        XT0, XT0B = XT[0]
        bank = [0]

        def mm4(lhs, lhsB, rhs, rhsB):
            bi = bank[0] % 8
            bank[0] += 1
            for c in range(4):
                _mm(P, ps[bi][:, c * 128:(c + 1) * 128], lhs[:, c * 128:(c + 1) * 128], rhs[:, c * 128:(c + 1) * 128],
                    True, True, [lhsB, rhsB], [psB[bi]], c == 3)
            return ps[bi], psB[bi]

        def evac(eng, dst, dstB, src, srcB):
            if eng == "act":
                P.op("act", lambda e: e.activation(out=dst[:], in_=src, func=AF.Copy), reads=[srcB], writes=[dstB])
            else:
                P.op(eng, lambda e: e.tensor_copy(out=dst[:], in_=src), reads=[srcB], writes=[dstB])

        def addto(dst, dstB, psrc, psrcB, other, otherB):
            P.op("dve", lambda e: e.tensor_tensor(out=dst[:], in0=psrc, in1=other[:], op=ALU.add),
                 reads=[psrcB, otherB], writes=[dstB])

        def masked(dst, dstB, src, srcB, mk):
            P.op("pool", lambda e: e.tensor_tensor(out=dst[:], in0=src[:], in1=mk, op=ALU.mult), reads=[srcB, cB], writes=[dstB])

        S, SB = iv["S0"]
        ST, STB = iv["S0T"]
        masked(S, SB, X0, X0B, m_d16)
        masked(ST, STB, XT0, XT0B, m_d16)
        J, JB = iv["J"]
        JT, JTB = iv["JT"]
        P.op("dve", lambda e: e.tensor_tensor(out=J[:], in0=S[:], in1=identx4, op=ALU.add), reads=[SB, cB], writes=[JB])
        P.op("dve", lambda e: e.tensor_tensor(out=JT[:], in0=ST[:], in1=identx4, op=ALU.add), reads=[STB, cB], writes=[JTB])
        F = FT = None
        for lev in range(3):
            Sn, SnB = iv["S%d" % (lev + 1)]
            STn, STnB = iv["S%dT" % (lev + 1)]
            p1, p1B = mm4(ST, STB, S, SB)
            p2, p2B = mm4(S, SB, ST, STB)
            evac("act", Sn, SnB, p1, p1B)
            evac("dve", STn, STnB, p2, p2B)
            if lev == 0:
                P.op("dve", lambda e: e.tensor_tensor(out=J[:], in0=J[:], in1=Sn[:], op=ALU.add), reads=[JB, SnB], writes=[JB])
                P.op("dve", lambda e: e.tensor_tensor(out=JT[:], in0=JT[:], in1=STn[:], op=ALU.add), reads=[JTB, STnB], writes=[JTB])
                q1, q1B = mm4(ST, STB, Sn, SnB)
                q2, q2B = mm4(S, SB, STn, STnB)
                F, FB = iv["Fa"]
                FT, FTB = iv["FaT"]
                addto(F, FB, q1, q1B, J, JB)
                addto(FT, FTB, q2, q2B, JT, JTB)
            else:
                q1, q1B = mm4(FT, FTB, Sn, SnB)
                q2, q2B = mm4(F, FB, STn, STnB)
                Fn, FnB = iv["Fb" if lev == 1 else "Fa"]
                FTn, FTnB = iv["FbT" if lev == 1 else "FaT"]
                addto(Fn, FnB, q1, q1B, F, FB)
                addto(FTn, FTnB, q2, q2B, FT, FTB)
                F, FB, FT, FTB = Fn, FnB, FTn, FTnB
            S, SB, ST, STB = Sn, SnB, STn, STnB
        Tk, TkB, Nk, NkB = F, FB, FT, FTB
        for li, mk in enumerate((m_o16, m_o32, m_o64)):
            U, UB = iv["U"]
            Lm, LB = iv["L"]
            masked(Lm, LB, XT0, XT0B, mk)
            last = li == 2
            if not last:
                masked(U, UB, X0, X0B, mk)
            w1, w1B = mm4(Lm, LB, Tk, TkB)
            W1, W1B = iv["W1"]
            evac("act", W1, W1B, w1, w1B)
            if not last:
                v1, v1B = mm4(U, UB, Nk, NkB)
                V1, V1B = iv["V1"]
                evac("dve", V1, V1B, v1, v1B)
            t2, t2B = mm4(Nk, NkB, W1, W1B)
            Tn, TnB = iv["Ta" if li % 2 == 0 else "Tb"]
            addto(Tn, TnB, t2, t2B, Tk, TkB)
            if not last:
                n2, n2B = mm4(Tk, TkB, V1, V1B)
                Nn, NnB = iv["Na" if li % 2 == 0 else "Nb"]
                addto(Nn, NnB, n2, n2B, Nk, NkB)
                Nk, NkB = Nn, NnB
            Tk, TkB = Tn, TnB
        zp, zpB = mm4(Tk, TkB, Z[0][0], Z[0][1])
        Zf, ZfB = Z[1]
        evac("act", Zf, ZfB, zp, zpB)
        Zv = v3(Zf[:], 4)
        for c in range(4):
            _mm(P, ps[0][0:64, c * 128:(c + 1) * 128], Zv[:, c, 64:128], Yrb[:, c * 128:(c + 1) * 128], True, True,
                [ZfB, YrbB], [psB[0]], c == 3)
        P.op("dve", lambda e: e.tensor_tensor(out=v3(RpT[:], 4), in0=v3(ps[0][0:64, :], 4), in1=ARv[:, :, 128:256], op=ALU.add),
             reads=[psB[0], ARB], writes=[RpTB])
        for c in range(4):
            _mm(P, ps[1][:, c * 64:(c + 1) * 64], Yrb[:, c * 128:(c + 1) * 128], Zv[:, c, 0:64], True, False,
                [YrbB, ZfB], [psB[1]], False)
            _mm(P, ps[1][:, c * 64:(c + 1) * 64], Yrk[:, c * 128:(c + 1) * 128], Vt[:, c * 64:(c + 1) * 64], False, True,
                [YrkB, VtB], [psB[1]], c == 3)
        P.op("act", lambda e: e.activation(out=Yl[:], in_=ps[1][:, 0:256], func=AF.Copy), reads=[psB[1]], writes=[YlB])
        for c in range(4):
            _mm(P, ps[2][0:64, c * 64:(c + 1) * 64], Zv[:, c, 64:128], Bh[:, c * 64:(c + 1) * 64], True, True,
                [ZfB, BhB], [psB[2]], c == 3)
        P.op("act", lambda e: e.activation(out=MT[:], in_=ps[2][0:64, 0:256], func=AF.Copy), reads=[psB[2]], writes=[MTB])
        for c in range(4):
            _mm(P, ps[3][0:64, c * 64:(c + 1) * 64], Kh[:, c * 64:(c + 1) * 64], Vt[:, c * 64:(c + 1) * 64], True, False,
                [KhB, VtB], [psB[3]], False)
            _mm(P, ps[3][0:64, c * 64:(c + 1) * 64], Bh[:, c * 64:(c + 1) * 64], Zv[:, c, 0:64], False, True,
                [BhB, ZfB], [psB[3]], c == 3)
        P.op("dve", lambda e: e.tensor_copy(out=Psi[:], in_=ps[3][0:64, 0:256]), reads=[psB[3]], writes=[PsiB])
        for c in range(4):
            Tc, TcB = Tst[tcur]
            Tn, TnB = Tst[1 - tcur]
            _mm(P, ps[4][:, c * 64:(c + 1) * 64], RpT[:, c * 128:(c + 1) * 128], Tc[:], True, True, [RpTB, TcB], [psB[4]], True)
            _mm(P, ps[5][0:64, c * 64:(c + 1) * 64], MT[:, c * 64:(c + 1) * 64], Tc[:], True, True, [MTB, TcB], [psB[5]], True)
            wc = E1[:, c * 128 + 127:c * 128 + 128]
            P.op("dve", lambda e: e.scalar_tensor_tensor(out=Tn[:], in0=Tc[:], scalar=wc, in1=Psi[:, c * 64:(c + 1) * 64],
                                                         op0=ALU.mult, op1=ALU.add), reads=[TcB, E1B, PsiB], writes=[TnB])
            P.op("dve", lambda e: e.tensor_tensor(out=Tn[:], in0=ps[5][0:64, c * 64:(c + 1) * 64], in1=Tn[:], op=ALU.add),
                 reads=[psB[5], TnB], writes=[TnB])
            tcur = 1 - tcur
        tstate[0] = tcur
        P.op("dve", lambda e: e.tensor_tensor(out=Yo[:], in0=ps[4][:, 0:256], in1=Yl[:], op=ALU.add), reads=[psB[4], YlB], writes=[YoB])
        Yov = v3(Yo[:], 4)
        for c in range(4):
            P.op("dve", lambda e: e.bn_stats(out=st6[:, c * 6:(c + 1) * 6], in_=Yov[:, c, :]), reads=[YoB], writes=[st6B])
        for c in range(4):
            P.op("dve", lambda e: e.bn_aggr(out=mv[:, c * 2:(c + 1) * 2], in_=st6[:, c * 6:(c + 1) * 6]), reads=[st6B], writes=[mvB])
        mvv = v3(mv[:], 4)
        P.op("dve", lambda e: e.tensor_scalar(out=rs[:], in0=mvv[:, :, 1], scalar1=64e-5, scalar2=None, op0=ALU.add),
             reads=[mvB], writes=[rsB])
        P.op("act", lambda e: e.activation(out=rs[:], in_=rs[:], func=AF.Sqrt), reads=[rsB], writes=[rsB])
        P.op("dve", lambda e: e.reciprocal(out=rs[:], in_=rs[:]), reads=[rsB], writes=[rsB])
        for c in range(4):
            P.op("dve", lambda e: e.tensor_scalar(out=Yov[:, c, :], in0=Yov[:, c, :], scalar1=mv[:, 2 * c:2 * c + 1],
                                                  scalar2=rs[:, c:c + 1], op0=ALU.subtract, op1=ALU.mult),
                 reads=[YoB, mvB, rsB], writes=[YoB])
        P.op("pool", lambda e: e.tensor_tensor(out=Yo[:], in0=Yo[:], in1=rgn[:, 0:256], op=ALU.mult), reads=[YoB, cB], writes=[YoB])
        P.op("pool", lambda e: e.tensor_tensor(out=Yo[:], in0=Yo[:], in1=rgn[:, 256:512], op=ALU.add), reads=[YoB, cB], writes=[YoB])
        for c in range(4):
            P.op("dve", lambda e: e.scalar_tensor_tensor(out=Yov[:, c, :], in0=Vt[:, c * 64:(c + 1) * 64], scalar=sbon[:, c:c + 1],
                                                         in1=Yov[:, c, :], op0=ALU.mult, op1=ALU.add),
                 reads=[VtB, sbonB, YoB], writes=[YoB])
        P.op("dve", lambda e: e.tensor_tensor(out=Yo16[:], in0=Yo[:], in1=gtm[:], op=ALU.mult), reads=[YoB, gtmB], writes=[Yo16B])
        Yo16v = v3(Yo16[:], 4)
        for c in range(4):
            _mm(P, ps[6][0:64, c * 128:(c + 1) * 128], Yo16v[:, c, :], ident16[:], True, True, [Yo16B, cB], [psB[6]], c == 3)
        o, oB = oT[tt % 2]
        P.op("act", lambda e: e.activation(out=o[:], in_=ps[6][0:64, :], func=AF.Copy), reads=[psB[6]], writes=[oB])
        P.dma("sp", rwT_out_d[:, sl], o[:], reads=[oB], is_output=True)

    prep(0)
    for tt in range(NQT):
        if tt + 1 < NQT:
            P.interleave(lambda: matrix(tt), lambda: prep(tt + 1), 3, 1)
        else:
            matrix(tt)


def build_mix_prog(parts=("att", "lru", "rwkv")):
    nc = bass.Bass("TRN2", target_bir_lowering=False)
    dt = lambda name, shape, dty, kind="ExternalInput": nc.dram_tensor(name, shape, dty, kind=kind).ap()
    hT_d = dt("hT", [D, SEQ], BF16)
    with ExitStack() as es:
        P = Prog(nc, es)
        M = MixCtx(P, nc, es, hT_d)
        if "att" in parts and "lru" in parts:
            w_att_d = dt("w_att", [D, 193], F32)
            avec_d = dt("avec", [128, 4], F32)
            cmask_d = dt("cmask", [128, 4 * TT + 128], BF16)
            att_o = dt("attT", [64, SEQ], BF16, "ExternalOutput")
            w_lru_d = dt("w_lru", [D, 256], F32)
            gab_d = dt("gab", [128, 256], F32)
            lvec_d = dt("lvec", [128, 8], F32)
            lru_o = dt("lruT", [128, SEQ], BF16, "ExternalOutput")
            with ExitStack() as es2:
                es3 = es2
                M2 = MixCtx(P, nc, es3, hT_d, parent=M, tag="l_")
                Ma = MixCtx(P, nc, es3, hT_d, parent=M, tag="a_")
                P.interleave(lambda: emit_attention(Ma, es2, w_att_d, avec_d, cmask_d, att_o, npb=2),
                             lambda: emit_lru(M2, es3, w_lru_d, gab_d, lvec_d, lru_o, banks=(4, 5)), 6, 1)
                P.barrier()
        elif "att" in parts:
            w_att_d = dt("w_att", [D, 193], F32)
            avec_d = dt("avec", [128, 4], F32)
            cmask_d = dt("cmask", [128, 4 * TT + 128], BF16)
            att_o = dt("attT", [64, SEQ], BF16, "ExternalOutput")
            with ExitStack() as es2:
                emit_attention(M, es2, w_att_d, avec_d, cmask_d, att_o)
                P.barrier()
        elif "lru" in parts:
            w_lru_d = dt("w_lru", [D, 256], F32)
            gab_d = dt("gab", [128, 256], F32)
            lvec_d = dt("lvec", [128, 8], F32)
            lru_o = dt("lruT", [128, SEQ], BF16, "ExternalOutput")
            with ExitStack() as es2:
                emit_lru(M, es2, w_lru_d, gab_d, lvec_d, lru_o)
                P.barrier()
        if "rwkv" in parts:
            w_rw_d = dt("w_rw", [D, 448], F32)
            rvec_d = dt("rvec", [128, 16], F32)
            rmat_d = dt("rmat", [128, 192], F32)
            rgn_d = dt("rgn", [128, 512], F32)
            rconst_d = dt("rconst", [128, 5120], F32)
            rw_o = dt("rwkvT", [64, SEQ], BF16, "ExternalOutput")
            with ExitStack() as es2:
                emit_rwkv(M, es2, w_rw_d, rvec_d, rmat_d, rgn_d, rconst_d, rw_o)
                P.barrier()
        P.finish()
    return nc


ATT_COLS = 1544
LRU_BASE = 1544
RWKV_BASE = 3592
GATE_BASE = 5384
_BF = ml_dtypes.bfloat16


def _cmask_const():
    p = np.arange(128)[:, None, None]
    a = np.arange(4)[None, :, None]
    c = np.arange(TT)[None, None, :]
    m = np.where(c - p - 128 * a >= 0, 0.0, MASKNEG).astype(np.float32).reshape(128, 4 * TT)
    return np.concatenate([m, np.eye(128, dtype=np.float32)], axis=1).astype(_BF)


def mix_inputs(inp, l, hT_full, parts=("att", "lru", "rwkv")):
    w_in = inp["w_in"][l]
    cm = _cmask_const()
    maps = []
    for j in range(NCORES):
        m = {"hT": hT_full}
        if "att" in parts:
            m["w_att"] = np.ascontiguousarray(np.concatenate(
                [w_in[:, 64 * j:64 * j + 64], w_in[:, 512 + 64 * j:512 + 64 * j + 64],
                 w_in[:, 1536 + j:1537 + j], w_in[:, 1024 + 64 * j:1024 + 64 * j + 64]], axis=1))
            av = np.zeros((128, 4), np.float32)
            av[:, 0] = inp["fox_f_bias"][l][j]
            m["avec"] = av
            m["cmask"] = cm
        if "lru" in parts:
            c0 = LRU_BASE + 128 * j
            m["w_lru"] = np.ascontiguousarray(np.concatenate([w_in[:, c0:c0 + 128], w_in[:, c0 + 1024:c0 + 1152]], axis=1))
            gab = np.zeros((128, 256), np.float32)
            for b in range(2):
                gab[64 * b:64 * b + 64, 64 * b:64 * b + 64] = inp["lru_ga_w"][l][2 * j + b]
                gab[64 * b:64 * b + 64, 128 + 64 * b:128 + 64 * b + 64] = inp["lru_gx_w"][l][2 * j + b]
            m["gab"] = gab
            ch = slice(128 * j, 128 * j + 128)
            m["lvec"] = np.ascontiguousarray(np.stack(
                [inp["lru_conv_w"][l][k][ch] for k in range(4)] +
                [inp["lru_conv_b"][l][ch], inp["lru_ga_b"][l][ch], inp["lru_gx_b"][l][ch], inp["lru_lambda"][l][ch]],
                axis=1).astype(np.float32))
        if "rwkv" in parts:
            m.update(rwkv_inputs(inp, l, j))
        maps.append(m)
    return maps


def _rconst():
    r = np.arange(128)[:, None]
    c = np.arange(128)[None, :]
    ident = np.eye(128, dtype=np.float32)
    su = (r < c).astype(np.float32)
    iu = (r <= c).astype(np.float32)
    sl = (c < r).astype(np.float32)
    cm = np.ones((128, 512), np.float32)
    cm[:, ::128] = 0.0
    first = np.concatenate([ident, np.zeros((128, 384), np.float32)], axis=1)
    blk = lambda k: (r // k) == (c // k)
    d16 = blk(16).astype(np.float32)
    off = lambda k: (blk(2 * k) & ~blk(k)).astype(np.float32)
    t4 = lambda m: np.tile(m, (1, 4))
    return np.ascontiguousarray(np.concatenate([first, t4(su), t4(iu), t4(sl), cm, t4(d16), t4(off(16)), t4(off(32)), t4(off(64)),
                                                t4(ident)], axis=1))


def rwkv_inputs(inp, l, j):
    w_in = inp["w_in"][l]
    b = RWKV_BASE
    hs = slice(64 * j, 64 * j + 64)
    cols = [w_in[:, b + 64 * j:b + 64 * j + 64], w_in[:, b + 512 + 64 * j:b + 512 + 64 * j + 64],
            w_in[:, b + 1024 + 64 * j:b + 1024 + 64 * j + 64], w_in[:, b + 1536:b + 1792]]
    mu = inp["rwkv_mu"][l]
    rvec = np.zeros((128, 16), np.float32)
    rvec[0:64, 0] = mu[64 * j:64 * j + 64]
    rvec[0:64, 1] = mu[512 + 64 * j:512 + 64 * j + 64]
    rvec[0:64, 2] = mu[1024 + 64 * j:1024 + 64 * j + 64]
    rvec[0:64, 3] = inp["rwkv_w0"][l][hs]
    rvec[0:64, 4] = inp["rwkv_a0"][l][hs]
    rvec[0:64, 5] = inp["rwkv_k_k"][l][hs]
    rvec[0:64, 6] = inp["rwkv_k_a"][l][hs]
    rvec[0:64, 7] = inp["rwkv_r_k"][l][j]
    rvec[:, 8] = mu[1536:1664]
    rvec[:, 9] = mu[1664:1792]
    rmat = np.zeros((128, 192), np.float32)
    rmat[0:64, 0:64] = inp["rwkv_w2"][l][:, hs]
    rmat[64:128, 64:128] = inp["rwkv_a2"][l][:, hs]
    rmat[:, 128:192] = inp["rwkv_g2"][l][:, hs]
    rgn = np.concatenate([np.tile(inp["rwkv_gn_w"][l][hs][None, :], (128, 4)),
                          np.tile(inp["rwkv_gn_b"][l][hs][None, :], (128, 4))], axis=1).astype(np.float32)
    return {"w_rw": np.ascontiguousarray(np.concatenate(cols, axis=1)), "rvec": rvec, "rmat": rmat,
            "rgn": np.ascontiguousarray(rgn), "rconst": _rconst()}


_PROGS = {}
_DBG = None


def _prog(name, builder):
    if name not in _PROGS:
        _PROGS[name] = builder()
    return _PROGS[name]


def _run(nc, maps):
    return run_bass_kernel_spmd(nc, maps, core_ids=list(range(NCORES))).results


def _pcol(v):
    return np.ascontiguousarray(np.asarray(v, np.float32).reshape(KC, 128).T)


def _vec(gate, lng, lnb, sh, sc):
    return np.ascontiguousarray(np.concatenate([_pcol(v) for v in (gate, lng, lnb, sh, sc)], axis=1))


def kernel(**inp):
    inp = {k: np.asarray(v) for k, v in inp.items()}
    x = inp["x"][0]
    c128 = _pcol(inp["c"][0])
    maps = []
    for c in range(NCORES):
        cs = slice(c * 1152, (c + 1) * 1152)
        bb = np.stack([inp["ada_b"][l][cs].reshape(9, 128).T for l in range(DEPTH)], axis=1).reshape(128, DEPTH * 9)
        maps.append({"c": c128, "ada_w": np.ascontiguousarray(inp["ada_w"][:, :, cs]), "ada_b": np.ascontiguousarray(bb.astype(np.float32))})
    res = _run(_prog("ada", build_ada_prog), maps)
    ada = np.zeros((DEPTH, 9 * D), np.float32)
    for c in range(NCORES):
        o = res[c]["ada_out"].reshape(128, DEPTH, 9)
        for l in range(DEPTH):
            ada[l, c * 1152:(c + 1) * 1152] = o[:, l, :].T.reshape(-1)
    adas = [np.split(ada[l], 9) for l in range(DEPTH)]
    if _DBG:
        _DBG("ada", 0, ada)
    zeros = np.zeros(D, np.float32)
    xT = [np.ascontiguousarray(x[c * TPC:(c + 1) * TPC].T) for c in range(NCORES)]
    v0 = _vec(zeros, zeros, zeros, adas[0][0], adas[0][1])
    res = _run(_prog("mod", build_mod_prog), [{"xT": xT[c], "vec": v0} for c in range(NCORES)])
    hT = [r["hT_out"] for r in res]
    if _DBG:
        _DBG("h1", 0, hT)
    for l in range(DEPTH):
        sh1, sc1, g1, sh2, sc2, g2, sh3, sc3, g3 = adas[l]
        v = _vec(g1, inp["ln_g"][l, 0], inp["ln_b"][l, 0], sh2, sc2)
        res = _run(_prog("ffn", build_ffn_prog), [{"xT": xT[c], "hT": hT[c], "w_up": inp["ffn_up"][l, 0],
                                                   "w_down": inp["ffn_down"][l, 0], "vec": v} for c in range(NCORES)])
        xT = [r["xT_out"] for r in res]
        hT = [r["hT_out"] for r in res]
        if _DBG:
            _DBG("x1", l, xT)
            _DBG("h2", l, hT)
        hfull = np.ascontiguousarray(np.concatenate(hT, axis=1))
        res = _run(_prog("mix", build_mix_prog), mix_inputs(inp, l, hfull))
        att = np.concatenate([r["attT"] for r in res], axis=0)
        lru = np.concatenate([r["lruT"] for r in res], axis=0)
        rwk = np.concatenate([r["rwkvT"] for r in res], axis=0)
        br = np.concatenate([att, lru, rwk], axis=0)
        if _DBG:
            _DBG("att", l, att)
            _DBG("lru", l, lru)
            _DBG("rwkv", l, rwk)
        wg = np.ascontiguousarray(inp["w_in"][l][:, GATE_BASE:GATE_BASE + 3072])
        wp = np.ascontiguousarray(np.concatenate([inp["w_proj_a"][l], inp["w_proj_b"][l], inp["w_proj_c"][l]], axis=0))
        v = _vec(g2, inp["ln_g"][l, 1], inp["ln_b"][l, 1], sh3, sc3)
        res = _run(_prog("mixpost", build_mixpost_prog),
                   [{"xT": xT[c], "hT": hT[c], "brT": np.ascontiguousarray(br[:, c * TPC:(c + 1) * TPC]), "wg": wg, "wp": wp,
                     "wo": inp["w_out"][l], "vec": v} for c in range(NCORES)])
        xT = [r["xT_out"] for r in res]
        hT = [r["hT_out"] for r in res]
        if _DBG:
            _DBG("x2", l, xT)
            _DBG("h3", l, hT)
        nsh, nsc = (adas[l + 1][0], adas[l + 1][1]) if l + 1 < DEPTH else (zeros, zeros)
        v = _vec(g3, inp["ln_g"][l, 2], inp["ln_b"][l, 2], nsh, nsc)
        res = _run(_prog("ffn", build_ffn_prog), [{"xT": xT[c], "hT": hT[c], "w_up": inp["ffn_up"][l, 1],
                                                   "w_down": inp["ffn_down"][l, 1], "vec": v} for c in range(NCORES)])
        xT = [r["xT_out"] for r in res]
        hT = [r["hT_out"] for r in res]
        if _DBG:
            _DBG("x3", l, xT)
    out = np.concatenate([t.T for t in xT], axis=0)[None].astype(np.float32)
    return out
```
